# Optimizing a Trainium2 kernel written in Bass

```python
import math
import jax, jax.numpy as jnp
from jax import lax
import numpy as np

D_MODEL = 1024
BATCH = 2
SEQ = 8192
DEPTH = 1
DEC_BATCH = 8
DEC_SEQ = 4096
PAST_LEN = 128

HEAD_DIM = 64
SSD_HEADS = 8
SSD_WIDTH = SSD_HEADS * HEAD_DIM
SSD_GROUPS = 2
SSD_STATE = 128
SSD_CONV = 5
SSD_CHUNK = 128
ATT_HEADS = 8
ATT_WIDTH = ATT_HEADS * HEAD_DIM
MIX_WIDTH = SSD_WIDTH + ATT_WIDTH
DILATED_PATTERNS = ((128, 1), (512, 4), (2048, 16))
ROPE_DIMS = HEAD_DIM // 4
ROPE_THETA = 500000.0
MEM_LEN = 256
XATT_HEADS = 4
XATT_HEAD_DIM = D_MODEL // XATT_HEADS
D_FF = 4 * D_MODEL
EPS = 1e-6
NEG_INF = -1e30
CONV_CH = SSD_WIDTH + 2 * SSD_GROUPS * SSD_STATE
OFF_Z = 0
OFF_XBC = OFF_Z + SSD_WIDTH
OFF_DT = OFF_XBC + CONV_CH
OFF_Q = OFF_DT + 2 * SSD_HEADS
OFF_K = OFF_Q + ATT_WIDTH
OFF_V = OFF_K + ATT_WIDTH
IN_COLS = OFF_V + ATT_WIDTH

kernel_name = 'hymba_ssd_dilated_encoder'


def rmsnorm(x, g):
    xf = x.astype(jnp.float32)
    y = xf * lax.rsqrt(jnp.mean(xf * xf, axis=-1, keepdims=True) + EPS)
    return (y * g.astype(jnp.float32)).astype(x.dtype)


def rope_partial(t, pos):
    half = ROPE_DIMS // 2
    inv_freq = jnp.power(jnp.float32(ROPE_THETA), -jnp.arange(half, dtype=jnp.float32) / half)
    ang = pos.astype(jnp.float32)[:, None] * inv_freq[None, :]
    cos = jnp.cos(ang)[:, None, :]
    sin = jnp.sin(ang)[:, None, :]
    t1 = t[..., :half]
    t2 = t[..., half:ROPE_DIMS]
    return jnp.concatenate([t1 * cos - t2 * sin, t2 * cos + t1 * sin, t[..., ROPE_DIMS:]], axis=-1)


def centred_depthwise_conv(u, w, b):
    pad = (SSD_CONV - 1) // 2
    out = lax.conv_general_dilated(u, w[:, None, :].astype(u.dtype), window_strides=(1,),
                                   padding=[(pad, pad)], dimension_numbers=('NWC', 'WIO', 'NWC'),
                                   feature_group_count=u.shape[-1])
    return out + b.astype(u.dtype)


def ssd_chunked_scan(x, dt, A, Bm, Cm):
    bsz, l, h, p = x.shape
    g, n = Bm.shape[2], Bm.shape[3]
    r = h // g
    q = SSD_CHUNK
    c = l // q
    x = x.reshape(bsz, c, q, g, r, p)
    dt = dt.reshape(bsz, c, q, g, r)
    Bm = Bm.reshape(bsz, c, q, g, n)
    Cm = Cm.reshape(bsz, c, q, g, n)
    a_cs = jnp.cumsum(dt * A.reshape(g, r), axis=2)
    xdt = x * dt[..., None]
    causal = jnp.tril(jnp.ones((q, q), dtype=bool))[:, :, None, None]
    seg = a_cs[:, :, :, None] - a_cs[:, :, None, :]
    lmat = jnp.where(causal, jnp.exp(jnp.where(causal, seg, 0.0)), 0.0)
    cb = jnp.einsum('bcign,bcjgn->bcijg', Cm, Bm)
    y_diag = jnp.einsum('bcijg,bcijgr,bcjgrp->bcigrp', cb, lmat, xdt)
    decay_to_end = jnp.exp(a_cs[:, :, -1:] - a_cs)
    chunk_states = jnp.einsum('bcjgn,bcjgr,bcjgrp->bcgrpn', Bm, decay_to_end, xdt)
    chunk_decay = jnp.exp(a_cs[:, :, -1])

    def step(carry, inp):
        st, dec = inp
        return carry * dec[..., None, None] + st, carry

    init = jnp.zeros((bsz, g, r, p, n), jnp.float32)
    _, prev = lax.scan(step, init, (jnp.moveaxis(chunk_states, 1, 0), jnp.moveaxis(chunk_decay, 1, 0)))
    prev = jnp.moveaxis(prev, 0, 1)
    y_off = jnp.einsum('bcign,bcgrpn,bcigr->bcigrp', Cm, prev, jnp.exp(a_cs))
    return (y_diag + y_off).reshape(bsz, l, h, p)


def ssd_mixer(u_z, u_xbc, u_dt, conv_w, conv_b, A_log, dt_bias, D, norm_g):
    bsz, s, _ = u_z.shape
    xbc = jax.nn.silu(centred_depthwise_conv(u_xbc, conv_w, conv_b)).astype(jnp.float32)
    xs = xbc[..., :SSD_WIDTH].reshape(bsz, s, SSD_HEADS, HEAD_DIM)
    nbc = SSD_GROUPS * SSD_STATE
    Bm = xbc[..., SSD_WIDTH:SSD_WIDTH + nbc].reshape(bsz, s, SSD_GROUPS, SSD_STATE)
    Cm = xbc[..., SSD_WIDTH + nbc:].reshape(bsz, s, SSD_GROUPS, SSD_STATE)
    dt = jax.nn.softplus(u_dt.astype(jnp.float32).reshape(bsz, s, 2, SSD_HEADS) + dt_bias.astype(jnp.float32))
    A = -jnp.exp(A_log.astype(jnp.float32))
    flip = lambda t: jnp.flip(t, axis=1)
    y_fwd = ssd_chunked_scan(xs, dt[:, :, 0], A[0], Bm, Cm)
    y_bwd = flip(ssd_chunked_scan(flip(xs), flip(dt[:, :, 1]), A[1], flip(Bm), flip(Cm)))
    y = (y_fwd + y_bwd + D.astype(jnp.float32)[:, None] * xs).reshape(bsz, s, SSD_WIDTH)
    y = y * jax.nn.silu(u_z.astype(jnp.float32))
    return rmsnorm(y, norm_g)


def band_pattern(q, k, v, dil, half):
    bsz, s, h, e = q.shape
    L = s // dil
    blk = half
    nb = -(-L // blk)
    Lp = nb * blk

    def to_classes(t, lo, hi):
        t = t.reshape(bsz, L, dil, h, e).transpose(0, 2, 1, 3, 4)
        return jnp.pad(t, ((0, 0), (0, 0), (lo, hi), (0, 0), (0, 0)))

    def windows(t):
        tr = to_classes(t, blk, Lp - L + blk).reshape(bsz, dil, nb + 2, blk, h, e)
        return jnp.concatenate([tr[:, :, :-2], tr[:, :, 1:-1], tr[:, :, 2:]], axis=3)

    qc = to_classes(q, 0, Lp - L).reshape(bsz, dil, nb, blk, h, e)
    kw = windows(k)
    vw = windows(v)
    ii = jnp.arange(blk)[:, None]
    tt = jnp.arange(3 * blk)[None, :]
    delta = tt - blk - ii
    m_k = jnp.arange(nb)[:, None, None] * blk - blk + tt[None]
    valid = (jnp.abs(delta) <= half)[None] & (m_k >= 0) & (m_k < L)
    scores = jnp.einsum('bdnihe,bdnthe->bdnhit', qc, kw) / math.sqrt(e)
    scores = jnp.where(valid[:, None], scores, NEG_INF)
    mx = jnp.max(scores, axis=-1, keepdims=True)
    pr = jnp.exp(scores - mx)
    den = jnp.sum(pr, axis=-1)
    o = jnp.einsum('bdnhit,bdnthe->bdnihe', pr, vw) / jnp.moveaxis(den, 3, 4)[..., None]
    lse = jnp.moveaxis(mx[..., 0] + jnp.log(den), 3, 4)

    def from_classes(t):
        tail = t.shape[4:]
        t = t.reshape(bsz, dil, Lp, *tail)[:, :, :L]
        return jnp.swapaxes(t, 1, 2).reshape(bsz, s, *tail)

    return from_classes(o), from_classes(lse)


def dilated_attention(q, k, v):
    outs = []
    lses = []
    for window, dil in DILATED_PATTERNS:
        o, lse = band_pattern(q, k, v, dil, window // (2 * dil))
        outs.append(o)
        lses.append(lse)
    w = jax.nn.softmax(jnp.stack(lses, axis=0), axis=0)
    return jnp.sum(w[..., None] * jnp.stack(outs, axis=0), axis=0)


def attention_mixer(u_q, u_k, u_v, q_g, k_g, out_g):
    bsz, s, _ = u_q.shape
    pos = jnp.arange(s)
    shp = (bsz, s, ATT_HEADS, HEAD_DIM)
    q = rope_partial(rmsnorm(u_q.reshape(shp).astype(jnp.float32), q_g), pos)
    k = rope_partial(rmsnorm(u_k.reshape(shp).astype(jnp.float32), k_g), pos)
    v = u_v.reshape(shp).astype(jnp.float32)
    o = dilated_attention(q, k, v).reshape(bsz, s, ATT_WIDTH)
    return rmsnorm(o, out_g)


def memory_cross_attention(hn, memn, wq, wkv, q_g, k_g, wo):
    bsz, s, _ = hn.shape
    m = memn.shape[1]
    q = (hn @ wq).reshape(bsz, s, XATT_HEADS, XATT_HEAD_DIM).astype(jnp.float32)
    kv = (memn @ wkv).reshape(bsz, m, 2, XATT_HEADS, XATT_HEAD_DIM).astype(jnp.float32)
    q = rmsnorm(q, q_g)
    k = rmsnorm(kv[:, :, 0], k_g)
    v = kv[:, :, 1]
    sc = jnp.einsum('bshe,bmhe->bhsm', q, k) / math.sqrt(XATT_HEAD_DIM)
    p = jax.nn.softmax(sc, axis=-1)
    o = jnp.einsum('bhsm,bmhe->bshe', p, v).reshape(bsz, s, D_MODEL)
    return o.astype(hn.dtype) @ wo


def encoder_layer(x, mem, p):
    u = rmsnorm(x, p['mix_norm_g']) @ p['w_in']
    ssd_out = ssd_mixer(u[..., OFF_Z:OFF_XBC], u[..., OFF_XBC:OFF_DT], u[..., OFF_DT:OFF_Q],
                        p['conv_w'], p['conv_b'], p['ssd_A_log'], p['ssd_dt_bias'], p['ssd_D'],
                        p['ssd_norm_g'])
    att_out = attention_mixer(u[..., OFF_Q:OFF_K], u[..., OFF_K:OFF_V], u[..., OFF_V:IN_COLS],
                              p['att_q_norm_g'], p['att_k_norm_g'], p['att_out_norm_g'])
    h = x + jnp.concatenate([ssd_out, att_out], axis=-1).astype(x.dtype) @ p['w_out']
    h = h + memory_cross_attention(rmsnorm(h, p['xatt_norm_g']), rmsnorm(mem, p['mem_norm_g']),
                                   p['xatt_wq'], p['xatt_wkv'], p['xatt_q_norm_g'],
                                   p['xatt_k_norm_g'], p['xatt_wo'])
    hm = rmsnorm(h, p['mlp_norm_g'])
    return h + jnp.square(jax.nn.relu(hm @ p['mlp_w1'])) @ p['mlp_w2']


def trunk(x, mem, params):
    for layer in range(DEPTH):
        x = encoder_layer(x, mem, {name: arr[layer] for name, arr in params.items()})
    return x


def setup_inputs(seed: int = 0) -> dict:
    key = jax.random.key(seed)
    ks = jax.random.split(key, 32)
    f32 = jnp.float32
    nrm = lambda k, shape, scale: jax.random.normal(k, shape, f32) * scale
    gain = lambda k, n: 1.0 + 0.02 * jax.random.normal(k, (DEPTH, n), f32)
    dt0 = jnp.exp(jax.random.uniform(ks[8], (DEPTH, 2, SSD_HEADS), f32, math.log(1e-3), math.log(1e-1)))
    return {
        'x_prompt': nrm(ks[0], (BATCH, SEQ, D_MODEL), 1.0),
        'x_sample': nrm(ks[1], (DEC_BATCH, DEC_SEQ, D_MODEL), 1.0),
        'mem_prompt': nrm(ks[2], (BATCH, MEM_LEN, D_MODEL), 1.0),
        'mem_sample': nrm(ks[3], (DEC_BATCH, MEM_LEN, D_MODEL), 1.0),
        'mix_norm_g': gain(ks[4], D_MODEL),
        'w_in': nrm(ks[5], (DEPTH, D_MODEL, IN_COLS), D_MODEL ** -0.5),
        'conv_w': nrm(ks[6], (DEPTH, SSD_CONV, CONV_CH), SSD_CONV ** -0.5),
        'conv_b': nrm(ks[7], (DEPTH, CONV_CH), 0.02),
        'ssd_A_log': jnp.log(jax.random.uniform(ks[9], (DEPTH, 2, SSD_HEADS), f32, 1.0, 16.0)),
        'ssd_dt_bias': dt0 + jnp.log(-jnp.expm1(-dt0)),
        'ssd_D': 1.0 + 0.02 * jax.random.normal(ks[10], (DEPTH, SSD_HEADS), f32),
        'ssd_norm_g': gain(ks[11], SSD_WIDTH),
        'att_q_norm_g': gain(ks[12], HEAD_DIM),
        'att_k_norm_g': gain(ks[13], HEAD_DIM),
        'att_out_norm_g': gain(ks[14], ATT_WIDTH),
        'w_out': nrm(ks[15], (DEPTH, MIX_WIDTH, D_MODEL), MIX_WIDTH ** -0.5),
        'xatt_norm_g': gain(ks[16], D_MODEL),
        'mem_norm_g': gain(ks[17], D_MODEL),
        'xatt_wq': nrm(ks[18], (DEPTH, D_MODEL, D_MODEL), D_MODEL ** -0.5),
        'xatt_wkv': nrm(ks[19], (DEPTH, D_MODEL, 2 * D_MODEL), D_MODEL ** -0.5),
        'xatt_q_norm_g': gain(ks[20], XATT_HEAD_DIM),
        'xatt_k_norm_g': gain(ks[21], XATT_HEAD_DIM),
        'xatt_wo': nrm(ks[22], (DEPTH, D_MODEL, D_MODEL), D_MODEL ** -0.5),
        'mlp_norm_g': gain(ks[23], D_MODEL),
        'mlp_w1': nrm(ks[24], (DEPTH, D_MODEL, D_FF), D_MODEL ** -0.5),
        'mlp_w2': nrm(ks[25], (DEPTH, D_FF, D_MODEL), D_FF ** -0.5),
    }


def reference(x_prompt, x_sample, mem_prompt, mem_sample, mix_norm_g, w_in, conv_w, conv_b,
              ssd_A_log, ssd_dt_bias, ssd_D, ssd_norm_g, att_q_norm_g, att_k_norm_g, att_out_norm_g,
              w_out, xatt_norm_g, mem_norm_g, xatt_wq, xatt_wkv, xatt_q_norm_g, xatt_k_norm_g,
              xatt_wo, mlp_norm_g, mlp_w1, mlp_w2):
    params = {
        'mix_norm_g': mix_norm_g, 'w_in': w_in, 'conv_w': conv_w, 'conv_b': conv_b,
        'ssd_A_log': ssd_A_log, 'ssd_dt_bias': ssd_dt_bias, 'ssd_D': ssd_D, 'ssd_norm_g': ssd_norm_g,
        'att_q_norm_g': att_q_norm_g, 'att_k_norm_g': att_k_norm_g, 'att_out_norm_g': att_out_norm_g,
        'w_out': w_out, 'xatt_norm_g': xatt_norm_g, 'mem_norm_g': mem_norm_g, 'xatt_wq': xatt_wq,
        'xatt_wkv': xatt_wkv, 'xatt_q_norm_g': xatt_q_norm_g, 'xatt_k_norm_g': xatt_k_norm_g,
        'xatt_wo': xatt_wo, 'mlp_norm_g': mlp_norm_g, 'mlp_w1': mlp_w1, 'mlp_w2': mlp_w2,
    }
    y_prompt = trunk(x_prompt, mem_prompt, params)
    y_sample = trunk(x_sample, mem_sample, params)
    return (y_prompt, y_sample)
```

```python
import math
import os
ATT_LEVEL = int(os.environ.get('ATT_LEVEL', '9'))
import numpy as np
import concourse.bass as bass
import concourse.mybir as mybir
from concourse.bass_utils import run_bass_kernel_spmd
from contextlib import ExitStack

F32 = mybir.dt.float32
BF16 = mybir.dt.bfloat16
ALU = mybir.AluOpType
AF = mybir.ActivationFunctionType
EPS = 1e-6
ROT = 20000
D = 1024
INC = 3088
PADV = 1024


class TR:
    def __init__(self, nc, es):
        self.nc = nc
        self.es = es
        self.names = ['pe', 'dve', 'act', 'pool', 'sp']
        self.cnt = {e: 0 for e in self.names}
        self.sems = {e: [] for e in self.names}
        self.dsem = {}
        self.dcnt = {}
        self.lastw = {}
        self.readers = {}
        self.waited = {e: {} for e in self.names}
        self.prog = {e: [] for e in self.names}

    def _sem(self, e, idx):
        j = (idx - 1) // ROT
        while len(self.sems[e]) <= j:
            self.sems[e].append(self.es.enter_context(self.nc.semaphore(f"s_{e}_{len(self.sems[e])}")))
        return self.sems[e][j], (idx - 1) % ROT + 1

    def _deps(self, reads, writes):
        deps = []
        for k in reads:
            if k in self.lastw:
                deps.append(self.lastw[k])
        for k in writes:
            if k in self.lastw:
                deps.append(self.lastw[k])
            deps.extend(self.readers.get(k, []))
        return deps

    def _wait(self, e, deps):
        out = []
        for d in deps:
            if d[0] == 'dma':
                _, key, val = d
                if self.waited[e].get(('dma', key), 0) >= val:
                    continue
                out.append((self.dsem[key], val))
                self.waited[e][('dma', key)] = val
            else:
                f, idx = d
                if f == 'pe' and e == 'pe':
                    continue
                if self.waited[e].get(f, 0) >= idx:
                    continue
                out.append(self._sem(f, idx))
                self.waited[e][f] = idx
        return out

    def _commit(self, ref, reads, writes):
        for k in reads:
            self.readers.setdefault(k, []).append(ref)
        for k in writes:
            self.lastw[k] = ref
            self.readers[k] = []

    def op(self, e, meth, *args, reads=(), writes=(), **kw):
        fn = (lambda h, meth=meth, args=args, kw=kw: getattr(h, meth)(*args, **kw))
        w = self._wait(e, self._deps(reads, writes))
        self.cnt[e] += 1
        s, v = self._sem(e, self.cnt[e])
        self.prog[e].append((w, fn, (s, 1)))
        self._commit((e, self.cnt[e]), reads, writes)

    def dma(self, q, out, in_, key, reads=(), writes=(), **kw):
        w = self._wait(q, self._deps(reads, writes))
        if key not in self.dsem:
            self.dsem[key] = self.es.enter_context(self.nc.semaphore(f"d_{len(self.dsem)}"))
            self.dcnt[key] = 0
        self.dcnt[key] += 16
        self.prog[q].append((w, (lambda h, out=out, in_=in_, kw=kw: h.dma_start(out=out, in_=in_, **kw)),
                             (self.dsem[key], 16)))
        self._commit(('dma', key, self.dcnt[key]), reads, writes)

    def batch_end(self, key, res_keys):
        for k in res_keys:
            self.lastw[k] = ('dma', key, self.dcnt[key])

    def barrier(self):
        fin = [(self.dsem[k], v) for k, v in self.dcnt.items()]
        for e in self.names:
            if self.cnt[e] > 0:
                fin.append(self._sem(e, self.cnt[e]))
        for e in self.names:
            w = []
            for (sm, v) in fin:
                w.append((sm, v))
            self.cnt[e] += 1
            s_, v_ = self._sem(e, self.cnt[e])
            self.prog[e].append((w, (lambda h: h.nop()), (s_, 1)))
        self.lastw = {}
        self.readers = {}

    def emit(self, final_q='sp'):
        fin = [(self.dsem[k], v) for k, v in self.dcnt.items()]
        for e in self.names:
            if self.cnt[e] > 0:
                fin.append(self._sem(e, self.cnt[e]))
        blk = self.es.enter_context(self.nc.Block())
        hmap = {'pe': blk.tensor, 'dve': blk.vector, 'act': blk.scalar, 'pool': blk.gpsimd, 'sp': blk.sync}
        for e in self.names:
            prog = self.prog[e]
            extra = fin if e == final_q else []

            def body(h, prog=prog, extra=extra):
                for (w, fn, inc) in prog:
                    for (s, v) in w:
                        h.wait_ge(s, v)
                    fn(h).then_inc(inc[0], inc[1])
                for (s, v) in extra:
                    h.wait_ge(s, v)
            if prog or extra:
                hmap[e](body)


def make_consts():
    k = np.arange(128)[:, None]
    i = np.arange(128)[None, :]
    ident = (k == i)
    U = (k <= i)
    L = (k >= i)
    Us = (k < i)
    Ls = (k > i)
    ones = np.ones((128, 128), bool)
    blk = ((k // 64) == (i // 64))
    P = np.zeros((128, 128), np.float32)
    for hb in (0, 64):
        for e in range(8):
            P[hb + e + 8, hb + e] = -1.0
            P[hb + e, hb + e + 8] = 1.0
    sel = np.zeros((128, 128), np.float32)
    sel[64, :] = 1.0
    negl = np.where(L, 0.0, -30000.0)
    negu = np.where(U, 0.0, -30000.0)
    mats = [ident, U, L, Us, Ls, ones, blk, P, sel, negl, negu]
    return np.concatenate([np.asarray(m, np.float32) for m in mats], axis=1)


C_ID, C_U, C_L, C_US, C_LS, C_ONES, C_BLK, C_P, C_SEL, C_NEGL, C_NEGU = range(11)
NCM = 11


def build(NT, dbg=(), stop_after=None):
    SEG = NT // 2
    NMT = NT // 512
    NCH = NT // 128
    AM = 2048
    NAM = NT // AM
    nc = bass.Bass("TRN2", target_bir_lowering=False)
    din = lambda n, s, dt=F32: nc.dram_tensor(n, s, dt, kind="ExternalInput").ap()
    dscr = lambda n, s, dt: nc.dram_tensor(n, s, dt, kind="Internal").ap()
    x_d = din("x", [NT, D])
    mem_d = din("mem", [2, 256, D])
    vt_d = din("vtab", [128, 18])
    cs_d = din("cs", [2, 128, NT])
    cst_d = din("consts", [128, NCM * 128])
    w_in_d = din("w_in", [D, INC])
    w_out_d = din("w_out", [D, D])
    wq_d = din("xatt_wq", [D, D])
    wkv_d = din("xatt_wkv", [D, 2 * D])
    wo_d = din("xatt_wo", [D, D])
    w1_d = din("mlp_w1", [D, 4 * D])
    w2_d = din("mlp_w2", [4 * D, D])
    g_mix_d = din("mix_norm_g", [1, D])
    g_xatt_d = din("xatt_norm_g", [1, D])
    g_mem_d = din("mem_norm_g", [1, D])
    g_mlp_d = din("mlp_norm_g", [1, D])
    g_ssd_d = din("ssd_norm_g", [1, 512])
    convw_d = din("conv_w", [5, D])
    convb_d = din("conv_b", [1, D])
    alog_d = din("ssd_A_log", [1, 16])
    dtb_d = din("ssd_dt_bias", [1, 16])
    dsk_d = din("ssd_D", [1, 8])
    gq_d = din("att_q_norm_g", [64, 1])
    gk_d = din("att_k_norm_g", [64, 1])
    gao_d = din("att_out_norm_g", [8, 64])
    gxq_d = din("xatt_q_norm_g", [2, 128])
    gxk_d = din("xatt_k_norm_g", [2, 128])
    y_d = nc.dram_tensor("y", [NT, D], F32, kind="ExternalOutput").ap()
    dbg_d = {n: nc.dram_tensor("dbg_" + n, s, dt, kind="ExternalOutput").ap() for (n, s, dt) in dbg}

    XNT = dscr("XNT", [8, 128, NT + 4], BF16)
    SSDT = dscr("SSDT", [4, 128, NT], BF16)
    ATTT = dscr("ATTT", [4, 128, NT], BF16)
    VS = dscr("VS", [NT + 2 * PADV, 520], BF16)
    WINB = dscr("WINB", [128, 8, INC], BF16)
    WOUTB = dscr("WOUTB", [128, 8, D], BF16)
    WQB = dscr("WQB", [128, 8, D], BF16)
    WKVB = dscr("WKVB", [128, 8, 2 * D], BF16)
    WOB = dscr("WOB", [128, 8, D], BF16)
    W1B = dscr("W1B", [128, 8, 4 * D], BF16)
    W2B = dscr("W2B", [128, 32, D], BF16)

    BSTD = dscr("BSTD", [NCH, 128, 512], BF16)
    ATTF = dscr("ATTF", [8, 64, NT], F32)
    XST_D = dscr("XST_D", [NMT, 128, 4, 512], F32)
    BCT_D = dscr("BCT_D", [NMT, 128, 4, 512], BF16)
    SZ_D = dscr("SZ_D", [NMT, 128, 4, 512], F32)
    with ExitStack() as es:
        tr = TR(nc, es)
        _n = [0]

        def sb(shape, dt, name=None):
            _n[0] += 1
            return es.enter_context(nc.sbuf_tensor("s_" + (name or f"t{_n[0]}"), shape, dt))

        ARW = 43008
        ARENA = sb([128, ARW], F32, "arena")
        apos = [0]

        def phase():
            if os.environ.get('PHASE_DBG'):
                print('phase boundary', dict(tr.cnt))
            tr.barrier()
            apos[0] = 0

        def ph(shape, dt, parts=128):
            n = 1
            for v in shape[1:]:
                n *= v
            w = n if dt == F32 else (n + 1) // 2
            o = apos[0]
            apos[0] += w
            assert apos[0] <= ARW, ("arena overflow", apos[0])
            v = ARENA[0:shape[0], o:o + w]
            if dt != F32:
                v = v.bitcast(dt)
            if len(shape) == 3:
                v = v.rearrange("p (a b) -> p a b", a=shape[1])
            elif len(shape) == 4:
                v = v.rearrange("p (a b c) -> p a b c", a=shape[1], b=shape[2])
            return v

        PB = [es.enter_context(nc.psum_tensor(f"pb{i}", [128, 512], F32)) for i in range(8)]
        pk = lambda i: ('ps', i)
        pbf = lambda i: PB[i][:].bitcast(BF16)

        CF = sb([128, NCM, 128], F32, "CF")
        CB = sb([128, NCM, 128], BF16, "CB")
        vtab = sb([128, 18], F32, "vtab")
        epsT = sb([128, 1], F32, "epsT")
        tr.dma('sp', CF[:].rearrange("p a b -> p (a b)"), cst_d, key=('c0', 1), writes=['CF'])
        tr.dma('sp', vtab[:], vt_d, key=('c0', 2), writes=['vtab'])
        tr.op('dve', 'tensor_copy', out=CB[:], in_=CF[:], reads=['CF'], writes=['CB'])
        tr.op('dve', 'memset', epsT[:], EPS, writes=['eps'])
        cf = lambda i: CF[:, i, :]
        cb = lambda i: CB[:, i, :]
        FUSE = vtab[:, 1:2]

        gB = sb([128, 4, D], F32, "gB")
        for i, gd in enumerate([g_mix_d, g_xatt_d, g_mem_d, g_mlp_d]):
            tr.dma('sp', gB[:, i, :], gd.partition_broadcast(128), key=('c0', 3), writes=['gB'])
        gssd = sb([128, 512], F32, "gssd")
        tr.dma('sp', gssd[:], g_ssd_d.partition_broadcast(128), key=('c0', 4), writes=['gssd'])
        cw = sb([128, 8, 5], F32, "cw")
        cbias = sb([128, 8], F32, "cbias")
        for k in range(5):
            tr.dma('sp', cw[:, :, k:k + 1], convw_d[k:k + 1, :].rearrange("o (c p) -> p c o", p=128), key=('c0', 5),
                   writes=['cw'], allow_slow_non_contiguous=True)
        tr.dma('sp', cbias[:].unsqueeze(2), convb_d.rearrange("o (c p) -> p c o", p=128), key=('c0', 6), writes=['cw'],
               allow_slow_non_contiguous=True)
        Abc = sb([128, 16], F32, "Abc")
        dtb = sb([128, 16], F32, "dtb")
        Dbc = sb([128, 8], F32, "Dbc")
        tr.dma('sp', Abc[:], alog_d.partition_broadcast(128), key=('c0', 7), writes=['Abc'])
        tr.dma('sp', dtb[:], dtb_d.partition_broadcast(128), key=('c0', 8), writes=['dtb'])
        tr.dma('sp', Dbc[:], dsk_d.partition_broadcast(128), key=('c0', 9), writes=['Dbc'])
        tr.op('act', 'activation', out=Abc[:], in_=Abc[:], func=AF.Exp, reads=['Abc'], writes=['Abc'])
        tr.op('dve', 'tensor_scalar', out=Abc[:], in0=Abc[:], scalar1=-1.0, scalar2=None, op0=ALU.mult,
              reads=['Abc'], writes=['Abc'])
        gqk = sb([128, 2], F32, "gqk")
        for hb in (0, 64):
            tr.dma('sp', gqk[hb:hb + 64, 0:1], gq_d, key=('c0', 10), writes=['gqk'])
            tr.dma('sp', gqk[hb:hb + 64, 1:2], gk_d, key=('c0', 11), writes=['gqk'])
        gao = sb([64, 8], F32, "gao")
        tr.dma('sp', gao[:], gao_d.rearrange("h e -> e h"), key=('c0', 12), writes=['gao'], allow_slow_non_contiguous=True)
        gx = sb([128, 4], F32, "gx")
        tr.dma('sp', gx[:, 0:2], gxq_d.rearrange("c p -> p c"), key=('c0', 13), writes=['gx'], allow_slow_non_contiguous=True)
        tr.dma('sp', gx[:, 2:4], gxk_d.rearrange("c p -> p c"), key=('c0', 14), writes=['gx'], allow_slow_non_contiguous=True)
        ssq2 = sb([128, 2], F32, "ssq2")
        rstd2 = sb([128, 2], F32, "rstd2")
        xt = sb([128, 2, D], F32, "xt")
        xn2 = sb([128, 2, D], BF16, "xn2")

        phase()
        wst = ph([128, 2, 2048], F32)
        wsb16 = ph([128, 2, 2048], BF16)
        wi = [0]

        pw_steps = []

        def cast_w(nm, src, dst, nk, ncol):
            for k in range(nk):
                for c0 in range(0, ncol, 2048):
                    cn = min(2048, ncol - c0)

                    def step(k=k, c0=c0, cn=cn):
                        s = wi[0] % 2
                        wi[0] += 1
                        tr.dma('sp', wst[:, s, 0:cn], src[k * 128:(k + 1) * 128, c0:c0 + cn], key=('wst', s),
                               writes=[('wst', s)])
                        if wi[0] % 2 == 0:
                            tr.op('dve', 'tensor_copy', out=wsb16[:, s, 0:cn], in_=wst[:, s, 0:cn], reads=[('wst', s)],
                                  writes=[('wsb', s)])
                        else:
                            tr.op('act', 'copy', out=wsb16[:, s, 0:cn], in_=wst[:, s, 0:cn], reads=[('wst', s)],
                                  writes=[('wsb', s)])
                        tr.dma('pool', dst[:, k, c0:c0 + cn], wsb16[:, s, 0:cn], key=('wsbo', s), reads=[('wsb', s)],
                               writes=[nm])
                    pw_steps.append(step)

        for nm, src, dst, nk, ncol in [("WINB", w_in_d, WINB, 8, INC), ("WOUTB", w_out_d, WOUTB, 8, D),
                                       ("WQB", wq_d, WQB, 8, D), ("WKVB", wkv_d, WKVB, 8, 2 * D),
                                       ("WOB", wo_d, WOB, 8, D), ("W1B", w1_d, W1B, 8, 4 * D),
                                       ("W2B", w2_d, W2B, 32, D)]:
            cast_w(nm, src, dst, nk, ncol)

        nslot = [0]

        def rms_rstd(src_ap, n, key_src, sl_, width):
            tr.op('act', 'activation', out=xn2[:, sl_, 0:width], in_=src_ap, func=AF.Square, accum_out=ssq2[:, sl_:sl_ + 1],
                  reads=[key_src], writes=[('xn', sl_), ('ssq', sl_)])
            tr.op('act', 'activation', out=rstd2[:, sl_:sl_ + 1], in_=ssq2[:, sl_:sl_ + 1], func=AF.Ln, scale=1.0 / n,
                  bias=epsT[:], reads=[('ssq', sl_), 'eps'], writes=[('rstd', sl_)])
            tr.op('act', 'activation', out=rstd2[:, sl_:sl_ + 1], in_=rstd2[:, sl_:sl_ + 1], func=AF.Exp, scale=-0.5,
                  reads=[('rstd', sl_)], writes=[('rstd', sl_)])

        def norm_T(src, skey, gi, dst, dkey, j, tbank):
            sl_ = nslot[0] % 2
            nslot[0] += 1
            rms_rstd(src, D, skey, sl_, D)
            tr.op('dve', 'scalar_tensor_tensor', out=xn2[:, sl_, :], in0=src, scalar=rstd2[:, sl_:sl_ + 1], in1=gB[:, gi, :],
                  op0=ALU.mult, op1=ALU.mult, reads=[skey, ('rstd', sl_), 'gB'], writes=[('xn', sl_)])
            pt = pbf(tbank)
            for k in range(8):
                tr.op('pe', 'transpose', out=pt[:, k * 128:(k + 1) * 128], in_=xn2[:, sl_, k * 128:(k + 1) * 128],
                      identity=cb(C_ID), reads=[('xn', sl_), 'CB'], writes=[pk(tbank)])
            tr.op('act', 'copy', out=dst[:, :, j * 128:(j + 1) * 128],
                  in_=pt.rearrange("p (k t) -> p k t", k=8), writes=[pk(tbank), dkey])

        xnT_st = ph([128, 8, 512], BF16)
        zt = ph([128, 8, 2], BF16)
        tr.op('dve', 'memset', zt, 0.0, writes=['zt'])
        tr.dma('sp', XNT[:, :, 0:2].rearrange("k p t -> p k t"), zt, key='xnt_o', reads=['zt'], writes=['XNT'])
        tr.dma('sp', XNT[:, :, NT + 2:NT + 4].rearrange("k p t -> p k t"), zt, key='xnt_o', reads=['zt'],
               writes=['XNT'])
        for mt in range(NMT):
            for j in range(4):
                t0 = mt * 512 + j * 128
                s = (mt * 4 + j) % 2
                tr.dma('sp', xt[:, s, :], x_d[t0:t0 + 128, :], key=('xt', s), writes=[('xt', s)])
                norm_T(xt[:, s, :], ('xt', s), 0, xnT_st, 'xnT_st', j, j % 2)
                for _ in range(2):
                    if pw_steps:
                        pw_steps.pop(0)()
            tr.dma('sp', XNT[:, :, 2 + mt * 512:2 + mt * 512 + 512].rearrange("k p t -> p k t"), xnT_st,
                   key='xnt_o', reads=['xnT_st'], writes=['XNT'])

        while pw_steps:
            pw_steps.pop(0)()
        phase()
        w_ssd = ph([128, 8, 1552], BF16)
        tr.dma('sp', w_ssd, WINB[:, :, 0:1552], key='w_ssd', writes=['w_ssd'])
        xnTm = ph([128, 2, 8, 516], BF16)
        uT = ph([128, 2, 516], F32)
        cacc = ph([128, 2, 512], F32)
        xsT = ph([128, 2, 4, 512], F32)
        BCT = ph([128, 2, 4, 512], BF16)
        xs = ph([128, 2, 512], F32)
        Btm = ph([128, 2, 256], BF16)
        dtr = ph([128, 2, 16], F32)
        sp1 = ph([128, 2, 16], F32)
        dt = ph([128, 2, 16], F32)
        dtA = ph([128, 2, 16], F32)
        TT = ph([128, 2, 48], F32)
        EE = ph([128, 2, 48], F32)
        dtd = ph([128, 2, 16], F32)
        xdt = ph([128, 2, 2, 512], BF16)
        xdtd = ph([128, 2, 512], BF16)
        R1 = ph([128, 2, 2, 8, 128], F32) if False else ph([128, 4, 8, 128], F32)
        lm = ph([128, 4, 8, 128], F32)
        cbm = ph([128, 4, 2, 128], F32)
        WT = ph([128, 4, 8, 128], BF16)
        sz = ph([128, 2, 512], F32)
        yb = ph([128, 2, 512], F32)
        t2 = ph([128, 2, 512], F32)
        yn = ph([128, 2, 512], BF16)
        ssdT_st = ph([128, 2, 4, 512], BF16)
        stF = ph([128, 512], F32)
        stB = ph([128, 512], F32)
        prevF = ph([128, 2, 512], BF16)
        bstl = ph([128, 2, 512], BF16)
        szst = ph([128, 2, 512], F32)
        Dfull = ph([128, 8, 64], F32)
        tr.op('dve', 'tensor_copy', out=Dfull, in_=Dbc[:].unsqueeze(2).to_broadcast([128, 8, 64]),
              reads=['Dbc'], writes=['Dfull'])
        tr.op('dve', 'memset', stF, 0.0, writes=['stF'])
        tr.op('dve', 'memset', stB, 0.0, writes=['stB'])
        tr.op('dve', 'memset', prevF, 0.0, writes=[('prevF', 0), ('prevF', 1)])
        b8 = lambda ap16: ap16.unsqueeze(2).to_broadcast([128, 8, 64])
        v3 = lambda t: t.rearrange("p (h e) -> p h e", h=8)

        def run_ssd(mode):
            fwd = (mode == 'fwd')
            cached = (mode != 'prep')
            chunks = list(range(NCH)) if mode != 'bwd' else list(range(NCH - 1, -1, -1))
            macros = list(range(NMT)) if mode != 'bwd' else list(range(NMT - 1, -1, -1))
            n = len(chunks)
            di = 0 if fwd else 1
            hs = slice(di * 8, di * 8 + 8)
            st = stF if fwd else stB
            skey = 'stF' if fwd else 'stB'

            def prep_load(mi):
                mt = macros[mi]
                ms = mi % 2
                tr.dma('sp', xnTm[:, ms], XNT[:, :, mt * 512:mt * 512 + 516].rearrange("k p t -> p k t"),
                       key=('xnTm', ms), writes=[('xnTm', ms)])
                if cached:
                    tr.dma('sp', xsT[:, ms], XST_D[mt], key=('xsT_l', ms), reads=[('XSC', mt)], writes=[('xsT', ms)])

            def prep_load_b(mi):
                mt = macros[mi]
                ms = mi % 2
                tr.dma('sp', BCT[:, ms], BCT_D[mt], key=('BCT_l', ms), reads=[('XSC', mt)], writes=[('BCT', ms)])

            def prep_A(mi, c):
                if cached:
                    return
                mt = macros[mi]
                ms = mi % 2
                xm = xnTm[:, ms]
                bk = 1 + (c % 2)
                col = 512 + c * 128
                for k in range(8):
                    tr.op('pe', 'matmul', PB[bk][:], lhsT=w_ssd[:, k, col:col + 128], rhs=xm[:, k, 2:514],
                          start=(k == 0), stop=(k == 7), reads=['w_ssd', ('xnTm', ms)], writes=[pk(bk)])
                tr.op('act', 'copy', out=uT[:, c % 2, 2:514], in_=PB[bk][:], writes=[pk(bk), ('uT', c % 2)])
                for hh, (a, b) in enumerate([(0, 2), (514, 516)]):
                    for k in range(8):
                        tr.op('pe', 'matmul', PB[3][:, 480 + 2 * hh:482 + 2 * hh], lhsT=w_ssd[:, k, col:col + 128],
                              rhs=xm[:, k, a:b], start=(k == 0), stop=(k == 7), skip_group_check=True,
                              reads=['w_ssd', ('xnTm', ms)], writes=[pk(3)])
                for hh, (a, b) in enumerate([(0, 2), (514, 516)]):
                    atb = (hh == 0 and mt * 512 == SEG) or (hh == 1 and mt * 512 + 512 == SEG)
                    if atb:
                        tr.op('act', 'activation', out=uT[:, c % 2, a:b], in_=PB[3][:, 480 + 2 * hh:482 + 2 * hh],
                              func=AF.Copy, scale=FUSE, reads=['vtab'], writes=[pk(3), ('uT', c % 2)])
                    else:
                        tr.op('act', 'copy', out=uT[:, c % 2, a:b], in_=PB[3][:, 480 + 2 * hh:482 + 2 * hh],
                              writes=[pk(3), ('uT', c % 2)])
                ca = cacc[:, c % 2]
                tr.op('act', 'activation', out=ca, in_=uT[:, c % 2, 0:512], func=AF.Identity, scale=cw[:, c, 0:1],
                      bias=cbias[:, c:c + 1], reads=[('uT', c % 2), 'cw'], writes=[('cacc', c % 2)])

            def prep_B(mi, c):
                mt = macros[mi]
                ms = mi % 2
                xm = xnTm[:, ms]
                ca = cacc[:, c % 2]
                for k in range(1, 5):
                    eng = 'dve'
                    tr.op(eng, 'scalar_tensor_tensor', out=ca, in0=uT[:, c % 2, k:k + 512], scalar=cw[:, c, k:k + 1],
                          in1=ca, op0=ALU.mult, op1=ALU.add, reads=[('uT', c % 2), 'cw'], writes=[('cacc', c % 2)])
                if c < 4:
                    tr.op('act', 'activation', out=xsT[:, ms, c, :], in_=ca, func=AF.Silu, reads=[('cacc', c % 2)],
                          writes=[('xsT', ms)])
                else:
                    tr.op('act', 'activation', out=BCT[:, ms, c - 4, :], in_=ca, func=AF.Silu, reads=[('cacc', c % 2)],
                          writes=[('BCT', ms)])
                if c == 3:
                    tr.dma('pool', XST_D[mt], xsT[:, ms], key=('xsc_o', ms), reads=[('xsT', ms)], writes=[('XSCa', mt)])
                if c == 7:
                    tr.dma('pool', BCT_D[mt], BCT[:, ms], key=('xsc_o', ms), reads=[('BCT', ms)], writes=[('XSC', mt)])
                    for j in range(4):
                        zs = j % 2
                        for k in range(8):
                            tr.op('pe', 'matmul', PB[7][:], lhsT=xm[:, k, 2 + j * 128:2 + (j + 1) * 128], rhs=w_ssd[:, k, 0:512],
                                  start=(k == 0), stop=(k == 7), reads=['w_ssd', ('xnTm', ms)], writes=[pk(7)])
                        tr.op('act', 'activation', out=szst[:, zs], in_=PB[7][:], func=AF.Silu, writes=[pk(7), ('szst', zs)])
                        tr.dma('pool', SZ_D[mt, :, j, :], szst[:, zs], key=('sz_o', zs), reads=[('szst', zs)],
                               writes=[('SZ', mt, j)])


            def prep_c(mi, c):
                if cached:
                    return
                prep_A(mi, c)
                prep_B(mi, c)

            def S1(p):
                ch = chunks[p]
                mt, j = ch // 4, ch % 4
                ms = (p // 4) % 2
                s = p % 2
                xm = xnTm[:, ms]
                tk = slice(j * 128, (j + 1) * 128)
                for c in range(4):
                    tr.op('pe', 'transpose', out=PB[4][:, c * 128:(c + 1) * 128], in_=xsT[:, ms, c, tk], identity=cf(C_ID),
                          reads=[('xsT', ms), 'CF'], writes=[pk(4)])
                tr.op('act', 'copy', out=xs[:, s], in_=PB[4][:], writes=[pk(4), ('xs', s)])
                pt = pbf(0)
                for g in range(2):
                    tr.op('pe', 'transpose', out=pt[:, g * 128:(g + 1) * 128], in_=BCT[:, ms, g, tk], identity=cb(C_ID),
                          reads=[('BCT', ms), 'CB'], writes=[pk(0)])
                tr.op('act', 'copy', out=Btm[:, s], in_=pt[:, 0:256], writes=[pk(0), ('Btm', s)])
                for k in range(8):
                    tr.op('pe', 'matmul', PB[3][:, 0:16], lhsT=xm[:, k, 2 + j * 128:2 + (j + 1) * 128],
                          rhs=w_ssd[:, k, 1536:1552], start=(k == 0), stop=(k == 7), skip_group_check=True,
                          reads=['w_ssd', ('xnTm', ms)], writes=[pk(3)])
                tr.op('dve', 'tensor_tensor', out=dtr[:, s], in0=PB[3][:, 0:16], in1=dtb[:], op=ALU.add,
                      reads=['dtb'], writes=[pk(3), ('dtr', s)])
                tr.op('act', 'activation', out=sp1[:, s], in_=dtr[:, s], func=AF.Abs, reads=[('dtr', s)],
                      writes=[('sp1', s)])
                tr.op('act', 'activation', out=sp1[:, s], in_=sp1[:, s], func=AF.Exp, scale=-1.0, reads=[('sp1', s)],
                      writes=[('sp1', s)])
                tr.op('act', 'activation', out=sp1[:, s], in_=sp1[:, s], func=AF.Ln, bias=1.0, reads=[('sp1', s)],
                      writes=[('sp1', s)])
                tr.op('dve', 'scalar_tensor_tensor', out=dt[:, s], in0=dtr[:, s], scalar=0.0, in1=sp1[:, s], op0=ALU.max,
                      op1=ALU.add, reads=[('dtr', s), ('sp1', s)], writes=[('dt', s)])
                tr.op('dve', 'tensor_tensor', out=dtA[:, s], in0=dt[:, s], in1=Abc[:], op=ALU.mult,
                      reads=[('dt', s), 'Abc'], writes=[('dtA', s)])
                if fwd:
                    for dd, tri in enumerate([C_U, C_L]):
                        tr.op('pool', 'tensor_tensor', out=R1[:, 2 * s + dd],
                              in0=cf(tri).unsqueeze(1).to_broadcast([128, 8, 128]),
                              in1=dtA[:, s, dd * 8:dd * 8 + 8].unsqueeze(2).to_broadcast([128, 8, 128]), op=ALU.mult,
                              reads=['CF', ('dtA', s)], writes=[('R1', s, dd)])
                    tr.dma('sp', sz[:, s], SZ_D[mt, :, j, :], key=('sz_l', s), reads=[('SZ', mt, j)], writes=[('sz', s)])

            def S2(p):
                ch = chunks[p]
                mt, j = ch // 4, ch % 4
                ms = (p // 4) % 2
                s = p % 2
                tk = slice(j * 128, (j + 1) * 128)
                tr.op('pe', 'matmul', PB[3][:, 16:24], lhsT=cf(C_U), rhs=dtA[:, s, 0:8], start=True, stop=True,
                      skip_group_check=True, reads=['CF', ('dtA', s)], writes=[pk(3)])
                tr.op('pe', 'matmul', PB[3][:, 24:32], lhsT=cf(C_L), rhs=dtA[:, s, 8:16], start=False, stop=True,
                      skip_group_check=True, reads=['CF', ('dtA', s)], writes=[pk(3)])
                tr.op('pe', 'matmul', PB[3][:, 32:48], lhsT=cf(C_ONES), rhs=dtA[:, s, 0:16], start=False, stop=True,
                      skip_group_check=True, reads=['CF', ('dtA', s)], writes=[pk(3)])
                tr.op('act', 'copy', out=TT[:, s, 0:32], in_=PB[3][:, 16:48], writes=[pk(3), ('TT', s)])
                tr.op('dve', 'tensor_tensor', out=TT[:, s, 32:48], in0=TT[:, s, 16:32], in1=TT[:, s, 0:16], op=ALU.subtract,
                      reads=[('TT', s)], writes=[('TT', s)])
                tr.op('act', 'activation', out=EE[:, s], in_=TT[:, s], func=AF.Exp, reads=[('TT', s)], writes=[('EE', s)])
                tr.op('dve', 'tensor_tensor', out=dtd[:, s], in0=dt[:, s], in1=EE[:, s, 32:48], op=ALU.mult,
                      reads=[('dt', s), ('EE', s)], writes=[('dtd', s)])
                tr.op('dve', 'tensor_tensor', out=v3(xdtd[:, s]), in0=v3(xs[:, s]), in1=b8(dtd[:, s, hs]), op=ALU.mult,
                      reads=[('xs', s), ('dtd', s)], writes=[('xdtd', s)])
                if fwd:
                    tr.dma('sp', bstl[:, s, :], BSTD[ch], key=('bstl', s), reads=[('BSTD', ch)], writes=[('bstl', s)])
                    for dd in range(2):
                        tr.op('dve', 'tensor_tensor', out=v3(xdt[:, s, dd]), in0=v3(xs[:, s]),
                              in1=b8(dt[:, s, dd * 8:dd * 8 + 8]), op=ALU.mult, reads=[('xs', s), ('dt', s)],
                              writes=[('xdt', s)])
                    for dd, lt in enumerate([C_LS, C_US]):
                        for hf in range(2):
                            bk = 5 + hf
                            tr.op('pe', 'matmul', PB[bk][:], lhsT=cf(lt),
                                  rhs=R1[:, 2 * s + dd, hf * 4:hf * 4 + 4, :].rearrange("p h i -> p (h i)"), start=True,
                                  stop=True, reads=['CF', ('R1', s, dd)], writes=[pk(bk)])
                            tr.op('act', 'activation',
                                  out=lm[:, 2 * s + dd, hf * 4:hf * 4 + 4, :].rearrange("p h i -> p (h i)"),
                                  in_=PB[bk][:], func=AF.Exp, writes=[pk(bk), ('lm', s, dd)])
                    for g in range(2):
                        tr.op('pe', 'matmul', PB[7][:, g * 128:(g + 1) * 128], lhsT=BCT[:, ms, g, tk], rhs=BCT[:, ms, 2 + g, tk],
                              start=(g == 0), stop=True, skip_group_check=True, reads=[('BCT', ms)], writes=[pk(7)])
                    for dd, tri in enumerate([C_U, C_L]):
                        tr.op('dve', 'tensor_tensor', out=cbm[:, 2 * s + dd],
                              in0=PB[7][:, 0:256].rearrange("p (g i) -> p g i", g=2),
                              in1=cf(tri).unsqueeze(1).to_broadcast([128, 2, 128]), op=ALU.mult,
                              reads=['CF'], writes=[pk(7), ('cbm', s, dd)])
                        tr.op('dve', 'tensor_tensor', out=WT[:, 2 * s + dd].rearrange("p (g r) i -> p g r i", g=2),
                              in0=lm[:, 2 * s + dd].rearrange("p (g r) i -> p g r i", g=2),
                              in1=cbm[:, 2 * s + dd].unsqueeze(2).to_broadcast([128, 2, 4, 128]), op=ALU.mult,
                              reads=[('lm', s, dd), ('cbm', s, dd)], writes=[('WT', s, dd)])
                if not fwd:
                    tr.op('act', 'copy', out=bstl[:, s, :], in_=stB, reads=['stB'], writes=[('bstl', s)])
                    tr.dma('sp', BSTD[ch], bstl[:, s, :], key=('bsto', s), reads=[('bstl', s)], writes=[('BSTD', ch)])
                for g in range(2):
                    tr.op('pe', 'matmul', PB[1][:, g * 256:(g + 1) * 256], lhsT=Btm[:, s, g * 128:(g + 1) * 128],
                          rhs=xdtd[:, s, g * 256:(g + 1) * 256], start=(g == 0), stop=True, skip_group_check=True,
                          reads=[('Btm', s), ('xdtd', s)], writes=[pk(1)])
                tr.op('dve', 'tensor_tensor', out=v3(st), in0=v3(st), in1=b8(EE[:, s, 16 + di * 8:24 + di * 8]),
                      op=ALU.mult, reads=[('EE', s)], writes=[skey])
                tr.op('dve', 'tensor_tensor', out=st, in0=st, in1=PB[1][:], op=ALU.add, writes=[pk(1), skey])
                if (fwd and (ch + 1) * 128 == SEG) or ((not fwd) and ch * 128 == SEG):
                    tr.op('dve', 'tensor_scalar', out=st, in0=st, scalar1=FUSE, scalar2=None, op0=ALU.mult,
                          reads=['vtab'], writes=[skey])
                if fwd:
                    tr.op('dve', 'tensor_copy', out=prevF[:, (p + 1) % 2], in_=stF, reads=['stF'], writes=[('prevF', (p + 1) % 2)])

            def S3(p):
                ch = chunks[p]
                mt, j = ch // 4, ch % 4
                ms = (p // 4) % 2
                s = p % 2
                tk = slice(j * 128, (j + 1) * 128)
                first = True
                for dd in range(2):
                    for h in range(8):
                        tr.op('pe', 'matmul', PB[4][:, h * 64:(h + 1) * 64], lhsT=WT[:, 2 * s + dd, h, :],
                              rhs=xdt[:, s, dd, h * 64:(h + 1) * 64], start=first, stop=True, skip_group_check=True,
                              reads=[('WT', s, dd), ('xdt', s)], writes=[pk(4)])
                        first = False
                for g in range(2):
                    tr.op('pe', 'matmul', PB[5][:, g * 256:(g + 1) * 256], lhsT=BCT[:, ms, 2 + g, tk],
                          rhs=prevF[:, s, g * 256:(g + 1) * 256], start=(g == 0), stop=True, skip_group_check=True,
                          reads=[('BCT', ms), ('prevF', s)], writes=[pk(5)])
                    tr.op('pe', 'matmul', PB[6][:, g * 256:(g + 1) * 256], lhsT=BCT[:, ms, 2 + g, tk],
                          rhs=bstl[:, s, g * 256:(g + 1) * 256], start=(g == 0), stop=True, skip_group_check=True,
                          reads=[('BCT', ms), ('bstl', s)], writes=[pk(6)])
                y_ = yb[:, s]
                t_ = t2[:, s]
                tr.op('dve', 'tensor_tensor', out=v3(y_), in0=v3(PB[5][:]), in1=b8(EE[:, s, 0:8]), op=ALU.mult,
                      reads=[('EE', s)], writes=[pk(5), ('yb', s)])
                tr.op('dve', 'tensor_tensor', out=v3(t_), in0=v3(PB[6][:]), in1=b8(EE[:, s, 8:16]), op=ALU.mult,
                      reads=[('EE', s)], writes=[pk(6), ('t2', s)])
                tr.op('pool', 'tensor_tensor', out=y_, in0=y_, in1=t_, op=ALU.add, reads=[('t2', s)], writes=[('yb', s)])
                tr.op('dve', 'tensor_tensor', out=y_, in0=y_, in1=PB[4][:], op=ALU.add, writes=[pk(4), ('yb', s)])
                tr.op('pool', 'tensor_tensor', out=t_, in0=xs[:, s], in1=Dfull.rearrange("p h e -> p (h e)"),
                      op=ALU.mult, reads=[('xs', s), 'Dfull'], writes=[('t2', s)])
                tr.op('pool', 'tensor_tensor', out=y_, in0=y_, in1=t_, op=ALU.add, reads=[('t2', s)], writes=[('yb', s)])
                tr.op('pool', 'tensor_tensor', out=y_, in0=y_, in1=sz[:, s], op=ALU.mult, reads=[('sz', s)],
                      writes=[('yb', s)])

            def S3b(p):
                s = p % 2
                y_ = yb[:, s]
                rms_rstd(y_, 512, ('yb', s), s, 512)
                tr.op('dve', 'scalar_tensor_tensor', out=yn[:, s], in0=y_, scalar=rstd2[:, s:s + 1], in1=gssd[:],
                      op0=ALU.mult, op1=ALU.mult, reads=[('yb', s), ('rstd', s), 'gssd'], writes=[('yn', s)])

            def S4(p):
                ch = chunks[p]
                mt, j = ch // 4, ch % 4
                mo = mt % 2
                s = p % 2
                tk = slice(j * 128, (j + 1) * 128)
                pt = pbf(2)
                for c in range(4):
                    tr.op('pe', 'transpose', out=pt[:, c * 128:(c + 1) * 128], in_=yn[:, s, c * 128:(c + 1) * 128],
                          identity=cb(C_ID), reads=[('yn', s), 'CB'], writes=[pk(2)])
                tr.op('act', 'copy', out=ssdT_st[:, mo, :, tk], in_=pt[:, 0:512].rearrange("p (c t) -> p c t", c=4),
                      writes=[pk(2), ('ssdT_st', mo)])
                if j == 3:
                    tr.dma('sp', SSDT[:, :, mt * 512:(mt + 1) * 512].rearrange("c p t -> p c t"), ssdT_st[:, mo],
                           key=('ssdt_o', mo), reads=[('ssdT_st', mo)], writes=['SSDT'])

            if mode == 'prep':
                prep_load(0)
                for mi in range(len(macros)):
                    if mi + 1 < len(macros):
                        prep_load(mi + 1)
                    prep_A(mi, 0)
                    for c in range(8):
                        if c + 1 < 8:
                            prep_A(mi, c + 1)
                        prep_B(mi, c)
                return
            for t in range(-7, n + 3):
                if fwd:
                    if 0 <= t - 2 < n:
                        S4(t - 2)
                    if 0 <= t - 1 < n:
                        S3b(t - 1)
                    if 0 <= t < n:
                        S3(t)
                if 0 <= t + 1 < n:
                    S2(t + 1)
                if 0 <= t + 2 < n:
                    S1(t + 2)
                for mi in range(len(macros)):
                    o = t - (4 * mi - 7)
                    if o == 0:
                        prep_load(mi)
                    elif 1 <= o <= 4:
                        prep_c(mi, 2 * (o - 1))
                        prep_c(mi, 2 * (o - 1) + 1)
                        if cached and o == 3:
                            prep_load_b(mi)

        run_ssd('prep')
        run_ssd('bwd')
        run_ssd('fwd')

        if stop_after not in ('ssd',):
            phase()
            w_v = ph([128, 8, 512], BF16)
            tr.dma('sp', w_v, WINB[:, :, 2576:3088], key='w_v', writes=['w_v'])
            xmv = ph([128, 2, 8, 512], BF16)
            vst = ph([128, 2, 4, 520], BF16)
            zrow = ph([128, 520], BF16)
            tr.op('dve', 'memset', zrow, 0.0, writes=['zrow'])
            tr.op('dve', 'memset', vst, 1.0, writes=[('vst', 0), ('vst', 1)])
            for i in range(PADV // 128):
                tr.dma('pool', VS[i * 128:(i + 1) * 128, :], zrow, key='vs_z', reads=['zrow'], writes=['VS'])
                tr.dma('pool', VS[PADV + NT + i * 128:PADV + NT + (i + 1) * 128, :], zrow, key='vs_z', reads=['zrow'],
                       writes=['VS'])
            for mt in range(NMT):
                s = mt % 2
                tr.dma('sp', xmv[:, s], XNT[:, :, 2 + mt * 512:2 + mt * 512 + 512].rearrange("k p t -> p k t"),
                       key=('xmv', s), writes=[('xmv', s)])
                for j in range(4):
                    bk = j % 2
                    for k in range(8):
                        tr.op('pe', 'matmul', PB[bk][:], lhsT=xmv[:, s, k, j * 128:(j + 1) * 128], rhs=w_v[:, k, :],
                              start=(k == 0), stop=(k == 7), reads=['w_v', ('xmv', s)], writes=[pk(bk)])
                    tr.op('act', 'copy', out=vst[:, s, j, :].rearrange("p (h e) -> p h e", h=8)[:, :, 0:64],
                          in_=PB[bk][:].rearrange("p (h e) -> p h e", h=8), writes=[pk(bk), ('vst', s)])
                tr.dma('pool', VS[PADV + mt * 512:PADV + (mt + 1) * 512, :].rearrange("(j p) c -> p j c", p=128),
                       vst[:, s], key=('vs_o', s), reads=[('vst', s)], writes=['VS'])

        if stop_after not in ('ssd', 'p3v'):
            phase()
            w_qk = ph([128, 8, 1024], BF16)
            tr.dma('sp', w_qk, WINB[:, :, 1552:2576], key='w_qk', writes=['w_qk'])
            kT = ph([128, 4, 4096], BF16)
            qT = ph([128, 4, 2048], BF16)
            xma = ph([128, 2, 8, 512], BF16)
            cst = ph([128, 2, 2, 512], F32)
            NRS = 2
            sqb = ph([128, NRS, 512], BF16)
            rsn = ph([128, 2, 512], F32)
            qn = ph([128, 2, 512], F32)
            qnb = ph([128, NRS, 512], BF16)
            tA = ph([128, 2, 512], F32)
            tB = ph([128, 2, 512], F32)
            Vt2 = ph([128, 69, 130], BF16)
            NS = 4
            LAG = 3
            PT = ph([128, NS, 256], BF16)
            accs = ph([65, 2, 2048], F32)
            rden = ph([64, 2, 512], F32)
            sqa = ph([64, 2, 512], BF16)
            ssacc = ph([64, 2048], F32)
            attl = rden
            attT_st = ph([64, 8, 512], BF16)
            tr.op('dve', 'memset', kT, 0.0, writes=['kT'])
            MLU = ph([128, 256], BF16)
            tr.op('dve', 'tensor_copy', out=MLU[:, 0:128], in_=cf(C_L), reads=['CF'], writes=['MLU'])
            tr.op('dve', 'tensor_copy', out=MLU[:, 128:256], in_=cf(C_U), reads=['CF'], writes=['MLU'])
            VCOL = {(1, 1): 0, ('f', 'f'): 1, (0, 0): 2, (1, 'f'): 3, ('f', 1): 4, (1, 0): 5, (0, 1): 6, ('f', 0): 7,
                    (0, 'f'): 8}
            NRS = 2

            def nr_stages(u, unit):
                col, gcol, dst, dkey, xs_ = unit
                s = u % NRS
                s2_ = u % 2
                pbank = u % 4
                b2 = 4 + (u % 2)
                b3 = 6 + (u % 2)

                def s1():
                    for k in range(8):
                        tr.op('pe', 'matmul', PB[pbank][:], lhsT=w_qk[:, k, col:col + 128], rhs=xma[:, xs_, k, :],
                              start=(k == 0), stop=(k == 7), reads=['w_qk', ('xma', xs_)], writes=[pk(pbank)])

                def s2():
                    tr.op('act', 'activation', out=sqb[:, s], in_=PB[pbank][:], func=AF.Square, reads=[pk(pbank)],
                          writes=[('sqb', s)])
                    tr.op('pe', 'matmul', PB[b2][:], lhsT=cb(C_BLK), rhs=sqb[:, s], start=True, stop=True,
                          reads=['CB', ('sqb', s)], writes=[pk(b2)])

                def s3():
                    tr.op('act', 'activation', out=rsn[:, s2_], in_=PB[b2][:], func=AF.Ln, scale=1.0 / 64, bias=epsT[:],
                          reads=['eps'], writes=[pk(b2), ('rsn', s2_)])
                    tr.op('act', 'activation', out=rsn[:, s2_], in_=rsn[:, s2_], func=AF.Exp, scale=-0.5, reads=[('rsn', s2_)], writes=[('rsn', s2_)])
                    tr.op('dve', 'scalar_tensor_tensor', out=qn[:, s2_], in0=PB[pbank][:], scalar=gqk[:, gcol:gcol + 1],
                          in1=rsn[:, s2_], op0=ALU.mult, op1=ALU.mult, reads=['gqk', ('rsn', s2_)],
                          writes=[pk(pbank), ('qn', s2_)])
                    tr.op('act', 'copy', out=qnb[:, s], in_=qn[:, s2_], reads=[('qn', s2_)], writes=[('qnb', s)])
                    tr.op('pe', 'matmul', PB[b3][:], lhsT=cb(C_P), rhs=qnb[:, s], start=True, stop=True,
                          reads=['CB', ('qnb', s)], writes=[pk(b3)])
                    tr.op('pool', 'tensor_tensor', out=tB[:, s2_], in0=qn[:, s2_], in1=cst[:, xs_, 0, :], op=ALU.mult,
                          reads=[('qn', s2_), ('cst', xs_)], writes=[('tB', s2_)])

                def s4():
                    tr.op('dve', 'tensor_tensor', out=tA[:, s2_], in0=PB[b3][:], in1=cst[:, xs_, 1, :], op=ALU.mult,
                          reads=[('cst', xs_)], writes=[pk(b3), ('tA', s2_)])
                    tr.op('dve', 'tensor_tensor', out=dst, in0=tA[:, s2_], in1=tB[:, s2_], op=ALU.add,
                          reads=[('tA', s2_), ('tB', s2_)], writes=[dkey])
                return [s1, s2, s3, s4]

            def run_units(units, hooks):
                st = [nr_stages(u, un) for u, un in enumerate(units)]
                n = len(st)
                for t in range(n + 3):
                    for fn in hooks.get(t, []):
                        fn()
                    for k in range(4):
                        if 0 <= t - k < n:
                            st[t - k][k]()

            slot_ctr = [0]
            pending = []

            def run_pending(nmax):
                k = 0
                while pending and k < nmax:
                    fn = pending.pop(0)
                    sl = slot_ctr[0] % NS
                    fn(sl)
                    k += 1

            def make_post(h, a0):
                ap_ = h % 2
                steps = []
                for b_ in range(4):
                    qs = slice(b_ * 512, (b_ + 1) * 512)

                    def st1(sl, b_=b_, qs=qs):
                        r_ = b_ % 2
                        tr.op('pe', 'matmul', PB[sl][0:64, :], lhsT=CF[0:65, C_SEL, 0:64], rhs=accs[:, ap_, qs], start=True,
                              stop=True, reads=['CF', ('accs', ap_)], writes=[pk(sl)])
                        tr.op('act', 'activation', out=rden[:, r_], in_=PB[sl][0:64, :], func=AF.Ln, writes=[pk(sl), ('rden', r_)])
                        tr.op('act', 'activation', out=rden[:, r_], in_=rden[:, r_], func=AF.Exp, scale=-1.0, reads=[('rden', r_)], writes=[('rden', r_)])
                        tr.op('dve', 'tensor_tensor', out=accs[0:64, ap_, qs], in0=accs[0:64, ap_, qs], in1=rden[:, r_],
                              op=ALU.mult, reads=[('rden', r_)], writes=[('accs', ap_)])
                        tr.op('act', 'activation', out=sqa[:, r_], in_=accs[0:64, ap_, qs], func=AF.Square,
                              reads=[('accs', ap_)], writes=[('sqa', r_)])

                    def st2(sl, b_=b_, qs=qs):
                        r_ = b_ % 2
                        tr.op('pe', 'matmul', PB[sl][0:64, :], lhsT=CB[0:64, C_ONES, 0:64], rhs=sqa[:, r_], start=True,
                              stop=True, reads=['CB', ('sqa', r_)], writes=[pk(sl)])
                        if h == 0:
                            tr.op('act', 'copy', out=ssacc[:, qs], in_=PB[sl][0:64, :], writes=[pk(sl), 'ssacc'])
                        else:
                            tr.op('dve', 'tensor_tensor', out=ssacc[:, qs], in0=ssacc[:, qs], in1=PB[sl][0:64, :],
                                  op=ALU.add, writes=[pk(sl), 'ssacc'])
                        if b_ == 3:
                            tr.dma('sp', ATTF[h, :, a0:a0 + AM], accs[0:64, ap_, :], key=('attf_o', ap_), reads=[('accs', ap_)],
                                   writes=[('ATTF', h)])
                    steps.append(st1)
                    steps.append(st2)
                return steps

            for am in range(NAM):
                a0 = am * AM
                sega = a0 // SEG
                valid = [sbk for sbk in range(8) if 0 <= a0 - 1024 + sbk * 512 < NT]
                units = []
                hooks = {}

                def mk_load(i):
                    sbk = valid[i]
                    tok0 = a0 - 1024 + sbk * 512
                    xs_ = i % 2

                    def fn():
                        tr.dma('sp', xma[:, xs_], XNT[:, :, 2 + tok0:2 + tok0 + 512].rearrange("k p t -> p k t"),
                               key=('xma', xs_), writes=[('xma', xs_)])
                        tr.dma('sp', cst[:, xs_], cs_d[:, :, tok0:tok0 + 512].rearrange("a p t -> p a t"),
                               key=('cst', xs_), writes=[('cst', xs_)])
                    return fn

                for i, sbk in enumerate(valid):
                    f_i = len(units)
                    if i == 0:
                        hooks.setdefault(0, []).append(mk_load(0))
                    if i + 1 < len(valid):
                        hooks.setdefault(f_i + (4 if i > 0 else 0), []).append(mk_load(i + 1))
                    for c in range(4):
                        units.append((512 + c * 128, 1, kT[:, c, sbk * 512:(sbk + 1) * 512], 'kT', i % 2))
                        if 2 <= sbk < 6:
                            units.append((c * 128, 0, qT[:, c, (sbk - 2) * 512:(sbk - 1) * 512], 'qT', i % 2))
                run_units(units, hooks)
                for hp in range(4):
                    tix = {}
                    ti = 0
                    for d in (1, 4, 16):
                        nu = AM // (128 * d)
                        for r in range(d):
                            for u in range(nu + 1):
                                tb = a0 - 64 * d + 128 * d * u
                                st_ = []
                                for hb_ in (tb, tb + 64 * d):
                                    if hb_ < 0 or hb_ >= NT:
                                        st_.append(0)
                                    elif hb_ // SEG == sega:
                                        st_.append(1)
                                    else:
                                        st_.append('f')
                                st_ = tuple(st_)
                                if st_ == (0, 0):
                                    continue
                                row0 = PADV + tb + r
                                tr.dma('pool', Vt2[:, ti, :], VS[row0:row0 + 127 * d + 1:d, hp * 130:(hp + 1) * 130],
                                       key=('vt', ti // 9), reads=['VS'], writes=[('Vt2', ti)])
                                tix[(d, r, u)] = (ti, VCOL[st_])
                                ti += 1
                    for g_ in range((ti + 8) // 9):
                        tr.batch_end(('vt', g_), [('Vt2', t_) for t_ in range(g_ * 9, min(ti, g_ * 9 + 9))])
                    for hh in range(2):
                        h = hp * 2 + hh
                        if os.environ.get('PHASE_DBG') and am == 1:
                            print('head start', h, tr.cnt['pe'])
                        c = hp
                        rb = hh * 64
                        apar = h % 2
                        first_in_bank = [True] * 4
                        items = []
                        for d in (1, 4, 16):
                            nu = AM // (128 * d)
                            for r in range(d):
                                for ub in range(nu):
                                    blocks = []
                                    for bi, u in enumerate((ub, ub + 1)):
                                        if (d, r, u) in tix:
                                            ti_, vc = tix[(d, r, u)]
                                            blocks.append((bi, ti_, vc, 1024 - 64 * d + 128 * d * u + r))
                                    items.append((d, r, ub, 128 * d * ub + r, blocks))
                        slots = {}

                        def emit_S(i):
                            d, r, ub, qc0, blocks = items[i]
                            sl = slot_ctr[0] % NS
                            slot_ctr[0] += 1
                            slots[i] = sl
                            for (bi, ti_, vc, kc0) in blocks:
                                tr.op('pe', 'matmul', PB[sl][:, bi * 128:(bi + 1) * 128],
                                      lhsT=kT[rb:rb + 64, c, kc0:kc0 + 127 * d + 1:d],
                                      rhs=qT[rb:rb + 64, c, qc0:qc0 + 127 * d + 1:d], start=True, stop=True,
                                      skip_group_check=True, reads=['kT', 'qT'], writes=[pk(sl)])
                            lo_ = min(b_[0] for b_ in blocks)
                            hi_ = max(b_[0] for b_ in blocks) + 1
                            if len(blocks) == 2 and blocks[0][2] == blocks[1][2]:
                                vc = blocks[0][2]
                                tr.op('act', 'activation', out=PT[:, sl, :], in_=PB[sl][:, 0:256], func=AF.Exp,
                                      scale=0.125, bias=vtab[:, 9 + vc:10 + vc], reads=['vtab'],
                                      writes=[pk(sl), ('PT', sl)])
                            else:
                                for (bi, ti_, vc, kc0) in blocks:
                                    tr.op('act', 'activation', out=PT[:, sl, bi * 128:(bi + 1) * 128],
                                          in_=PB[sl][:, bi * 128:(bi + 1) * 128], func=AF.Exp, scale=0.125,
                                          bias=vtab[:, 9 + vc:10 + vc], reads=['vtab'], writes=[pk(sl), ('PT', sl)])
                            tr.op('dve', 'tensor_tensor', out=PT[:, sl, lo_ * 128:hi_ * 128],
                                  in0=PT[:, sl, lo_ * 128:hi_ * 128],
                                  in1=CB[:, C_L + lo_:C_L + hi_, :].rearrange("p a b -> p (a b)") if lo_ == 0 and hi_ == 1 else
                                  (MLU[:, lo_ * 128:hi_ * 128]), op=ALU.mult, reads=['MLU'], writes=[('PT', sl)])

                        def emit_PV(i):
                            d, r, ub, qc0, blocks = items[i]
                            sl = slots[i]
                            for (bi, ti_, vc, kc0) in blocks:
                                vl = Vt2[:, ti_, hh * 65:hh * 65 + 65]
                                if d == 16:
                                    pieces = [(b_, PB[4 + b_][0:65, r:512:16],
                                               PT[:, sl, bi * 128 + 32 * b_:bi * 128 + 32 * b_ + 32]) for b_ in range(4)]
                                elif d == 4:
                                    pieces = [(ub, PB[4 + ub][0:65, r:512:4], PT[:, sl, bi * 128:(bi + 1) * 128])]
                                else:
                                    b_ = qc0 // 512
                                    pieces = [(b_, PB[4 + b_][0:65, qc0 % 512:qc0 % 512 + 128],
                                               PT[:, sl, bi * 128:(bi + 1) * 128])]
                                for (b_, oap, rap) in pieces:
                                    tr.op('pe', 'matmul', oap, lhsT=vl, rhs=rap, start=first_in_bank[b_], stop=True,
                                          skip_group_check=True, reads=[('PT', sl), ('Vt2', ti_)], writes=[pk(4 + b_)])
                                    first_in_bank[b_] = False

                        n_it = len(items)
                        for i in range(n_it + LAG):
                            if i < n_it:
                                emit_S(i)
                            if i % 5 == 4:
                                run_pending(1)
                            if i >= LAG:
                                emit_PV(i - LAG)
                        run_pending(100)
                        for b_ in range(4):
                            tr.op('act', 'copy', out=accs[:, apar, b_ * 512:(b_ + 1) * 512], in_=PB[4 + b_][0:65, :],
                                  writes=[pk(4 + b_), ('accs', apar)])
                        pending.extend(make_post(h, a0))
                run_pending(100)
                tr.op('act', 'activation', out=ssacc, in_=ssacc, func=AF.Ln, scale=1.0 / 512, bias=epsT[0:64, :],
                      reads=['ssacc', 'eps'], writes=['ssacc'])
                tr.op('act', 'activation', out=ssacc, in_=ssacc, func=AF.Exp, scale=-0.5, reads=['ssacc'], writes=['ssacc'])
                for qb in range(4):
                    qs = slice(qb * 512, (qb + 1) * 512)
                    for h in range(8):
                        sl = h % 2
                        tr.dma('pool', attl[:, sl, :], ATTF[h, :, a0 + qb * 512:a0 + (qb + 1) * 512], key=('attl', sl),
                               reads=[('ATTF', h)], writes=[('rden', sl)])
                        tr.op('pool', 'tensor_tensor', out=attl[:, sl, :], in0=attl[:, sl, :], in1=ssacc[:, qs], op=ALU.mult,
                              reads=['ssacc'], writes=[('rden', sl)])
                        tr.op('pool', 'tensor_scalar', out=attT_st[:, h, :], in0=attl[:, sl, :], scalar1=gao[:, h:h + 1],
                              scalar2=None, op0=ALU.mult, reads=[('rden', sl), 'gao'], writes=['attT_st'])
                    tr.dma('pool', ATTT.rearrange("c (two e) t -> e (c two) t", two=2)[:, :, a0 + qb * 512:a0 + (qb + 1) * 512],
                           attT_st, key='attt_o', reads=['attT_st'], writes=['ATTT'])
        if stop_after not in ('ssd', 'p3v', 'p3a'):
            phase()
            WA = ph([128, 1, 8, 1024], BF16)
            w1p = ph([128, 3, 8, 512], BF16)
            w2p = ph([128, 3, 4, 1024], BF16)
            hTp = ph([128, 2, 4, 512], BF16)
            hbuf = ph([128, 2, 4, 1024], F32)
            mixT = ph([128, 8, 512], BF16)
            hnT = ph([128, 2, 8, 512], BF16)
            kmT = ph([128, 2, 8, 256], BF16)
            Vm = ph([128, 2, 2, 1024], BF16)
            sqx = ph([128, 2, 512], BF16)
            rsx = ph([128, 512], F32)
            qx = ph([128, 2, 512], BF16)
            PTx = ph([128, 2, 512], BF16)
            rdx = ph([128, 512], F32)
            oT = ph([128, 8, 512], BF16)
            rl = ph([128, 2, 512], F32)
            memT = rl.bitcast(BF16).rearrange("p a (b c) -> p (a b) c", b=4)
            wa_i = [0]

            def load_wa(src_ap):
                s = 0
                wa_i[0] += 1
                tr.dma('sp', WA[:, s], src_ap, key=('WA', s), writes=[('WA', s)])
                return s

            for sg in range(2):
                for j in range(2):
                    s = j % 2
                    tr.dma('sp', xt[:, s, :], mem_d[sg, j * 128:(j + 1) * 128, :], key=('xt', s), writes=[('xt', s)])
                    norm_T(xt[:, s, :], ('xt', s), 2, memT, 'memT', j, 0)
                sk = load_wa(WKVB[:, :, 0:1024])
                for hh in range(4):
                    for cc in range(2):
                        c = 2 * hh + cc
                        for k in range(8):
                            tr.op('pe', 'matmul', PB[cc][:, 0:256], lhsT=WA[:, sk, k, c * 128:(c + 1) * 128], rhs=memT[:, k, :],
                                  start=(k == 0), stop=(k == 7), reads=[('WA', sk), 'memT'], writes=[pk(cc)])
                        tr.op('act', 'activation', out=sqx[:, cc, 0:256], in_=PB[cc][:, 0:256], func=AF.Square,
                              reads=[pk(cc)], writes=[('sqx', cc)])
                    for cc in range(2):
                        tr.op('pe', 'matmul', PB[2][:, 0:256], lhsT=cb(C_ONES), rhs=sqx[:, cc, 0:256], start=(cc == 0),
                              stop=(cc == 1), reads=['CB', ('sqx', cc)], writes=[pk(2)])
                    tr.op('act', 'activation', out=rsx[:, 0:256], in_=PB[2][:, 0:256], func=AF.Ln, scale=1.0 / 256,
                          bias=epsT[:], reads=['eps'], writes=[pk(2), 'rsx'])
                    tr.op('act', 'activation', out=rsx[:, 0:256], in_=rsx[:, 0:256], func=AF.Exp, scale=-0.5, reads=['rsx'], writes=['rsx'])
                    for cc in range(2):
                        tr.op('dve', 'scalar_tensor_tensor', out=kmT[:, sg, 2 * hh + cc, :], in0=PB[cc][:, 0:256],
                              scalar=gx[:, 2 + cc:3 + cc], in1=rsx[:, 0:256], op0=ALU.mult, op1=ALU.mult,
                              reads=['gx', 'rsx'], writes=[pk(cc), 'kmT'])
                sv = load_wa(WKVB[:, :, 1024:2048])
                for mtl in range(2):
                    for half in range(2):
                        bk = half
                        for k in range(8):
                            tr.op('pe', 'matmul', PB[bk][:], lhsT=memT[:, k, mtl * 128:(mtl + 1) * 128],
                                  rhs=WA[:, sv, k, half * 512:(half + 1) * 512], start=(k == 0), stop=(k == 7),
                                  reads=[('WA', sv), 'memT'], writes=[pk(bk)])
                        tr.op('act', 'copy', out=Vm[:, sg, mtl, half * 512:(half + 1) * 512], in_=PB[bk][:],
                              writes=[pk(bk), 'Vm'])
            tr.barrier()

            def F_gen(mt):
                t0 = mt * 512
                sg = t0 // SEG
                par = mt % 2
                hb = hbuf[:, par]
                hn = hnT[:, par]
                hkey = lambda j: ('hbuf', par, j)
                nkey = ('hnT', par)
                tr.dma('sp', mixT[:, 0:4, :], SSDT[:, :, t0:t0 + 512].rearrange("c p t -> p c t"), key='mixT',
                       writes=['mixT'])
                tr.dma('sp', mixT[:, 4:8, :], ATTT[:, :, t0:t0 + 512].rearrange("c p t -> p c t"), key='mixT',
                       writes=['mixT'])
                s_ = load_wa(WOUTB)
                yield
                for j in range(4):
                    xs_ = j % 2
                    tr.dma('sp', xt[:, xs_, :], x_d[t0 + j * 128:t0 + (j + 1) * 128, :], key=('xt', xs_), writes=[('xt', xs_)])
                    for half in range(2):
                        bk = half
                        for kc in range(8):
                            tr.op('pe', 'matmul', PB[bk][:], lhsT=mixT[:, kc, j * 128:(j + 1) * 128],
                                  rhs=WA[:, s_, kc, half * 512:(half + 1) * 512], start=(kc == 0), stop=(kc == 7),
                                  reads=[('WA', s_), 'mixT'], writes=[pk(bk)])
                        tr.op('dve', 'tensor_tensor', out=hb[:, j, half * 512:(half + 1) * 512], in0=PB[bk][:],
                              in1=xt[:, xs_, half * 512:(half + 1) * 512], op=ALU.add, reads=[('xt', xs_)],
                              writes=[pk(bk), hkey(j)])
                for j in range(4):
                    norm_T(hb[:, j, :], hkey(j), 1, hn, nkey, j, j % 2)
                    yield
                s_ = load_wa(WQB)
                for hh in range(4):
                    for cc in range(2):
                        c = 2 * hh + cc
                        for k in range(8):
                            tr.op('pe', 'matmul', PB[cc][:], lhsT=WA[:, s_, k, c * 128:(c + 1) * 128], rhs=hn[:, k, :],
                                  start=(k == 0), stop=(k == 7), reads=[('WA', s_), nkey], writes=[pk(cc)])
                        tr.op('act', 'activation', out=sqx[:, cc, :], in_=PB[cc][:], func=AF.Square, reads=[pk(cc)],
                              writes=[('sqx', cc)])
                    yield
                    for cc in range(2):
                        tr.op('pe', 'matmul', PB[2][:], lhsT=cb(C_ONES), rhs=sqx[:, cc, :], start=(cc == 0), stop=(cc == 1),
                              reads=['CB', ('sqx', cc)], writes=[pk(2)])
                    yield
                    tr.op('act', 'activation', out=rsx, in_=PB[2][:], func=AF.Ln, scale=1.0 / 256, bias=epsT[:],
                          reads=['eps'], writes=[pk(2), 'rsx'])
                    tr.op('act', 'activation', out=rsx, in_=rsx, func=AF.Exp, scale=-0.5, reads=['rsx'], writes=['rsx'])
                    for cc in range(2):
                        tr.op('dve', 'scalar_tensor_tensor', out=qx[:, cc, :], in0=PB[cc][:], scalar=gx[:, cc:cc + 1],
                              in1=rsx, op0=ALU.mult, op1=ALU.mult, reads=['gx', 'rsx'], writes=[pk(cc), 'qx'])
                    yield
                    for mtl in range(2):
                        for cc in range(2):
                            tr.op('pe', 'matmul', PB[2][:], lhsT=kmT[:, sg, 2 * hh + cc, mtl * 128:(mtl + 1) * 128],
                                  rhs=qx[:, cc, :], start=(cc == 0), stop=(cc == 1), reads=['kmT', 'qx'], writes=[pk(2)])
                        tr.op('act', 'activation', out=PTx[:, mtl, :], in_=PB[2][:], func=AF.Exp, scale=1.0 / 16,
                              writes=[pk(2), 'PTx'])
                    yield
                    for mtl in range(2):
                        tr.op('pe', 'matmul', PB[2][:], lhsT=cb(C_ONES), rhs=PTx[:, mtl, :], start=(mtl == 0), stop=(mtl == 1),
                              reads=['CB', 'PTx'], writes=[pk(2)])
                    for cc in range(2):
                        c = 2 * hh + cc
                        for mtl in range(2):
                            tr.op('pe', 'matmul', PB[cc][:], lhsT=Vm[:, sg, mtl, c * 128:(c + 1) * 128], rhs=PTx[:, mtl, :],
                                  start=(mtl == 0), stop=(mtl == 1), reads=['Vm', 'PTx'], writes=[pk(cc)])
                    tr.op('act', 'activation', out=rdx, in_=PB[2][:], func=AF.Ln, writes=[pk(2), 'rdx'])
                    tr.op('act', 'activation', out=rdx, in_=rdx, func=AF.Exp, scale=-1.0, reads=['rdx'], writes=['rdx'])
                    for cc in range(2):
                        c = 2 * hh + cc
                        tr.op('dve', 'tensor_tensor', out=oT[:, c, :], in0=PB[cc][:], in1=rdx, op=ALU.mult, reads=['rdx'],
                              writes=[pk(cc), 'oT'])
                    yield
                s_ = load_wa(WOB)
                yield
                for j in range(4):
                    for half in range(2):
                        bk = half
                        for kc in range(8):
                            tr.op('pe', 'matmul', PB[bk][:], lhsT=oT[:, kc, j * 128:(j + 1) * 128],
                                  rhs=WA[:, s_, kc, half * 512:(half + 1) * 512], start=(kc == 0), stop=(kc == 7),
                                  reads=[('WA', s_), 'oT'], writes=[pk(bk)])
                        tr.op('dve', 'tensor_tensor', out=hb[:, j, half * 512:(half + 1) * 512], in0=PB[bk][:],
                              in1=hb[:, j, half * 512:(half + 1) * 512], op=ALU.add, writes=[pk(bk), hkey(j)])
                for j in range(4):
                    norm_T(hb[:, j, :], hkey(j), 3, hn, nkey, j, j % 2)
                    yield

            def MLP_gen(mt):
                t0 = mt * 512
                par = mt % 2
                hb = hbuf[:, par]
                hn = hnT[:, par]
                hkey = lambda j: ('hbuf', par, j)
                nkey = ('hnT', par)

                def mlp_load(pc):
                    ws_ = pc % 3
                    tr.dma('sp', w1p[:, ws_], W1B[:, :, pc * 512:(pc + 1) * 512], key=('w1p', ws_), writes=[('w1p', ws_)])
                    tr.dma('pool', w2p[:, ws_], W2B[:, pc * 4:(pc + 1) * 4, :], key=('w2p', ws_), writes=[('w2p', ws_)])

                def mlp_w1(pc):
                    ps_ = pc % 2
                    ws_ = pc % 3
                    if pc + 1 < 8:
                        mlp_load(pc + 1)
                    for f in range(4):
                        bk = 3 + (f % 2)
                        for k in range(8):
                            tr.op('pe', 'matmul', PB[bk][:], lhsT=w1p[:, ws_, k, f * 128:(f + 1) * 128], rhs=hn[:, k, :],
                                  start=(k == 0), stop=(k == 7), reads=[('w1p', ws_), nkey], writes=[pk(bk)])
                        tr.op('act', 'activation', out=rl[:, f % 2, :], in_=PB[bk][:], func=AF.Relu,
                              writes=[pk(bk), ('rl', f % 2)])
                        tr.op('dve', 'tensor_tensor', out=hTp[:, ps_, f, :], in0=rl[:, f % 2, :], in1=rl[:, f % 2, :],
                              op=ALU.mult, reads=[('rl', f % 2)], writes=[('hTp', ps_, f)])
                        yield

                def mlp_w2(pc):
                    ps_ = pc % 2
                    ws_ = pc % 3
                    for j in range(4):
                        for half in range(2):
                            bk = 5 + ((2 * j + half) % 3)
                            for f in range(4):
                                tr.op('pe', 'matmul', PB[bk][:], lhsT=hTp[:, ps_, f, j * 128:(j + 1) * 128],
                                      rhs=w2p[:, ws_, f, half * 512:(half + 1) * 512], start=(f == 0), stop=(f == 3),
                                      reads=[('hTp', ps_, f), ('w2p', ws_)], writes=[pk(bk)])
                            tr.op('dve', 'tensor_tensor', out=hb[:, j, half * 512:(half + 1) * 512], in0=PB[bk][:],
                                  in1=hb[:, j, half * 512:(half + 1) * 512], op=ALU.add, writes=[pk(bk), hkey(j)])
                            yield

                mlp_load(0)
                yield from mlp_w1(0)
                for pc in range(8):
                    if pc + 1 < 8:
                        yield from mlp_w1(pc + 1)
                    yield from mlp_w2(pc)
                tr.dma('pool', y_d[t0:t0 + 512, :].rearrange("(j p) c -> p j c", p=128), hb, key=('y_o', par),
                       reads=[hkey(j) for j in range(4)], writes=['y'])

            def interleave(gm, gf, ratio=4):
                m_alive, f_alive = True, True
                while m_alive or f_alive:
                    if f_alive:
                        try:
                            next(gf)
                        except StopIteration:
                            f_alive = False
                    for _ in range(ratio if f_alive else 1000000):
                        if not m_alive:
                            break
                        try:
                            next(gm)
                        except StopIteration:
                            m_alive = False

            for _ in F_gen(0):
                pass
            for mt in range(NMT):
                gens = [MLP_gen(mt)]
                if mt + 1 < NMT:
                    gens.append(F_gen(mt + 1))
                interleave(*gens) if len(gens) == 2 else [None for _ in gens[0]]
        for n in dbg_d:
            if n == 'ssdt':
                tr.dma('sp', dbg_d[n], SSDT, key='dbg', reads=['SSDT'])
            if n == 'xnt':
                tr.dma('sp', dbg_d[n], XNT, key='dbg', reads=['XNT'])
            if n == 'attt':
                tr.dma('sp', dbg_d[n], ATTT, key='dbg', reads=['ATTT'])
        print('op counts', tr.cnt, 'dma sems', len(tr.dsem))
        tr.emit('sp')
    return nc


def host_inputs(NT, xs_list, mem_list, fuse_list, params, pos_list):
    consts = make_consts()
    half = 8
    inv_freq = np.power(np.float32(500000.0), -np.arange(half, dtype=np.float32) / half).astype(np.float32)
    maps = []
    for c in range(len(xs_list)):
        f = float(fuse_list[c])
        lo = np.arange(128) < 64
        cols = []
        for (a, b) in [(1, 1), (f, f), (0, 0), (1, f), (f, 1), (1, 0), (0, 1), (f, 0), (0, f)]:
            cols.append(np.where(lo, a, b))
        vtab = np.stack(cols, 1).astype(np.float32)
        vtab = np.concatenate([vtab, (vtab - 1.0) * 30000.0], axis=1).astype(np.float32)
        ang = pos_list[c].astype(np.float32)[None, :] * inv_freq[:, None]
        cos = np.ones((128, NT), np.float32)
        sin = np.zeros((128, NT), np.float32)
        for hb in (0, 64):
            cos[hb:hb + 8] = np.cos(ang)
            cos[hb + 8:hb + 16] = np.cos(ang)
            sin[hb:hb + 8] = np.sin(ang)
            sin[hb + 8:hb + 16] = np.sin(ang)
        m = {"x": np.ascontiguousarray(xs_list[c], np.float32), "mem": np.ascontiguousarray(mem_list[c], np.float32),
             "vtab": vtab, "cs": np.stack([cos, sin], 0), "consts": consts}
        p = params
        m["w_in"] = p["w_in"][0]
        m["w_out"] = p["w_out"][0]
        m["xatt_wq"] = p["xatt_wq"][0]
        m["xatt_wkv"] = p["xatt_wkv"][0]
        m["xatt_wo"] = p["xatt_wo"][0]
        m["mlp_w1"] = p["mlp_w1"][0]
        m["mlp_w2"] = p["mlp_w2"][0]
        for k in ["mix_norm_g", "xatt_norm_g", "mem_norm_g", "mlp_norm_g", "ssd_norm_g", "conv_b"]:
            m[k] = p[k].reshape(1, -1)
        m["conv_w"] = p["conv_w"][0]
        m["ssd_A_log"] = p["ssd_A_log"].reshape(1, 16)
        m["ssd_dt_bias"] = p["ssd_dt_bias"].reshape(1, 16)
        m["ssd_D"] = p["ssd_D"].reshape(1, 8)
        m["att_q_norm_g"] = p["att_q_norm_g"].reshape(64, 1)
        m["att_k_norm_g"] = p["att_k_norm_g"].reshape(64, 1)
        m["att_out_norm_g"] = p["att_out_norm_g"].reshape(8, 64)
        m["xatt_q_norm_g"] = p["xatt_q_norm_g"].reshape(2, 128)
        m["xatt_k_norm_g"] = p["xatt_k_norm_g"].reshape(2, 128)
        maps.append({k: np.ascontiguousarray(np.asarray(v, np.float32)) for k, v in m.items()})
    return maps


_NC_CACHE = {}


def kernel(x_prompt, x_sample, mem_prompt, mem_sample, **params):
    NT = 8192
    x_prompt = np.asarray(x_prompt, np.float32)
    x_sample = np.asarray(x_sample, np.float32)
    mem_prompt = np.asarray(mem_prompt, np.float32)
    mem_sample = np.asarray(mem_sample, np.float32)
    p = {k: np.asarray(v, np.float32) for k, v in params.items()}
    xs_list, mem_list, fuse, pos = [], [], [], []
    for b in range(2):
        xs_list.append(x_prompt[b])
        mem_list.append(np.stack([mem_prompt[b], mem_prompt[b]]))
        fuse.append(1)
        pos.append(np.arange(NT))
    for c in range(4):
        xs_list.append(np.concatenate([x_sample[2 * c], x_sample[2 * c + 1]], 0))
        mem_list.append(np.stack([mem_sample[2 * c], mem_sample[2 * c + 1]]))
        fuse.append(0)
        pos.append(np.concatenate([np.arange(NT // 2)] * 2))
    for c in range(2):
        xs_list.append(np.zeros((NT, D), np.float32))
        mem_list.append(np.zeros((2, 256, D), np.float32))
        fuse.append(0)
        pos.append(np.concatenate([np.arange(NT // 2)] * 2))
    maps = host_inputs(NT, xs_list, mem_list, fuse, p, pos)
    if NT not in _NC_CACHE:
        _NC_CACHE[NT] = build(NT)
    res = run_bass_kernel_spmd(_NC_CACHE[NT], maps, core_ids=list(range(8)))
    ys = [np.asarray(r["y"], np.float32) for r in res.results]
    y_prompt = np.stack([ys[0], ys[1]], 0)
    y_sample = np.stack([ys[2 + c // 2][(c % 2) * (NT // 2):(c % 2 + 1) * (NT // 2)] for c in range(8)], 0)
    return (y_prompt, y_sample)
```

```python
import math
import os
ATT_LEVEL = int(os.environ.get('ATT_LEVEL', '9'))
import numpy as np
import concourse.bass as bass
import concourse.mybir as mybir
from concourse.bass_utils import run_bass_kernel_spmd
from contextlib import ExitStack

F32 = mybir.dt.float32
BF16 = mybir.dt.bfloat16
ALU = mybir.AluOpType
AF = mybir.ActivationFunctionType
EPS = 1e-6
ROT = 20000
D = 1024
INC = 3088
PADV = 1024


class TR:
    def __init__(self, nc, es):
        self.nc = nc
        self.es = es
        self.names = ['pe', 'dve', 'act', 'pool', 'sp']
        self.cnt = {e: 0 for e in self.names}
        self.sems = {e: [] for e in self.names}
        self.dsem = {}
        self.dcnt = {}
        self.lastw = {}
        self.readers = {}
        self.waited = {e: {} for e in self.names}
        self.prog = {e: [] for e in self.names}

    def _sem(self, e, idx):
        j = (idx - 1) // ROT
        while len(self.sems[e]) <= j:
            self.sems[e].append(self.es.enter_context(self.nc.semaphore(f"s_{e}_{len(self.sems[e])}")))
        return self.sems[e][j], (idx - 1) % ROT + 1

    def _deps(self, reads, writes):
        deps = []
        for k in reads:
            if k in self.lastw:
                deps.append(self.lastw[k])
        for k in writes:
            if k in self.lastw:
                deps.append(self.lastw[k])
            deps.extend(self.readers.get(k, []))
        return deps

    def _wait(self, e, deps):
        out = []
        for d in deps:
            if d[0] == 'dma':
                _, key, val = d
                if self.waited[e].get(('dma', key), 0) >= val:
                    continue
                out.append((self.dsem[key], val))
                self.waited[e][('dma', key)] = val
            else:
                f, idx = d
                if f == 'pe' and e == 'pe':
                    continue
                if self.waited[e].get(f, 0) >= idx:
                    continue
                out.append(self._sem(f, idx))
                self.waited[e][f] = idx
        return out

    def _commit(self, ref, reads, writes):
        for k in reads:
            self.readers.setdefault(k, []).append(ref)
        for k in writes:
            self.lastw[k] = ref
            self.readers[k] = []

    def op(self, e, meth, *args, reads=(), writes=(), **kw):
        fn = (lambda h, meth=meth, args=args, kw=kw: getattr(h, meth)(*args, **kw))
        w = self._wait(e, self._deps(reads, writes))
        self.cnt[e] += 1
        s, v = self._sem(e, self.cnt[e])
        self.prog[e].append((w, fn, (s, 1)))
        self._commit((e, self.cnt[e]), reads, writes)

    def dma(self, q, out, in_, key, reads=(), writes=(), **kw):
        w = self._wait(q, self._deps(reads, writes))
        if key not in self.dsem:
            self.dsem[key] = self.es.enter_context(self.nc.semaphore(f"d_{len(self.dsem)}"))
            self.dcnt[key] = 0
        self.dcnt[key] += 16
        self.prog[q].append((w, (lambda h, out=out, in_=in_, kw=kw: h.dma_start(out=out, in_=in_, **kw)),
                             (self.dsem[key], 16)))
        self._commit(('dma', key, self.dcnt[key]), reads, writes)

    def batch_end(self, key, res_keys):
        for k in res_keys:
            self.lastw[k] = ('dma', key, self.dcnt[key])

    def barrier(self):
        fin = [(self.dsem[k], v) for k, v in self.dcnt.items()]
        for e in self.names:
            if self.cnt[e] > 0:
                fin.append(self._sem(e, self.cnt[e]))
        for e in self.names:
            w = []
            for (sm, v) in fin:
                w.append((sm, v))
            self.cnt[e] += 1
            s_, v_ = self._sem(e, self.cnt[e])
            self.prog[e].append((w, (lambda h: h.nop()), (s_, 1)))
        self.lastw = {}
        self.readers = {}

    def emit(self, final_q='sp'):
        fin = [(self.dsem[k], v) for k, v in self.dcnt.items()]
        for e in self.names:
            if self.cnt[e] > 0:
                fin.append(self._sem(e, self.cnt[e]))
        blk = self.es.enter_context(self.nc.Block())
        hmap = {'pe': blk.tensor, 'dve': blk.vector, 'act': blk.scalar, 'pool': blk.gpsimd, 'sp': blk.sync}
        for e in self.names:
            prog = self.prog[e]
            extra = fin if e == final_q else []

            def body(h, prog=prog, extra=extra):
                for (w, fn, inc) in prog:
                    for (s, v) in w:
                        h.wait_ge(s, v)
                    fn(h).then_inc(inc[0], inc[1])
                for (s, v) in extra:
                    h.wait_ge(s, v)
            if prog or extra:
                hmap[e](body)


def make_consts():
    k = np.arange(128)[:, None]
    i = np.arange(128)[None, :]
    ident = (k == i)
    U = (k <= i)
    L = (k >= i)
    Us = (k < i)
    Ls = (k > i)
    ones = np.ones((128, 128), bool)
    blk = ((k // 64) == (i // 64))
    P = np.zeros((128, 128), np.float32)
    for hb in (0, 64):
        for e in range(8):
            P[hb + e + 8, hb + e] = -1.0
            P[hb + e, hb + e + 8] = 1.0
    sel = np.zeros((128, 128), np.float32)
    sel[64, :] = 1.0
    negl = np.where(L, 0.0, -30000.0)
    negu = np.where(U, 0.0, -30000.0)
    mats = [ident, U, L, Us, Ls, ones, blk, P, sel, negl, negu]
    return np.concatenate([np.asarray(m, np.float32) for m in mats], axis=1)


C_ID, C_U, C_L, C_US, C_LS, C_ONES, C_BLK, C_P, C_SEL, C_NEGL, C_NEGU = range(11)
NCM = 11


def build(NT, dbg=(), stop_after=None):
    SEG = NT // 2
    NMT = NT // 512
    NCH = NT // 128
    AM = 2048
    NAM = NT // AM
    nc = bass.Bass("TRN2", target_bir_lowering=False)
    din = lambda n, s, dt=F32: nc.dram_tensor(n, s, dt, kind="ExternalInput").ap()
    dscr = lambda n, s, dt: nc.dram_tensor(n, s, dt, kind="Internal").ap()
    x_d = din("x", [NT, D])
    mem_d = din("mem", [2, 256, D])
    vt_d = din("vtab", [128, 18])
    cs_d = din("cs", [2, 128, NT])
    cst_d = din("consts", [128, NCM * 128])
    w_in_d = din("w_in", [D, INC])
    w_out_d = din("w_out", [D, D])
    wq_d = din("xatt_wq", [D, D])
    wkv_d = din("xatt_wkv", [D, 2 * D])
    wo_d = din("xatt_wo", [D, D])
    w1_d = din("mlp_w1", [D, 4 * D])
    w2_d = din("mlp_w2", [4 * D, D])
    g_mix_d = din("mix_norm_g", [1, D])
    g_xatt_d = din("xatt_norm_g", [1, D])
    g_mem_d = din("mem_norm_g", [1, D])
    g_mlp_d = din("mlp_norm_g", [1, D])
    g_ssd_d = din("ssd_norm_g", [1, 512])
    convw_d = din("conv_w", [5, D])
    convb_d = din("conv_b", [1, D])
    alog_d = din("ssd_A_log", [1, 16])
    dtb_d = din("ssd_dt_bias", [1, 16])
    dsk_d = din("ssd_D", [1, 8])
    gq_d = din("att_q_norm_g", [64, 1])
    gk_d = din("att_k_norm_g", [64, 1])
    gao_d = din("att_out_norm_g", [8, 64])
    gxq_d = din("xatt_q_norm_g", [2, 128])
    gxk_d = din("xatt_k_norm_g", [2, 128])
    y_d = nc.dram_tensor("y", [NT, D], F32, kind="ExternalOutput").ap()
    dbg_d = {n: nc.dram_tensor("dbg_" + n, s, dt, kind="ExternalOutput").ap() for (n, s, dt) in dbg}

    XNT = dscr("XNT", [8, 128, NT + 4], BF16)
    SSDT = dscr("SSDT", [4, 128, NT], BF16)
    ATTT = dscr("ATTT", [4, 128, NT], BF16)
    VS = dscr("VS", [NT + 2 * PADV, 520], BF16)
    WINB = dscr("WINB", [128, 8, INC], BF16)
    WOUTB = dscr("WOUTB", [128, 8, D], BF16)
    WQB = dscr("WQB", [128, 8, D], BF16)
    WKVB = dscr("WKVB", [128, 8, 2 * D], BF16)
    WOB = dscr("WOB", [128, 8, D], BF16)
    W1B = dscr("W1B", [128, 8, 4 * D], BF16)
    W2B = dscr("W2B", [128, 32, D], BF16)

    BSTD = dscr("BSTD", [NCH, 128, 512], BF16)
    ATTF = dscr("ATTF", [8, 64, NT], F32)
    XST_D = dscr("XST_D", [NMT, 128, 4, 512], F32)
    BCT_D = dscr("BCT_D", [NMT, 128, 4, 512], BF16)
    SZ_D = dscr("SZ_D", [NMT, 128, 4, 512], F32)
    with ExitStack() as es:
        tr = TR(nc, es)
        _n = [0]

        def sb(shape, dt, name=None):
            _n[0] += 1
            return es.enter_context(nc.sbuf_tensor("s_" + (name or f"t{_n[0]}"), shape, dt))

        ARW = 43008
        ARENA = sb([128, ARW], F32, "arena")
        apos = [0]

        def phase():
            if os.environ.get('PHASE_DBG'):
                print('phase boundary', dict(tr.cnt))
            tr.barrier()
            apos[0] = 0

        def ph(shape, dt, parts=128):
            n = 1
            for v in shape[1:]:
                n *= v
            w = n if dt == F32 else (n + 1) // 2
            o = apos[0]
            apos[0] += w
            assert apos[0] <= ARW, ("arena overflow", apos[0])
            v = ARENA[0:shape[0], o:o + w]
            if dt != F32:
                v = v.bitcast(dt)
            if len(shape) == 3:
                v = v.rearrange("p (a b) -> p a b", a=shape[1])
            elif len(shape) == 4:
                v = v.rearrange("p (a b c) -> p a b c", a=shape[1], b=shape[2])
            return v

        PB = [es.enter_context(nc.psum_tensor(f"pb{i}", [128, 512], F32)) for i in range(8)]
        pk = lambda i: ('ps', i)
        pbf = lambda i: PB[i][:].bitcast(BF16)

        CF = sb([128, NCM, 128], F32, "CF")
        CB = sb([128, NCM, 128], BF16, "CB")
        vtab = sb([128, 18], F32, "vtab")
        epsT = sb([128, 1], F32, "epsT")
        tr.dma('sp', CF[:].rearrange("p a b -> p (a b)"), cst_d, key=('c0', 1), writes=['CF'])
        tr.dma('sp', vtab[:], vt_d, key=('c0', 2), writes=['vtab'])
        tr.op('dve', 'tensor_copy', out=CB[:], in_=CF[:], reads=['CF'], writes=['CB'])
        tr.op('dve', 'memset', epsT[:], EPS, writes=['eps'])
        cf = lambda i: CF[:, i, :]
        cb = lambda i: CB[:, i, :]
        FUSE = vtab[:, 1:2]

        gB = sb([128, 4, D], F32, "gB")
        for i, gd in enumerate([g_mix_d, g_xatt_d, g_mem_d, g_mlp_d]):
            tr.dma('sp', gB[:, i, :], gd.partition_broadcast(128), key=('c0', 3), writes=['gB'])
        gssd = sb([128, 512], F32, "gssd")
        tr.dma('sp', gssd[:], g_ssd_d.partition_broadcast(128), key=('c0', 4), writes=['gssd'])
        cw = sb([128, 8, 5], F32, "cw")
        cbias = sb([128, 8], F32, "cbias")
        for k in range(5):
            tr.dma('sp', cw[:, :, k:k + 1], convw_d[k:k + 1, :].rearrange("o (c p) -> p c o", p=128), key=('c0', 5),
                   writes=['cw'], allow_slow_non_contiguous=True)
        tr.dma('sp', cbias[:].unsqueeze(2), convb_d.rearrange("o (c p) -> p c o", p=128), key=('c0', 6), writes=['cw'],
               allow_slow_non_contiguous=True)
        Abc = sb([128, 16], F32, "Abc")
        dtb = sb([128, 16], F32, "dtb")
        Dbc = sb([128, 8], F32, "Dbc")
        tr.dma('sp', Abc[:], alog_d.partition_broadcast(128), key=('c0', 7), writes=['Abc'])
        tr.dma('sp', dtb[:], dtb_d.partition_broadcast(128), key=('c0', 8), writes=['dtb'])
        tr.dma('sp', Dbc[:], dsk_d.partition_broadcast(128), key=('c0', 9), writes=['Dbc'])
        tr.op('act', 'activation', out=Abc[:], in_=Abc[:], func=AF.Exp, reads=['Abc'], writes=['Abc'])
        tr.op('dve', 'tensor_scalar', out=Abc[:], in0=Abc[:], scalar1=-1.0, scalar2=None, op0=ALU.mult,
              reads=['Abc'], writes=['Abc'])
        gqk = sb([128, 2], F32, "gqk")
        for hb in (0, 64):
            tr.dma('sp', gqk[hb:hb + 64, 0:1], gq_d, key=('c0', 10), writes=['gqk'])
            tr.dma('sp', gqk[hb:hb + 64, 1:2], gk_d, key=('c0', 11), writes=['gqk'])
        gao = sb([64, 8], F32, "gao")
        tr.dma('sp', gao[:], gao_d.rearrange("h e -> e h"), key=('c0', 12), writes=['gao'], allow_slow_non_contiguous=True)
        gx = sb([128, 4], F32, "gx")
        tr.dma('sp', gx[:, 0:2], gxq_d.rearrange("c p -> p c"), key=('c0', 13), writes=['gx'], allow_slow_non_contiguous=True)
        tr.dma('sp', gx[:, 2:4], gxk_d.rearrange("c p -> p c"), key=('c0', 14), writes=['gx'], allow_slow_non_contiguous=True)
        ssq2 = sb([128, 2], F32, "ssq2")
        rstd2 = sb([128, 2], F32, "rstd2")
        xt = sb([128, 2, D], F32, "xt")
        xn2 = sb([128, 2, D], BF16, "xn2")

        phase()
        wst = ph([128, 2, 2048], F32)
        wsb16 = ph([128, 2, 2048], BF16)
        wi = [0]

        pw_steps = []

        def cast_w(nm, src, dst, nk, ncol):
            for k in range(nk):
                for c0 in range(0, ncol, 2048):
                    cn = min(2048, ncol - c0)

                    def step(k=k, c0=c0, cn=cn):
                        s = wi[0] % 2
                        wi[0] += 1
                        tr.dma('sp', wst[:, s, 0:cn], src[k * 128:(k + 1) * 128, c0:c0 + cn], key=('wst', s),
                               writes=[('wst', s)])
                        if wi[0] % 2 == 0:
                            tr.op('dve', 'tensor_copy', out=wsb16[:, s, 0:cn], in_=wst[:, s, 0:cn], reads=[('wst', s)],
                                  writes=[('wsb', s)])
                        else:
                            tr.op('act', 'copy', out=wsb16[:, s, 0:cn], in_=wst[:, s, 0:cn], reads=[('wst', s)],
                                  writes=[('wsb', s)])
                        tr.dma('pool', dst[:, k, c0:c0 + cn], wsb16[:, s, 0:cn], key=('wsbo', s), reads=[('wsb', s)],
                               writes=[nm])
                    pw_steps.append(step)

        for nm, src, dst, nk, ncol in [("WINB", w_in_d, WINB, 8, INC), ("WOUTB", w_out_d, WOUTB, 8, D),
                                       ("WQB", wq_d, WQB, 8, D), ("WKVB", wkv_d, WKVB, 8, 2 * D),
                                       ("WOB", wo_d, WOB, 8, D), ("W1B", w1_d, W1B, 8, 4 * D),
                                       ("W2B", w2_d, W2B, 32, D)]:
            cast_w(nm, src, dst, nk, ncol)

        nslot = [0]

        def rms_rstd(src_ap, n, key_src, sl_, width):
            tr.op('act', 'activation', out=xn2[:, sl_, 0:width], in_=src_ap, func=AF.Square, accum_out=ssq2[:, sl_:sl_ + 1],
                  reads=[key_src], writes=[('xn', sl_), ('ssq', sl_)])
            tr.op('act', 'activation', out=rstd2[:, sl_:sl_ + 1], in_=ssq2[:, sl_:sl_ + 1], func=AF.Ln, scale=1.0 / n,
                  bias=epsT[:], reads=[('ssq', sl_), 'eps'], writes=[('rstd', sl_)])
            tr.op('act', 'activation', out=rstd2[:, sl_:sl_ + 1], in_=rstd2[:, sl_:sl_ + 1], func=AF.Exp, scale=-0.5,
                  reads=[('rstd', sl_)], writes=[('rstd', sl_)])

        def norm_T(src, skey, gi, dst, dkey, j, tbank):
            sl_ = nslot[0] % 2
            nslot[0] += 1
            rms_rstd(src, D, skey, sl_, D)
            tr.op('dve', 'scalar_tensor_tensor', out=xn2[:, sl_, :], in0=src, scalar=rstd2[:, sl_:sl_ + 1], in1=gB[:, gi, :],
                  op0=ALU.mult, op1=ALU.mult, reads=[skey, ('rstd', sl_), 'gB'], writes=[('xn', sl_)])
            pt = pbf(tbank)
            for k in range(8):
                tr.op('pe', 'transpose', out=pt[:, k * 128:(k + 1) * 128], in_=xn2[:, sl_, k * 128:(k + 1) * 128],
                      identity=cb(C_ID), reads=[('xn', sl_), 'CB'], writes=[pk(tbank)])
            tr.op('act', 'copy', out=dst[:, :, j * 128:(j + 1) * 128],
                  in_=pt.rearrange("p (k t) -> p k t", k=8), writes=[pk(tbank), dkey])

        xnT_st = ph([128, 8, 512], BF16)
        zt = ph([128, 8, 2], BF16)
        tr.op('dve', 'memset', zt, 0.0, writes=['zt'])
        tr.dma('sp', XNT[:, :, 0:2].rearrange("k p t -> p k t"), zt, key='xnt_o', reads=['zt'], writes=['XNT'])
        tr.dma('sp', XNT[:, :, NT + 2:NT + 4].rearrange("k p t -> p k t"), zt, key='xnt_o', reads=['zt'],
               writes=['XNT'])
        for mt in range(NMT):
            for j in range(4):
                t0 = mt * 512 + j * 128
                s = (mt * 4 + j) % 2
                tr.dma('sp', xt[:, s, :], x_d[t0:t0 + 128, :], key=('xt', s), writes=[('xt', s)])
                norm_T(xt[:, s, :], ('xt', s), 0, xnT_st, 'xnT_st', j, j % 2)
                for _ in range(2):
                    if pw_steps:
                        pw_steps.pop(0)()
            tr.dma('sp', XNT[:, :, 2 + mt * 512:2 + mt * 512 + 512].rearrange("k p t -> p k t"), xnT_st,
                   key='xnt_o', reads=['xnT_st'], writes=['XNT'])

        while pw_steps:
            pw_steps.pop(0)()
        phase()
        w_ssd = ph([128, 8, 1552], BF16)
        tr.dma('sp', w_ssd, WINB[:, :, 0:1552], key='w_ssd', writes=['w_ssd'])
        xnTm = ph([128, 2, 8, 516], BF16)
        uT = ph([128, 2, 516], F32)
        cacc = ph([128, 2, 512], F32)
        xsT = ph([128, 2, 4, 512], F32)
        BCT = ph([128, 2, 4, 512], BF16)
        xs = ph([128, 2, 512], F32)
        Btm = ph([128, 2, 256], BF16)
        dtr = ph([128, 2, 16], F32)
        sp1 = ph([128, 2, 16], F32)
        dt = ph([128, 2, 16], F32)
        dtA = ph([128, 2, 16], F32)
        TT = ph([128, 2, 48], F32)
        EE = ph([128, 2, 48], F32)
        dtd = ph([128, 2, 16], F32)
        xdt = ph([128, 2, 2, 512], BF16)
        xdtd = ph([128, 2, 512], BF16)
        R1 = ph([128, 2, 2, 8, 128], F32) if False else ph([128, 4, 8, 128], F32)
        lm = ph([128, 4, 8, 128], F32)
        cbm = ph([128, 4, 2, 128], F32)
        WT = ph([128, 4, 8, 128], BF16)
        sz = ph([128, 2, 512], F32)
        yb = ph([128, 2, 512], F32)
        t2 = ph([128, 2, 512], F32)
        yn = ph([128, 2, 512], BF16)
        ssdT_st = ph([128, 2, 4, 512], BF16)
        stF = ph([128, 512], F32)
        stB = ph([128, 512], F32)
        prevF = ph([128, 2, 512], BF16)
        bstl = ph([128, 2, 512], BF16)
        szst = ph([128, 2, 512], F32)
        Dfull = ph([128, 8, 64], F32)
        tr.op('dve', 'tensor_copy', out=Dfull, in_=Dbc[:].unsqueeze(2).to_broadcast([128, 8, 64]),
              reads=['Dbc'], writes=['Dfull'])
        tr.op('dve', 'memset', stF, 0.0, writes=['stF'])
        tr.op('dve', 'memset', stB, 0.0, writes=['stB'])
        tr.op('dve', 'memset', prevF, 0.0, writes=[('prevF', 0), ('prevF', 1)])
        b8 = lambda ap16: ap16.unsqueeze(2).to_broadcast([128, 8, 64])
        v3 = lambda t: t.rearrange("p (h e) -> p h e", h=8)

        def run_ssd(mode):
            fwd = (mode == 'fwd')
            cached = (mode != 'prep')
            chunks = list(range(NCH)) if mode != 'bwd' else list(range(NCH - 1, -1, -1))
            macros = list(range(NMT)) if mode != 'bwd' else list(range(NMT - 1, -1, -1))
            n = len(chunks)
            di = 0 if fwd else 1
            hs = slice(di * 8, di * 8 + 8)
            st = stF if fwd else stB
            skey = 'stF' if fwd else 'stB'

            def prep_load(mi):
                mt = macros[mi]
                ms = mi % 2
                tr.dma('sp', xnTm[:, ms], XNT[:, :, mt * 512:mt * 512 + 516].rearrange("k p t -> p k t"),
                       key=('xnTm', ms), writes=[('xnTm', ms)])
                if cached:
                    tr.dma('sp', xsT[:, ms], XST_D[mt], key=('xsT_l', ms), reads=[('XSCa', mt)], writes=[('xsT', ms)])

            def prep_load_b(mi):
                mt = macros[mi]
                ms = mi % 2
                tr.dma('sp', BCT[:, ms], BCT_D[mt], key=('BCT_l', ms), reads=[('XSC', mt)], writes=[('BCT', ms)])

            def prep_A(mi, c):
                if cached:
                    return
                mt = macros[mi]
                ms = mi % 2
                xm = xnTm[:, ms]
                bk = 1 + (c % 2)
                col = 512 + c * 128
                for k in range(8):
                    tr.op('pe', 'matmul', PB[bk][:], lhsT=w_ssd[:, k, col:col + 128], rhs=xm[:, k, 2:514],
                          start=(k == 0), stop=(k == 7), reads=['w_ssd', ('xnTm', ms)], writes=[pk(bk)])
                tr.op('act', 'copy', out=uT[:, c % 2, 2:514], in_=PB[bk][:], writes=[pk(bk), ('uT', c % 2)])
                for hh, (a, b) in enumerate([(0, 2), (514, 516)]):
                    for k in range(8):
                        tr.op('pe', 'matmul', PB[3][:, 480 + 2 * hh:482 + 2 * hh], lhsT=w_ssd[:, k, col:col + 128],
                              rhs=xm[:, k, a:b], start=(k == 0), stop=(k == 7), skip_group_check=True,
                              reads=['w_ssd', ('xnTm', ms)], writes=[pk(3)])
                for hh, (a, b) in enumerate([(0, 2), (514, 516)]):
                    atb = (hh == 0 and mt * 512 == SEG) or (hh == 1 and mt * 512 + 512 == SEG)
                    if atb:
                        tr.op('act', 'activation', out=uT[:, c % 2, a:b], in_=PB[3][:, 480 + 2 * hh:482 + 2 * hh],
                              func=AF.Copy, scale=FUSE, reads=['vtab'], writes=[pk(3), ('uT', c % 2)])
                    else:
                        tr.op('act', 'copy', out=uT[:, c % 2, a:b], in_=PB[3][:, 480 + 2 * hh:482 + 2 * hh],
                              writes=[pk(3), ('uT', c % 2)])
                ca = cacc[:, c % 2]
                tr.op('act', 'activation', out=ca, in_=uT[:, c % 2, 0:512], func=AF.Identity, scale=cw[:, c, 0:1],
                      bias=cbias[:, c:c + 1], reads=[('uT', c % 2), 'cw'], writes=[('cacc', c % 2)])

            def prep_B(mi, c):
                mt = macros[mi]
                ms = mi % 2
                xm = xnTm[:, ms]
                ca = cacc[:, c % 2]
                for k in range(1, 5):
                    eng = 'dve'
                    tr.op(eng, 'scalar_tensor_tensor', out=ca, in0=uT[:, c % 2, k:k + 512], scalar=cw[:, c, k:k + 1],
                          in1=ca, op0=ALU.mult, op1=ALU.add, reads=[('uT', c % 2), 'cw'], writes=[('cacc', c % 2)])
                if c < 4:
                    tr.op('act', 'activation', out=xsT[:, ms, c, :], in_=ca, func=AF.Silu, reads=[('cacc', c % 2)],
                          writes=[('xsT', ms)])
                else:
                    tr.op('act', 'activation', out=BCT[:, ms, c - 4, :], in_=ca, func=AF.Silu, reads=[('cacc', c % 2)],
                          writes=[('BCT', ms)])
                if c == 3:
                    tr.dma('pool', XST_D[mt], xsT[:, ms], key=('xst_o', ms), reads=[('xsT', ms)], writes=[('XSCa', mt)])
                if c == 7:
                    tr.dma('pool', BCT_D[mt], BCT[:, ms], key=('bct_o', ms), reads=[('BCT', ms)], writes=[('XSC', mt)])
                    for j in range(4):
                        zs = j % 2
                        for k in range(8):
                            tr.op('pe', 'matmul', PB[7][:], lhsT=xm[:, k, 2 + j * 128:2 + (j + 1) * 128], rhs=w_ssd[:, k, 0:512],
                                  start=(k == 0), stop=(k == 7), reads=['w_ssd', ('xnTm', ms)], writes=[pk(7)])
                        tr.op('act', 'activation', out=szst[:, zs], in_=PB[7][:], func=AF.Silu, writes=[pk(7), ('szst', zs)])
                        tr.dma('pool', SZ_D[mt, :, j, :], szst[:, zs], key=('sz_o', zs), reads=[('szst', zs)],
                               writes=[('SZ', mt, j)])


            def prep_c(mi, c):
                if cached:
                    return
                prep_A(mi, c)
                prep_B(mi, c)

            def S1(p):
                ch = chunks[p]
                mt, j = ch // 4, ch % 4
                ms = (p // 4) % 2
                s = p % 2
                xm = xnTm[:, ms]
                tk = slice(j * 128, (j + 1) * 128)
                for c in range(4):
                    tr.op('pe', 'transpose', out=PB[4][:, c * 128:(c + 1) * 128], in_=xsT[:, ms, c, tk], identity=cf(C_ID),
                          reads=[('xsT', ms), 'CF'], writes=[pk(4)])
                tr.op('act', 'copy', out=xs[:, s], in_=PB[4][:], writes=[pk(4), ('xs', s)])
                pt = pbf(0)
                for g in range(2):
                    tr.op('pe', 'transpose', out=pt[:, g * 128:(g + 1) * 128], in_=BCT[:, ms, g, tk], identity=cb(C_ID),
                          reads=[('BCT', ms), 'CB'], writes=[pk(0)])
                tr.op('act', 'copy', out=Btm[:, s], in_=pt[:, 0:256], writes=[pk(0), ('Btm', s)])
                for k in range(8):
                    tr.op('pe', 'matmul', PB[3][:, 0:16], lhsT=xm[:, k, 2 + j * 128:2 + (j + 1) * 128],
                          rhs=w_ssd[:, k, 1536:1552], start=(k == 0), stop=(k == 7), skip_group_check=True,
                          reads=['w_ssd', ('xnTm', ms)], writes=[pk(3)])
                tr.op('dve', 'tensor_tensor', out=dtr[:, s], in0=PB[3][:, 0:16], in1=dtb[:], op=ALU.add,
                      reads=['dtb'], writes=[pk(3), ('dtr', s)])
                tr.op('act', 'activation', out=sp1[:, s], in_=dtr[:, s], func=AF.Abs, reads=[('dtr', s)],
                      writes=[('sp1', s)])
                tr.op('act', 'activation', out=sp1[:, s], in_=sp1[:, s], func=AF.Exp, scale=-1.0, reads=[('sp1', s)],
                      writes=[('sp1', s)])
                tr.op('act', 'activation', out=sp1[:, s], in_=sp1[:, s], func=AF.Ln, bias=1.0, reads=[('sp1', s)],
                      writes=[('sp1', s)])
                tr.op('dve', 'scalar_tensor_tensor', out=dt[:, s], in0=dtr[:, s], scalar=0.0, in1=sp1[:, s], op0=ALU.max,
                      op1=ALU.add, reads=[('dtr', s), ('sp1', s)], writes=[('dt', s)])
                tr.op('dve', 'tensor_tensor', out=dtA[:, s], in0=dt[:, s], in1=Abc[:], op=ALU.mult,
                      reads=[('dt', s), 'Abc'], writes=[('dtA', s)])
                if fwd:
                    for dd, tri in enumerate([C_U, C_L]):
                        tr.op('pool', 'tensor_tensor', out=R1[:, 2 * s + dd],
                              in0=cf(tri).unsqueeze(1).to_broadcast([128, 8, 128]),
                              in1=dtA[:, s, dd * 8:dd * 8 + 8].unsqueeze(2).to_broadcast([128, 8, 128]), op=ALU.mult,
                              reads=['CF', ('dtA', s)], writes=[('R1', s, dd)])
                    tr.dma('sp', sz[:, s], SZ_D[mt, :, j, :], key=('sz_l', s), reads=[('SZ', mt, j)], writes=[('sz', s)])

            def S2(p):
                ch = chunks[p]
                mt, j = ch // 4, ch % 4
                ms = (p // 4) % 2
                s = p % 2
                tk = slice(j * 128, (j + 1) * 128)
                tr.op('pe', 'matmul', PB[3][:, 16:24], lhsT=cf(C_U), rhs=dtA[:, s, 0:8], start=True, stop=True,
                      skip_group_check=True, reads=['CF', ('dtA', s)], writes=[pk(3)])
                tr.op('pe', 'matmul', PB[3][:, 24:32], lhsT=cf(C_L), rhs=dtA[:, s, 8:16], start=False, stop=True,
                      skip_group_check=True, reads=['CF', ('dtA', s)], writes=[pk(3)])
                tr.op('pe', 'matmul', PB[3][:, 32:48], lhsT=cf(C_ONES), rhs=dtA[:, s, 0:16], start=False, stop=True,
                      skip_group_check=True, reads=['CF', ('dtA', s)], writes=[pk(3)])
                tr.op('act', 'copy', out=TT[:, s, 0:32], in_=PB[3][:, 16:48], writes=[pk(3), ('TT', s)])
                tr.op('dve', 'tensor_tensor', out=TT[:, s, 32:48], in0=TT[:, s, 16:32], in1=TT[:, s, 0:16], op=ALU.subtract,
                      reads=[('TT', s)], writes=[('TT', s)])
                tr.op('act', 'activation', out=EE[:, s], in_=TT[:, s], func=AF.Exp, reads=[('TT', s)], writes=[('EE', s)])
                tr.op('dve', 'tensor_tensor', out=dtd[:, s], in0=dt[:, s], in1=EE[:, s, 32:48], op=ALU.mult,
                      reads=[('dt', s), ('EE', s)], writes=[('dtd', s)])
                tr.op('dve', 'tensor_tensor', out=v3(xdtd[:, s]), in0=v3(xs[:, s]), in1=b8(dtd[:, s, hs]), op=ALU.mult,
                      reads=[('xs', s), ('dtd', s)], writes=[('xdtd', s)])
                if fwd:
                    tr.dma('sp', bstl[:, s, :], BSTD[ch], key=('bstl', s), reads=[('BSTD', ch)], writes=[('bstl', s)])
                    for dd in range(2):
                        tr.op('dve', 'tensor_tensor', out=v3(xdt[:, s, dd]), in0=v3(xs[:, s]),
                              in1=b8(dt[:, s, dd * 8:dd * 8 + 8]), op=ALU.mult, reads=[('xs', s), ('dt', s)],
                              writes=[('xdt', s)])
                    for dd, lt in enumerate([C_LS, C_US]):
                        for hf in range(2):
                            bk = 5 + hf
                            tr.op('pe', 'matmul', PB[bk][:], lhsT=cf(lt),
                                  rhs=R1[:, 2 * s + dd, hf * 4:hf * 4 + 4, :].rearrange("p h i -> p (h i)"), start=True,
                                  stop=True, reads=['CF', ('R1', s, dd)], writes=[pk(bk)])
                            tr.op('act', 'activation',
                                  out=lm[:, 2 * s + dd, hf * 4:hf * 4 + 4, :].rearrange("p h i -> p (h i)"),
                                  in_=PB[bk][:], func=AF.Exp, writes=[pk(bk), ('lm', s, dd)])
                    for g in range(2):
                        tr.op('pe', 'matmul', PB[7][:, g * 128:(g + 1) * 128], lhsT=BCT[:, ms, g, tk], rhs=BCT[:, ms, 2 + g, tk],
                              start=(g == 0), stop=True, skip_group_check=True, reads=[('BCT', ms)], writes=[pk(7)])
                    for dd, tri in enumerate([C_U, C_L]):
                        tr.op('dve', 'tensor_tensor', out=cbm[:, 2 * s + dd],
                              in0=PB[7][:, 0:256].rearrange("p (g i) -> p g i", g=2),
                              in1=cf(tri).unsqueeze(1).to_broadcast([128, 2, 128]), op=ALU.mult,
                              reads=['CF'], writes=[pk(7), ('cbm', s, dd)])
                        tr.op('dve', 'tensor_tensor', out=WT[:, 2 * s + dd].rearrange("p (g r) i -> p g r i", g=2),
                              in0=lm[:, 2 * s + dd].rearrange("p (g r) i -> p g r i", g=2),
                              in1=cbm[:, 2 * s + dd].unsqueeze(2).to_broadcast([128, 2, 4, 128]), op=ALU.mult,
                              reads=[('lm', s, dd), ('cbm', s, dd)], writes=[('WT', s, dd)])
                if not fwd:
                    tr.op('act', 'copy', out=bstl[:, s, :], in_=stB, reads=['stB'], writes=[('bstl', s)])
                    tr.dma('sp', BSTD[ch], bstl[:, s, :], key=('bsto', s), reads=[('bstl', s)], writes=[('BSTD', ch)])
                for g in range(2):
                    tr.op('pe', 'matmul', PB[1][:, g * 256:(g + 1) * 256], lhsT=Btm[:, s, g * 128:(g + 1) * 128],
                          rhs=xdtd[:, s, g * 256:(g + 1) * 256], start=(g == 0), stop=True, skip_group_check=True,
                          reads=[('Btm', s), ('xdtd', s)], writes=[pk(1)])
                tr.op('dve', 'tensor_tensor', out=v3(st), in0=v3(st), in1=b8(EE[:, s, 16 + di * 8:24 + di * 8]),
                      op=ALU.mult, reads=[('EE', s)], writes=[skey])
                tr.op('dve', 'tensor_tensor', out=st, in0=st, in1=PB[1][:], op=ALU.add, writes=[pk(1), skey])
                if (fwd and (ch + 1) * 128 == SEG) or ((not fwd) and ch * 128 == SEG):
                    tr.op('dve', 'tensor_scalar', out=st, in0=st, scalar1=FUSE, scalar2=None, op0=ALU.mult,
                          reads=['vtab'], writes=[skey])
                if fwd:
                    tr.op('dve', 'tensor_copy', out=prevF[:, (p + 1) % 2], in_=stF, reads=['stF'], writes=[('prevF', (p + 1) % 2)])

            def S3(p):
                ch = chunks[p]
                mt, j = ch // 4, ch % 4
                ms = (p // 4) % 2
                s = p % 2
                tk = slice(j * 128, (j + 1) * 128)
                first = True
                for dd in range(2):
                    for h in range(8):
                        tr.op('pe', 'matmul', PB[4][:, h * 64:(h + 1) * 64], lhsT=WT[:, 2 * s + dd, h, :],
                              rhs=xdt[:, s, dd, h * 64:(h + 1) * 64], start=first, stop=True, skip_group_check=True,
                              reads=[('WT', s, dd), ('xdt', s)], writes=[pk(4)])
                        first = False
                for g in range(2):
                    tr.op('pe', 'matmul', PB[5][:, g * 256:(g + 1) * 256], lhsT=BCT[:, ms, 2 + g, tk],
                          rhs=prevF[:, s, g * 256:(g + 1) * 256], start=(g == 0), stop=True, skip_group_check=True,
                          reads=[('BCT', ms), ('prevF', s)], writes=[pk(5)])
                    tr.op('pe', 'matmul', PB[6][:, g * 256:(g + 1) * 256], lhsT=BCT[:, ms, 2 + g, tk],
                          rhs=bstl[:, s, g * 256:(g + 1) * 256], start=(g == 0), stop=True, skip_group_check=True,
                          reads=[('BCT', ms), ('bstl', s)], writes=[pk(6)])
                y_ = yb[:, s]
                t_ = t2[:, s]
                tr.op('dve', 'tensor_tensor', out=v3(y_), in0=v3(PB[5][:]), in1=b8(EE[:, s, 0:8]), op=ALU.mult,
                      reads=[('EE', s)], writes=[pk(5), ('yb', s)])
                tr.op('dve', 'tensor_tensor', out=v3(t_), in0=v3(PB[6][:]), in1=b8(EE[:, s, 8:16]), op=ALU.mult,
                      reads=[('EE', s)], writes=[pk(6), ('t2', s)])
                tr.op('pool', 'tensor_tensor', out=y_, in0=y_, in1=t_, op=ALU.add, reads=[('t2', s)], writes=[('yb', s)])
                tr.op('dve', 'tensor_tensor', out=y_, in0=y_, in1=PB[4][:], op=ALU.add, writes=[pk(4), ('yb', s)])
                tr.op('pool', 'tensor_tensor', out=t_, in0=xs[:, s], in1=Dfull.rearrange("p h e -> p (h e)"),
                      op=ALU.mult, reads=[('xs', s), 'Dfull'], writes=[('t2', s)])
                tr.op('pool', 'tensor_tensor', out=y_, in0=y_, in1=t_, op=ALU.add, reads=[('t2', s)], writes=[('yb', s)])
                tr.op('pool', 'tensor_tensor', out=y_, in0=y_, in1=sz[:, s], op=ALU.mult, reads=[('sz', s)],
                      writes=[('yb', s)])

            def S3b(p):
                s = p % 2
                y_ = yb[:, s]
                rms_rstd(y_, 512, ('yb', s), s, 512)
                tr.op('dve', 'scalar_tensor_tensor', out=yn[:, s], in0=y_, scalar=rstd2[:, s:s + 1], in1=gssd[:],
                      op0=ALU.mult, op1=ALU.mult, reads=[('yb', s), ('rstd', s), 'gssd'], writes=[('yn', s)])

            def S4(p):
                ch = chunks[p]
                mt, j = ch // 4, ch % 4
                mo = mt % 2
                s = p % 2
                tk = slice(j * 128, (j + 1) * 128)
                pt = pbf(2)
                for c in range(4):
                    tr.op('pe', 'transpose', out=pt[:, c * 128:(c + 1) * 128], in_=yn[:, s, c * 128:(c + 1) * 128],
                          identity=cb(C_ID), reads=[('yn', s), 'CB'], writes=[pk(2)])
                tr.op('act', 'copy', out=ssdT_st[:, mo, :, tk], in_=pt[:, 0:512].rearrange("p (c t) -> p c t", c=4),
                      writes=[pk(2), ('ssdT_st', mo)])
                if j == 3:
                    tr.dma('sp', SSDT[:, :, mt * 512:(mt + 1) * 512].rearrange("c p t -> p c t"), ssdT_st[:, mo],
                           key=('ssdt_o', mo), reads=[('ssdT_st', mo)], writes=['SSDT'])

            if mode == 'prep':
                prep_load(0)
                for mi in range(len(macros)):
                    if mi + 1 < len(macros):
                        prep_load(mi + 1)
                    prep_A(mi, 0)
                    for c in range(8):
                        if c + 1 < 8:
                            prep_A(mi, c + 1)
                        prep_B(mi, c)
                return
            for t in range(-7, n + 3):
                if fwd:
                    if 0 <= t - 2 < n:
                        S4(t - 2)
                    if 0 <= t - 1 < n:
                        S3b(t - 1)
                    if 0 <= t < n:
                        S3(t)
                if 0 <= t + 1 < n:
                    S2(t + 1)
                if 0 <= t + 2 < n:
                    S1(t + 2)
                for mi in range(len(macros)):
                    o = t - (4 * mi - 7)
                    if o == 0:
                        prep_load(mi)
                    elif 1 <= o <= 4:
                        prep_c(mi, 2 * (o - 1))
                        prep_c(mi, 2 * (o - 1) + 1)
                        if cached and o == 3:
                            prep_load_b(mi)

        run_ssd('prep')
        run_ssd('bwd')
        run_ssd('fwd')

        if stop_after not in ('ssd',):
            phase()
            w_v = ph([128, 8, 512], BF16)
            tr.dma('sp', w_v, WINB[:, :, 2576:3088], key='w_v', writes=['w_v'])
            xmv = ph([128, 2, 8, 512], BF16)
            vst = ph([128, 2, 4, 520], BF16)
            zrow = ph([128, 520], BF16)
            tr.op('dve', 'memset', zrow, 0.0, writes=['zrow'])
            tr.op('dve', 'memset', vst, 1.0, writes=[('vst', 0), ('vst', 1)])
            for i in range(PADV // 128):
                tr.dma('pool', VS[i * 128:(i + 1) * 128, :], zrow, key='vs_z', reads=['zrow'], writes=['VS'])
                tr.dma('pool', VS[PADV + NT + i * 128:PADV + NT + (i + 1) * 128, :], zrow, key='vs_z', reads=['zrow'],
                       writes=['VS'])
            for mt in range(NMT):
                s = mt % 2
                tr.dma('sp', xmv[:, s], XNT[:, :, 2 + mt * 512:2 + mt * 512 + 512].rearrange("k p t -> p k t"),
                       key=('xmv', s), writes=[('xmv', s)])
                for j in range(4):
                    bk = j % 2
                    for k in range(8):
                        tr.op('pe', 'matmul', PB[bk][:], lhsT=xmv[:, s, k, j * 128:(j + 1) * 128], rhs=w_v[:, k, :],
                              start=(k == 0), stop=(k == 7), reads=['w_v', ('xmv', s)], writes=[pk(bk)])
                    tr.op('act', 'copy', out=vst[:, s, j, :].rearrange("p (h e) -> p h e", h=8)[:, :, 0:64],
                          in_=PB[bk][:].rearrange("p (h e) -> p h e", h=8), writes=[pk(bk), ('vst', s)])
                tr.dma('pool', VS[PADV + mt * 512:PADV + (mt + 1) * 512, :].rearrange("(j p) c -> p j c", p=128),
                       vst[:, s], key=('vs_o', s), reads=[('vst', s)], writes=['VS'])

        if stop_after not in ('ssd', 'p3v'):
            phase()
            w_qk = ph([128, 8, 1024], BF16)
            tr.dma('sp', w_qk, WINB[:, :, 1552:2576], key='w_qk', writes=['w_qk'])
            kT = ph([128, 4, 4096], BF16)
            qT = ph([128, 4, 2048], BF16)
            xma = ph([128, 2, 8, 512], BF16)
            cst = ph([128, 2, 2, 512], F32)
            NRS = 2
            sqb = ph([128, NRS, 512], BF16)
            rsn = ph([128, 2, 512], F32)
            qn = ph([128, 2, 512], F32)
            qnb = ph([128, NRS, 512], BF16)
            tA = ph([128, 2, 512], F32)
            tB = ph([128, 2, 512], F32)
            Vt2 = ph([128, 69, 130], BF16)
            NS = 4
            LAG = 3
            PT = ph([128, NS, 256], BF16)
            accs = ph([65, 2, 2048], F32)
            rden = ph([64, 2, 512], F32)
            sqa = ph([64, 2, 512], BF16)
            ssacc = ph([64, 2048], F32)
            attl = rden
            attT_st = ph([64, 8, 512], BF16)
            tr.op('dve', 'memset', kT, 0.0, writes=['kT'])
            MLU = ph([128, 256], BF16)
            tr.op('dve', 'tensor_copy', out=MLU[:, 0:128], in_=cf(C_L), reads=['CF'], writes=['MLU'])
            tr.op('dve', 'tensor_copy', out=MLU[:, 128:256], in_=cf(C_U), reads=['CF'], writes=['MLU'])
            VCOL = {(1, 1): 0, ('f', 'f'): 1, (0, 0): 2, (1, 'f'): 3, ('f', 1): 4, (1, 0): 5, (0, 1): 6, ('f', 0): 7,
                    (0, 'f'): 8}
            NRS = 2

            def nr_stages(u, unit):
                col, gcol, dst, dkey, xs_ = unit
                s = u % NRS
                s2_ = u % 2
                pbank = u % 4
                b2 = 4 + (u % 2)
                b3 = 6 + (u % 2)

                def s1():
                    for k in range(8):
                        tr.op('pe', 'matmul', PB[pbank][:], lhsT=w_qk[:, k, col:col + 128], rhs=xma[:, xs_, k, :],
                              start=(k == 0), stop=(k == 7), reads=['w_qk', ('xma', xs_)], writes=[pk(pbank)])

                def s2():
                    tr.op('act', 'activation', out=sqb[:, s], in_=PB[pbank][:], func=AF.Square, reads=[pk(pbank)],
                          writes=[('sqb', s)])
                    tr.op('pe', 'matmul', PB[b2][:], lhsT=cb(C_BLK), rhs=sqb[:, s], start=True, stop=True,
                          reads=['CB', ('sqb', s)], writes=[pk(b2)])

                def s3():
                    tr.op('act', 'activation', out=rsn[:, s2_], in_=PB[b2][:], func=AF.Ln, scale=1.0 / 64, bias=epsT[:],
                          reads=['eps'], writes=[pk(b2), ('rsn', s2_)])
                    tr.op('act', 'activation', out=rsn[:, s2_], in_=rsn[:, s2_], func=AF.Exp, scale=-0.5, reads=[('rsn', s2_)], writes=[('rsn', s2_)])
                    tr.op('dve', 'scalar_tensor_tensor', out=qn[:, s2_], in0=PB[pbank][:], scalar=gqk[:, gcol:gcol + 1],
                          in1=rsn[:, s2_], op0=ALU.mult, op1=ALU.mult, reads=['gqk', ('rsn', s2_)],
                          writes=[pk(pbank), ('qn', s2_)])
                    tr.op('act', 'copy', out=qnb[:, s], in_=qn[:, s2_], reads=[('qn', s2_)], writes=[('qnb', s)])
                    tr.op('pe', 'matmul', PB[b3][:], lhsT=cb(C_P), rhs=qnb[:, s], start=True, stop=True,
                          reads=['CB', ('qnb', s)], writes=[pk(b3)])
                    tr.op('pool', 'tensor_tensor', out=tB[:, s2_], in0=qn[:, s2_], in1=cst[:, xs_, 0, :], op=ALU.mult,
                          reads=[('qn', s2_), ('cst', xs_)], writes=[('tB', s2_)])

                def s4():
                    tr.op('dve', 'tensor_tensor', out=tA[:, s2_], in0=PB[b3][:], in1=cst[:, xs_, 1, :], op=ALU.mult,
                          reads=[('cst', xs_)], writes=[pk(b3), ('tA', s2_)])
                    tr.op('dve', 'tensor_tensor', out=dst, in0=tA[:, s2_], in1=tB[:, s2_], op=ALU.add,
                          reads=[('tA', s2_), ('tB', s2_)], writes=[dkey])
                return [s1, s2, s3, s4]

            def run_units(units, hooks):
                st = [nr_stages(u, un) for u, un in enumerate(units)]
                n = len(st)
                for t in range(n + 3):
                    for fn in hooks.get(t, []):
                        fn()
                    for k in range(4):
                        if 0 <= t - k < n:
                            st[t - k][k]()

            slot_ctr = [0]
            pending = []

            def run_pending(nmax):
                k = 0
                while pending and k < nmax:
                    fn = pending.pop(0)
                    sl = slot_ctr[0] % NS
                    fn(sl)
                    k += 1

            def make_post(h, a0):
                ap_ = h % 2
                steps = []
                for b_ in range(4):
                    qs = slice(b_ * 512, (b_ + 1) * 512)

                    def st1(sl, b_=b_, qs=qs):
                        r_ = b_ % 2
                        tr.op('pe', 'matmul', PB[sl][0:64, :], lhsT=CF[0:65, C_SEL, 0:64], rhs=accs[:, ap_, qs], start=True,
                              stop=True, reads=['CF', ('accs', ap_)], writes=[pk(sl)])
                        tr.op('act', 'activation', out=rden[:, r_], in_=PB[sl][0:64, :], func=AF.Ln, writes=[pk(sl), ('rden', r_)])
                        tr.op('act', 'activation', out=rden[:, r_], in_=rden[:, r_], func=AF.Exp, scale=-1.0, reads=[('rden', r_)], writes=[('rden', r_)])
                        tr.op('dve', 'tensor_tensor', out=accs[0:64, ap_, qs], in0=accs[0:64, ap_, qs], in1=rden[:, r_],
                              op=ALU.mult, reads=[('rden', r_)], writes=[('accs', ap_)])
                        tr.op('act', 'activation', out=sqa[:, r_], in_=accs[0:64, ap_, qs], func=AF.Square,
                              reads=[('accs', ap_)], writes=[('sqa', r_)])

                    def st2(sl, b_=b_, qs=qs):
                        r_ = b_ % 2
                        tr.op('pe', 'matmul', PB[sl][0:64, :], lhsT=CB[0:64, C_ONES, 0:64], rhs=sqa[:, r_], start=True,
                              stop=True, reads=['CB', ('sqa', r_)], writes=[pk(sl)])
                        if h == 0:
                            tr.op('act', 'copy', out=ssacc[:, qs], in_=PB[sl][0:64, :], writes=[pk(sl), 'ssacc'])
                        else:
                            tr.op('dve', 'tensor_tensor', out=ssacc[:, qs], in0=ssacc[:, qs], in1=PB[sl][0:64, :],
                                  op=ALU.add, writes=[pk(sl), 'ssacc'])
                        if b_ == 3:
                            tr.dma('sp', ATTF[h, :, a0:a0 + AM], accs[0:64, ap_, :], key=('attf_o', ap_), reads=[('accs', ap_)],
                                   writes=[('ATTF', h)])
                    steps.append(st1)
                    steps.append(st2)
                return steps

            for am in range(NAM):
                a0 = am * AM
                sega = a0 // SEG
                valid = [sbk for sbk in range(8) if 0 <= a0 - 1024 + sbk * 512 < NT]
                units = []
                hooks = {}

                def mk_load(i):
                    sbk = valid[i]
                    tok0 = a0 - 1024 + sbk * 512
                    xs_ = i % 2

                    def fn():
                        tr.dma('sp', xma[:, xs_], XNT[:, :, 2 + tok0:2 + tok0 + 512].rearrange("k p t -> p k t"),
                               key=('xma', xs_), writes=[('xma', xs_)])
                        tr.dma('sp', cst[:, xs_], cs_d[:, :, tok0:tok0 + 512].rearrange("a p t -> p a t"),
                               key=('cst', xs_), writes=[('cst', xs_)])
                    return fn

                for i, sbk in enumerate(valid):
                    f_i = len(units)
                    if i == 0:
                        hooks.setdefault(0, []).append(mk_load(0))
                    if i + 1 < len(valid):
                        hooks.setdefault(f_i + (4 if i > 0 else 0), []).append(mk_load(i + 1))
                    for c in range(4):
                        units.append((512 + c * 128, 1, kT[:, c, sbk * 512:(sbk + 1) * 512], 'kT', i % 2))
                        if 2 <= sbk < 6:
                            units.append((c * 128, 0, qT[:, c, (sbk - 2) * 512:(sbk - 1) * 512], 'qT', i % 2))
                run_units(units, hooks)
                for hp in range(4):
                    tix = {}
                    ti = 0
                    for d in (1, 4, 16):
                        nu = AM // (128 * d)
                        for r in range(d):
                            for u in range(nu + 1):
                                tb = a0 - 64 * d + 128 * d * u
                                st_ = []
                                for hb_ in (tb, tb + 64 * d):
                                    if hb_ < 0 or hb_ >= NT:
                                        st_.append(0)
                                    elif hb_ // SEG == sega:
                                        st_.append(1)
                                    else:
                                        st_.append('f')
                                st_ = tuple(st_)
                                if st_ == (0, 0):
                                    continue
                                row0 = PADV + tb + r
                                tr.dma('pool', Vt2[:, ti, :], VS[row0:row0 + 127 * d + 1:d, hp * 130:(hp + 1) * 130],
                                       key=('vt', ti // 9), reads=['VS'], writes=[('Vt2', ti)])
                                tix[(d, r, u)] = (ti, VCOL[st_])
                                ti += 1
                    for g_ in range((ti + 8) // 9):
                        tr.batch_end(('vt', g_), [('Vt2', t_) for t_ in range(g_ * 9, min(ti, g_ * 9 + 9))])
                    for hh in range(2):
                        h = hp * 2 + hh
                        if os.environ.get('PHASE_DBG') and am == 1:
                            print('head start', h, tr.cnt['pe'])
                        c = hp
                        rb = hh * 64
                        apar = h % 2
                        first_in_bank = [True] * 4
                        items = []
                        for d in (1, 4, 16):
                            nu = AM // (128 * d)
                            for r in range(d):
                                for ub in range(nu):
                                    blocks = []
                                    for bi, u in enumerate((ub, ub + 1)):
                                        if (d, r, u) in tix:
                                            ti_, vc = tix[(d, r, u)]
                                            blocks.append((bi, ti_, vc, 1024 - 64 * d + 128 * d * u + r))
                                    items.append((d, r, ub, 128 * d * ub + r, blocks))
                        slots = {}

                        def emit_S(i):
                            d, r, ub, qc0, blocks = items[i]
                            sl = slot_ctr[0] % NS
                            slot_ctr[0] += 1
                            slots[i] = sl
                            for (bi, ti_, vc, kc0) in blocks:
                                tr.op('pe', 'matmul', PB[sl][:, bi * 128:(bi + 1) * 128],
                                      lhsT=kT[rb:rb + 64, c, kc0:kc0 + 127 * d + 1:d],
                                      rhs=qT[rb:rb + 64, c, qc0:qc0 + 127 * d + 1:d], start=True, stop=True,
                                      skip_group_check=True, reads=['kT', 'qT'], writes=[pk(sl)])
                            lo_ = min(b_[0] for b_ in blocks)
                            hi_ = max(b_[0] for b_ in blocks) + 1
                            if len(blocks) == 2 and blocks[0][2] == blocks[1][2]:
                                vc = blocks[0][2]
                                tr.op('act', 'activation', out=PT[:, sl, :], in_=PB[sl][:, 0:256], func=AF.Exp,
                                      scale=0.125, bias=vtab[:, 9 + vc:10 + vc], reads=['vtab'],
                                      writes=[pk(sl), ('PT', sl)])
                            else:
                                for (bi, ti_, vc, kc0) in blocks:
                                    tr.op('act', 'activation', out=PT[:, sl, bi * 128:(bi + 1) * 128],
                                          in_=PB[sl][:, bi * 128:(bi + 1) * 128], func=AF.Exp, scale=0.125,
                                          bias=vtab[:, 9 + vc:10 + vc], reads=['vtab'], writes=[pk(sl), ('PT', sl)])
                            tr.op('dve', 'tensor_tensor', out=PT[:, sl, lo_ * 128:hi_ * 128],
                                  in0=PT[:, sl, lo_ * 128:hi_ * 128],
                                  in1=CB[:, C_L + lo_:C_L + hi_, :].rearrange("p a b -> p (a b)") if lo_ == 0 and hi_ == 1 else
                                  (MLU[:, lo_ * 128:hi_ * 128]), op=ALU.mult, reads=['MLU'], writes=[('PT', sl)])

                        def emit_PV(i):
                            d, r, ub, qc0, blocks = items[i]
                            sl = slots[i]
                            for (bi, ti_, vc, kc0) in blocks:
                                vl = Vt2[:, ti_, hh * 65:hh * 65 + 65]
                                if d == 16:
                                    pieces = [(b_, PB[4 + b_][0:65, r:512:16],
                                               PT[:, sl, bi * 128 + 32 * b_:bi * 128 + 32 * b_ + 32]) for b_ in range(4)]
                                elif d == 4:
                                    pieces = [(ub, PB[4 + ub][0:65, r:512:4], PT[:, sl, bi * 128:(bi + 1) * 128])]
                                else:
                                    b_ = qc0 // 512
                                    pieces = [(b_, PB[4 + b_][0:65, qc0 % 512:qc0 % 512 + 128],
                                               PT[:, sl, bi * 128:(bi + 1) * 128])]
                                for (b_, oap, rap) in pieces:
                                    tr.op('pe', 'matmul', oap, lhsT=vl, rhs=rap, start=first_in_bank[b_], stop=True,
                                          skip_group_check=True, reads=[('PT', sl), ('Vt2', ti_)], writes=[pk(4 + b_)])
                                    first_in_bank[b_] = False

                        n_it = len(items)
                        for i in range(n_it + LAG):
                            if i < n_it:
                                emit_S(i)
                            if i % 5 == 4:
                                run_pending(1)
                            if i >= LAG:
                                emit_PV(i - LAG)
                        run_pending(100)
                        for b_ in range(4):
                            tr.op('act', 'copy', out=accs[:, apar, b_ * 512:(b_ + 1) * 512], in_=PB[4 + b_][0:65, :],
                                  writes=[pk(4 + b_), ('accs', apar)])
                        pending.extend(make_post(h, a0))
                run_pending(100)
                tr.op('act', 'activation', out=ssacc, in_=ssacc, func=AF.Ln, scale=1.0 / 512, bias=epsT[0:64, :],
                      reads=['ssacc', 'eps'], writes=['ssacc'])
                tr.op('act', 'activation', out=ssacc, in_=ssacc, func=AF.Exp, scale=-0.5, reads=['ssacc'], writes=['ssacc'])
                for qb in range(4):
                    qs = slice(qb * 512, (qb + 1) * 512)
                    for h in range(8):
                        sl = h % 2
                        tr.dma('pool', attl[:, sl, :], ATTF[h, :, a0 + qb * 512:a0 + (qb + 1) * 512], key=('attl', sl),
                               reads=[('ATTF', h)], writes=[('rden', sl)])
                        tr.op('dve', 'scalar_tensor_tensor', out=attT_st[:, h, :], in0=attl[:, sl, :], scalar=gao[:, h:h + 1],
                              in1=ssacc[:, qs], op0=ALU.mult, op1=ALU.mult, reads=[('rden', sl), 'gao', 'ssacc'],
                              writes=['attT_st'])
                    tr.dma('pool', ATTT.rearrange("c (two e) t -> e (c two) t", two=2)[:, :, a0 + qb * 512:a0 + (qb + 1) * 512],
                           attT_st, key='attt_o', reads=['attT_st'], writes=['ATTT'])
        if stop_after not in ('ssd', 'p3v', 'p3a'):
            phase()
            WA = ph([128, 1, 8, 1024], BF16)
            w1p = ph([128, 3, 8, 512], BF16)
            w2p = ph([128, 3, 4, 1024], BF16)
            hTp = ph([128, 2, 4, 512], BF16)
            hbuf = ph([128, 2, 4, 1024], F32)
            mixT = ph([128, 8, 512], BF16)
            hnT = ph([128, 2, 8, 512], BF16)
            kmT = ph([128, 2, 8, 256], BF16)
            Vm = ph([128, 2, 2, 1024], BF16)
            sqx = ph([128, 2, 512], BF16)
            rsx = ph([128, 512], F32)
            qx = ph([128, 2, 512], BF16)
            PTx = ph([128, 2, 512], BF16)
            rdx = ph([128, 512], F32)
            oT = ph([128, 8, 512], BF16)
            rl = ph([128, 2, 512], F32)
            memT = rl.bitcast(BF16).rearrange("p a (b c) -> p (a b) c", b=4)
            wa_i = [0]

            def load_wa(src_ap):
                s = 0
                wa_i[0] += 1
                tr.dma('sp', WA[:, s], src_ap, key=('WA', s), writes=[('WA', s)])
                return s

            for sg in range(2):
                for j in range(2):
                    s = j % 2
                    tr.dma('sp', xt[:, s, :], mem_d[sg, j * 128:(j + 1) * 128, :], key=('xt', s), writes=[('xt', s)])
                    norm_T(xt[:, s, :], ('xt', s), 2, memT, 'memT', j, 0)
                sk = load_wa(WKVB[:, :, 0:1024])
                for hh in range(4):
                    for cc in range(2):
                        c = 2 * hh + cc
                        for k in range(8):
                            tr.op('pe', 'matmul', PB[cc][:, 0:256], lhsT=WA[:, sk, k, c * 128:(c + 1) * 128], rhs=memT[:, k, :],
                                  start=(k == 0), stop=(k == 7), reads=[('WA', sk), 'memT'], writes=[pk(cc)])
                        tr.op('act', 'activation', out=sqx[:, cc, 0:256], in_=PB[cc][:, 0:256], func=AF.Square,
                              reads=[pk(cc)], writes=[('sqx', cc)])
                    for cc in range(2):
                        tr.op('pe', 'matmul', PB[2][:, 0:256], lhsT=cb(C_ONES), rhs=sqx[:, cc, 0:256], start=(cc == 0),
                              stop=(cc == 1), reads=['CB', ('sqx', cc)], writes=[pk(2)])
                    tr.op('act', 'activation', out=rsx[:, 0:256], in_=PB[2][:, 0:256], func=AF.Ln, scale=1.0 / 256,
                          bias=epsT[:], reads=['eps'], writes=[pk(2), 'rsx'])
                    tr.op('act', 'activation', out=rsx[:, 0:256], in_=rsx[:, 0:256], func=AF.Exp, scale=-0.5, reads=['rsx'], writes=['rsx'])
                    for cc in range(2):
                        tr.op('dve', 'scalar_tensor_tensor', out=kmT[:, sg, 2 * hh + cc, :], in0=PB[cc][:, 0:256],
                              scalar=gx[:, 2 + cc:3 + cc], in1=rsx[:, 0:256], op0=ALU.mult, op1=ALU.mult,
                              reads=['gx', 'rsx'], writes=[pk(cc), 'kmT'])
                sv = load_wa(WKVB[:, :, 1024:2048])
                for mtl in range(2):
                    for half in range(2):
                        bk = half
                        for k in range(8):
                            tr.op('pe', 'matmul', PB[bk][:], lhsT=memT[:, k, mtl * 128:(mtl + 1) * 128],
                                  rhs=WA[:, sv, k, half * 512:(half + 1) * 512], start=(k == 0), stop=(k == 7),
                                  reads=[('WA', sv), 'memT'], writes=[pk(bk)])
                        tr.op('act', 'copy', out=Vm[:, sg, mtl, half * 512:(half + 1) * 512], in_=PB[bk][:],
                              writes=[pk(bk), 'Vm'])
            tr.barrier()

            def F_gen(mt):
                t0 = mt * 512
                sg = t0 // SEG
                par = mt % 2
                hb = hbuf[:, par]
                hn = hnT[:, par]
                hkey = lambda j: ('hbuf', par, j)
                nkey = ('hnT', par)
                tr.dma('sp', mixT[:, 0:4, :], SSDT[:, :, t0:t0 + 512].rearrange("c p t -> p c t"), key='mixT',
                       writes=['mixT'])
                tr.dma('sp', mixT[:, 4:8, :], ATTT[:, :, t0:t0 + 512].rearrange("c p t -> p c t"), key='mixT',
                       writes=['mixT'])
                s_ = load_wa(WOUTB)
                yield
                for j in range(4):
                    xs_ = j % 2
                    tr.dma('sp', xt[:, xs_, :], x_d[t0 + j * 128:t0 + (j + 1) * 128, :], key=('xt', xs_), writes=[('xt', xs_)])
                    for half in range(2):
                        bk = half
                        for kc in range(8):
                            tr.op('pe', 'matmul', PB[bk][:], lhsT=mixT[:, kc, j * 128:(j + 1) * 128],
                                  rhs=WA[:, s_, kc, half * 512:(half + 1) * 512], start=(kc == 0), stop=(kc == 7),
                                  reads=[('WA', s_), 'mixT'], writes=[pk(bk)])
                        tr.op('dve', 'tensor_tensor', out=hb[:, j, half * 512:(half + 1) * 512], in0=PB[bk][:],
                              in1=xt[:, xs_, half * 512:(half + 1) * 512], op=ALU.add, reads=[('xt', xs_)],
                              writes=[pk(bk), hkey(j)])
                for j in range(4):
                    norm_T(hb[:, j, :], hkey(j), 1, hn, nkey, j, j % 2)
                    yield
                s_ = load_wa(WQB)
                for hh in range(4):
                    for cc in range(2):
                        c = 2 * hh + cc
                        for k in range(8):
                            tr.op('pe', 'matmul', PB[cc][:], lhsT=WA[:, s_, k, c * 128:(c + 1) * 128], rhs=hn[:, k, :],
                                  start=(k == 0), stop=(k == 7), reads=[('WA', s_), nkey], writes=[pk(cc)])
                        tr.op('act', 'activation', out=sqx[:, cc, :], in_=PB[cc][:], func=AF.Square, reads=[pk(cc)],
                              writes=[('sqx', cc)])
                    yield
                    for cc in range(2):
                        tr.op('pe', 'matmul', PB[2][:], lhsT=cb(C_ONES), rhs=sqx[:, cc, :], start=(cc == 0), stop=(cc == 1),
                              reads=['CB', ('sqx', cc)], writes=[pk(2)])
                    yield
                    tr.op('act', 'activation', out=rsx, in_=PB[2][:], func=AF.Ln, scale=1.0 / 256, bias=epsT[:],
                          reads=['eps'], writes=[pk(2), 'rsx'])
                    tr.op('act', 'activation', out=rsx, in_=rsx, func=AF.Exp, scale=-0.5, reads=['rsx'], writes=['rsx'])
                    for cc in range(2):
                        tr.op('dve', 'scalar_tensor_tensor', out=qx[:, cc, :], in0=PB[cc][:], scalar=gx[:, cc:cc + 1],
                              in1=rsx, op0=ALU.mult, op1=ALU.mult, reads=['gx', 'rsx'], writes=[pk(cc), 'qx'])
                    yield
                    for mtl in range(2):
                        for cc in range(2):
                            tr.op('pe', 'matmul', PB[2][:], lhsT=kmT[:, sg, 2 * hh + cc, mtl * 128:(mtl + 1) * 128],
                                  rhs=qx[:, cc, :], start=(cc == 0), stop=(cc == 1), reads=['kmT', 'qx'], writes=[pk(2)])
                        tr.op('act', 'activation', out=PTx[:, mtl, :], in_=PB[2][:], func=AF.Exp, scale=1.0 / 16,
                              writes=[pk(2), 'PTx'])
                    yield
                    for mtl in range(2):
                        tr.op('pe', 'matmul', PB[2][:], lhsT=cb(C_ONES), rhs=PTx[:, mtl, :], start=(mtl == 0), stop=(mtl == 1),
                              reads=['CB', 'PTx'], writes=[pk(2)])
                    for cc in range(2):
                        c = 2 * hh + cc
                        for mtl in range(2):
                            tr.op('pe', 'matmul', PB[cc][:], lhsT=Vm[:, sg, mtl, c * 128:(c + 1) * 128], rhs=PTx[:, mtl, :],
                                  start=(mtl == 0), stop=(mtl == 1), reads=['Vm', 'PTx'], writes=[pk(cc)])
                    tr.op('act', 'activation', out=rdx, in_=PB[2][:], func=AF.Ln, writes=[pk(2), 'rdx'])
                    tr.op('act', 'activation', out=rdx, in_=rdx, func=AF.Exp, scale=-1.0, reads=['rdx'], writes=['rdx'])
                    for cc in range(2):
                        c = 2 * hh + cc
                        tr.op('dve', 'tensor_tensor', out=oT[:, c, :], in0=PB[cc][:], in1=rdx, op=ALU.mult, reads=['rdx'],
                              writes=[pk(cc), 'oT'])
                    yield
                s_ = load_wa(WOB)
                yield
                for j in range(4):
                    for half in range(2):
                        bk = half
                        for kc in range(8):
                            tr.op('pe', 'matmul', PB[bk][:], lhsT=oT[:, kc, j * 128:(j + 1) * 128],
                                  rhs=WA[:, s_, kc, half * 512:(half + 1) * 512], start=(kc == 0), stop=(kc == 7),
                                  reads=[('WA', s_), 'oT'], writes=[pk(bk)])
                        tr.op('dve', 'tensor_tensor', out=hb[:, j, half * 512:(half + 1) * 512], in0=PB[bk][:],
                              in1=hb[:, j, half * 512:(half + 1) * 512], op=ALU.add, writes=[pk(bk), hkey(j)])
                for j in range(4):
                    norm_T(hb[:, j, :], hkey(j), 3, hn, nkey, j, j % 2)
                    yield

            def MLP_gen(mt):
                t0 = mt * 512
                par = mt % 2
                hb = hbuf[:, par]
                hn = hnT[:, par]
                hkey = lambda j: ('hbuf', par, j)
                nkey = ('hnT', par)

                def mlp_load(pc):
                    ws_ = pc % 3
                    tr.dma('sp', w1p[:, ws_], W1B[:, :, pc * 512:(pc + 1) * 512], key=('w1p', ws_), writes=[('w1p', ws_)])
                    tr.dma('pool', w2p[:, ws_], W2B[:, pc * 4:(pc + 1) * 4, :], key=('w2p', ws_), writes=[('w2p', ws_)])

                def mlp_w1(pc):
                    ps_ = pc % 2
                    ws_ = pc % 3
                    if pc + 1 < 8:
                        mlp_load(pc + 1)
                    for f in range(4):
                        bk = 3 + (f % 2)
                        for k in range(8):
                            tr.op('pe', 'matmul', PB[bk][:], lhsT=w1p[:, ws_, k, f * 128:(f + 1) * 128], rhs=hn[:, k, :],
                                  start=(k == 0), stop=(k == 7), reads=[('w1p', ws_), nkey], writes=[pk(bk)])
                        tr.op('act', 'activation', out=rl[:, f % 2, :], in_=PB[bk][:], func=AF.Relu,
                              writes=[pk(bk), ('rl', f % 2)])
                        tr.op('dve', 'tensor_tensor', out=hTp[:, ps_, f, :], in0=rl[:, f % 2, :], in1=rl[:, f % 2, :],
                              op=ALU.mult, reads=[('rl', f % 2)], writes=[('hTp', ps_, f)])
                        yield

                def mlp_w2(pc):
                    ps_ = pc % 2
                    ws_ = pc % 3
                    for j in range(4):
                        for half in range(2):
                            bk = 5 + ((2 * j + half) % 3)
                            for f in range(4):
                                tr.op('pe', 'matmul', PB[bk][:], lhsT=hTp[:, ps_, f, j * 128:(j + 1) * 128],
                                      rhs=w2p[:, ws_, f, half * 512:(half + 1) * 512], start=(f == 0), stop=(f == 3),
                                      reads=[('hTp', ps_, f), ('w2p', ws_)], writes=[pk(bk)])
                            tr.op('dve', 'tensor_tensor', out=hb[:, j, half * 512:(half + 1) * 512], in0=PB[bk][:],
                                  in1=hb[:, j, half * 512:(half + 1) * 512], op=ALU.add, writes=[pk(bk), hkey(j)])
                            yield

                mlp_load(0)
                yield from mlp_w1(0)
                for pc in range(8):
                    if pc + 1 < 8:
                        yield from mlp_w1(pc + 1)
                    yield from mlp_w2(pc)
                tr.dma('pool', y_d[t0:t0 + 512, :].rearrange("(j p) c -> p j c", p=128), hb, key=('y_o', par),
                       reads=[hkey(j) for j in range(4)], writes=['y'])

            def interleave(gm, gf, ratio=4):
                m_alive, f_alive = True, True
                while m_alive or f_alive:
                    if f_alive:
                        try:
                            next(gf)
                        except StopIteration:
                            f_alive = False
                    for _ in range(ratio if f_alive else 1000000):
                        if not m_alive:
                            break
                        try:
                            next(gm)
                        except StopIteration:
                            m_alive = False

            for _ in F_gen(0):
                pass
            for mt in range(NMT):
                gens = [MLP_gen(mt)]
                if mt + 1 < NMT:
                    gens.append(F_gen(mt + 1))
                interleave(*gens) if len(gens) == 2 else [None for _ in gens[0]]
        for n in dbg_d:
            if n == 'ssdt':
                tr.dma('sp', dbg_d[n], SSDT, key='dbg', reads=['SSDT'])
            if n == 'xnt':
                tr.dma('sp', dbg_d[n], XNT, key='dbg', reads=['XNT'])
            if n == 'attt':
                tr.dma('sp', dbg_d[n], ATTT, key='dbg', reads=['ATTT'])
        print('op counts', tr.cnt, 'dma sems', len(tr.dsem))
        tr.emit('sp')
    return nc


def host_inputs(NT, xs_list, mem_list, fuse_list, params, pos_list):
    consts = make_consts()
    half = 8
    inv_freq = np.power(np.float32(500000.0), -np.arange(half, dtype=np.float32) / half).astype(np.float32)
    maps = []
    for c in range(len(xs_list)):
        f = float(fuse_list[c])
        lo = np.arange(128) < 64
        cols = []
        for (a, b) in [(1, 1), (f, f), (0, 0), (1, f), (f, 1), (1, 0), (0, 1), (f, 0), (0, f)]:
            cols.append(np.where(lo, a, b))
        vtab = np.stack(cols, 1).astype(np.float32)
        vtab = np.concatenate([vtab, (vtab - 1.0) * 30000.0], axis=1).astype(np.float32)
        ang = pos_list[c].astype(np.float32)[None, :] * inv_freq[:, None]
        cos = np.ones((128, NT), np.float32)
        sin = np.zeros((128, NT), np.float32)
        for hb in (0, 64):
            cos[hb:hb + 8] = np.cos(ang)
            cos[hb + 8:hb + 16] = np.cos(ang)
            sin[hb:hb + 8] = np.sin(ang)
            sin[hb + 8:hb + 16] = np.sin(ang)
        m = {"x": np.ascontiguousarray(xs_list[c], np.float32), "mem": np.ascontiguousarray(mem_list[c], np.float32),
             "vtab": vtab, "cs": np.stack([cos, sin], 0), "consts": consts}
        p = params
        m["w_in"] = p["w_in"][0]
        m["w_out"] = p["w_out"][0]
        m["xatt_wq"] = p["xatt_wq"][0]
        m["xatt_wkv"] = p["xatt_wkv"][0]
        m["xatt_wo"] = p["xatt_wo"][0]
        m["mlp_w1"] = p["mlp_w1"][0]
        m["mlp_w2"] = p["mlp_w2"][0]
        for k in ["mix_norm_g", "xatt_norm_g", "mem_norm_g", "mlp_norm_g", "ssd_norm_g", "conv_b"]:
            m[k] = p[k].reshape(1, -1)
        m["conv_w"] = p["conv_w"][0]
        m["ssd_A_log"] = p["ssd_A_log"].reshape(1, 16)
        m["ssd_dt_bias"] = p["ssd_dt_bias"].reshape(1, 16)
        m["ssd_D"] = p["ssd_D"].reshape(1, 8)
        m["att_q_norm_g"] = p["att_q_norm_g"].reshape(64, 1)
        m["att_k_norm_g"] = p["att_k_norm_g"].reshape(64, 1)
        m["att_out_norm_g"] = p["att_out_norm_g"].reshape(8, 64)
        m["xatt_q_norm_g"] = p["xatt_q_norm_g"].reshape(2, 128)
        m["xatt_k_norm_g"] = p["xatt_k_norm_g"].reshape(2, 128)
        maps.append({k: np.ascontiguousarray(np.asarray(v, np.float32)) for k, v in m.items()})
    return maps


_NC_CACHE = {}


def kernel(x_prompt, x_sample, mem_prompt, mem_sample, **params):
    NT = 8192
    x_prompt = np.asarray(x_prompt, np.float32)
    x_sample = np.asarray(x_sample, np.float32)
    mem_prompt = np.asarray(mem_prompt, np.float32)
    mem_sample = np.asarray(mem_sample, np.float32)
    p = {k: np.asarray(v, np.float32) for k, v in params.items()}
    xs_list, mem_list, fuse, pos = [], [], [], []
    for b in range(2):
        xs_list.append(x_prompt[b])
        mem_list.append(np.stack([mem_prompt[b], mem_prompt[b]]))
        fuse.append(1)
        pos.append(np.arange(NT))
    for c in range(4):
        xs_list.append(np.concatenate([x_sample[2 * c], x_sample[2 * c + 1]], 0))
        mem_list.append(np.stack([mem_sample[2 * c], mem_sample[2 * c + 1]]))
        fuse.append(0)
        pos.append(np.concatenate([np.arange(NT // 2)] * 2))
    for c in range(2):
        xs_list.append(np.zeros((NT, D), np.float32))
        mem_list.append(np.zeros((2, 256, D), np.float32))
        fuse.append(0)
        pos.append(np.concatenate([np.arange(NT // 2)] * 2))
    maps = host_inputs(NT, xs_list, mem_list, fuse, p, pos)
    if NT not in _NC_CACHE:
        _NC_CACHE[NT] = build(NT)
    res = run_bass_kernel_spmd(_NC_CACHE[NT], maps, core_ids=list(range(8)))
    ys = [np.asarray(r["y"], np.float32) for r in res.results]
    y_prompt = np.stack([ys[0], ys[1]], 0)
    y_sample = np.stack([ys[2 + c // 2][(c % 2) * (NT // 2):(c % 2 + 1) * (NT // 2)] for c in range(8)], 0)
    return (y_prompt, y_sample)
```

```python
import math
import os
ATT_LEVEL = int(os.environ.get('ATT_LEVEL', '9'))
import numpy as np
import concourse.bass as bass
import concourse.mybir as mybir
from concourse.bass_utils import run_bass_kernel_spmd
from contextlib import ExitStack

F32 = mybir.dt.float32
BF16 = mybir.dt.bfloat16
ALU = mybir.AluOpType
AF = mybir.ActivationFunctionType
EPS = 1e-6
ROT = 20000
D = 1024
INC = 3088
PADV = 1024


class TR:
    def __init__(self, nc, es):
        self.nc = nc
        self.es = es
        self.names = ['pe', 'dve', 'act', 'pool', 'sp']
        self.cnt = {e: 0 for e in self.names}
        self.sems = {e: [] for e in self.names}
        self.dsem = {}
        self.dcnt = {}
        self.lastw = {}
        self.readers = {}
        self.waited = {e: {} for e in self.names}
        self.prog = {e: [] for e in self.names}

    def _sem(self, e, idx):
        j = (idx - 1) // ROT
        while len(self.sems[e]) <= j:
            self.sems[e].append(self.es.enter_context(self.nc.semaphore(f"s_{e}_{len(self.sems[e])}")))
        return self.sems[e][j], (idx - 1) % ROT + 1

    def _deps(self, reads, writes):
        deps = []
        for k in reads:
            if k in self.lastw:
                deps.append(self.lastw[k])
        for k in writes:
            if k in self.lastw:
                deps.append(self.lastw[k])
            deps.extend(self.readers.get(k, []))
        return deps

    def _wait(self, e, deps):
        out = []
        for d in deps:
            if d[0] == 'dma':
                _, key, val = d
                if self.waited[e].get(('dma', key), 0) >= val:
                    continue
                out.append((self.dsem[key], val))
                self.waited[e][('dma', key)] = val
            else:
                f, idx = d
                if f == 'pe' and e == 'pe':
                    continue
                if self.waited[e].get(f, 0) >= idx:
                    continue
                out.append(self._sem(f, idx))
                self.waited[e][f] = idx
        return out

    def _commit(self, ref, reads, writes):
        for k in reads:
            self.readers.setdefault(k, []).append(ref)
        for k in writes:
            self.lastw[k] = ref
            self.readers[k] = []

    def op(self, e, meth, *args, reads=(), writes=(), **kw):
        fn = (lambda h, meth=meth, args=args, kw=kw: getattr(h, meth)(*args, **kw))
        w = self._wait(e, self._deps(reads, writes))
        self.cnt[e] += 1
        s, v = self._sem(e, self.cnt[e])
        self.prog[e].append((w, fn, (s, 1)))
        self._commit((e, self.cnt[e]), reads, writes)

    def dma(self, q, out, in_, key, reads=(), writes=(), **kw):
        w = self._wait(q, self._deps(reads, writes))
        if key not in self.dsem:
            self.dsem[key] = self.es.enter_context(self.nc.semaphore(f"d_{len(self.dsem)}"))
            self.dcnt[key] = 0
        self.dcnt[key] += 16
        self.prog[q].append((w, (lambda h, out=out, in_=in_, kw=kw: h.dma_start(out=out, in_=in_, **kw)),
                             (self.dsem[key], 16)))
        self._commit(('dma', key, self.dcnt[key]), reads, writes)

    def batch_end(self, key, res_keys):
        for k in res_keys:
            self.lastw[k] = ('dma', key, self.dcnt[key])

    def barrier(self):
        fin = [(self.dsem[k], v) for k, v in self.dcnt.items()]
        for e in self.names:
            if self.cnt[e] > 0:
                fin.append(self._sem(e, self.cnt[e]))
        for e in self.names:
            w = []
            for (sm, v) in fin:
                w.append((sm, v))
            self.cnt[e] += 1
            s_, v_ = self._sem(e, self.cnt[e])
            self.prog[e].append((w, (lambda h: h.nop()), (s_, 1)))
        self.lastw = {}
        self.readers = {}

    def emit(self, final_q='sp'):
        fin = [(self.dsem[k], v) for k, v in self.dcnt.items()]
        for e in self.names:
            if self.cnt[e] > 0:
                fin.append(self._sem(e, self.cnt[e]))
        blk = self.es.enter_context(self.nc.Block())
        hmap = {'pe': blk.tensor, 'dve': blk.vector, 'act': blk.scalar, 'pool': blk.gpsimd, 'sp': blk.sync}
        for e in self.names:
            prog = self.prog[e]
            extra = fin if e == final_q else []

            def body(h, prog=prog, extra=extra):
                for (w, fn, inc) in prog:
                    for (s, v) in w:
                        h.wait_ge(s, v)
                    fn(h).then_inc(inc[0], inc[1])
                for (s, v) in extra:
                    h.wait_ge(s, v)
            if prog or extra:
                hmap[e](body)


def make_consts():
    k = np.arange(128)[:, None]
    i = np.arange(128)[None, :]
    ident = (k == i)
    U = (k <= i)
    L = (k >= i)
    Us = (k < i)
    Ls = (k > i)
    ones = np.ones((128, 128), bool)
    blk = ((k // 64) == (i // 64))
    P = np.zeros((128, 128), np.float32)
    for hb in (0, 64):
        for e in range(8):
            P[hb + e + 8, hb + e] = -1.0
            P[hb + e, hb + e + 8] = 1.0
    sel = np.zeros((128, 128), np.float32)
    sel[64, :] = 1.0
    negl = np.where(L, 0.0, -30000.0)
    negu = np.where(U, 0.0, -30000.0)
    mats = [ident, U, L, Us, Ls, ones, blk, P, sel, negl, negu]
    return np.concatenate([np.asarray(m, np.float32) for m in mats], axis=1)


C_ID, C_U, C_L, C_US, C_LS, C_ONES, C_BLK, C_P, C_SEL, C_NEGL, C_NEGU = range(11)
NCM = 11


def build(NT, dbg=(), stop_after=None):
    SEG = NT // 2
    NMT = NT // 512
    NCH = NT // 128
    AM = 2048
    NAM = NT // AM
    nc = bass.Bass("TRN2", target_bir_lowering=False)
    din = lambda n, s, dt=F32: nc.dram_tensor(n, s, dt, kind="ExternalInput").ap()
    dscr = lambda n, s, dt: nc.dram_tensor(n, s, dt, kind="Internal").ap()
    x_d = din("x", [NT, D])
    mem_d = din("mem", [2, 256, D])
    vt_d = din("vtab", [128, 18])
    cs_d = din("cs", [2, 128, NT])
    cst_d = din("consts", [128, NCM * 128])
    w_in_d = din("w_in", [D, INC])
    w_out_d = din("w_out", [D, D])
    wq_d = din("xatt_wq", [D, D])
    wkv_d = din("xatt_wkv", [D, 2 * D])
    wo_d = din("xatt_wo", [D, D])
    w1_d = din("mlp_w1", [D, 4 * D])
    w2_d = din("mlp_w2", [4 * D, D])
    g_mix_d = din("mix_norm_g", [1, D])
    g_xatt_d = din("xatt_norm_g", [1, D])
    g_mem_d = din("mem_norm_g", [1, D])
    g_mlp_d = din("mlp_norm_g", [1, D])
    g_ssd_d = din("ssd_norm_g", [1, 512])
    convw_d = din("conv_w", [5, D])
    convb_d = din("conv_b", [1, D])
    alog_d = din("ssd_A_log", [1, 16])
    dtb_d = din("ssd_dt_bias", [1, 16])
    dsk_d = din("ssd_D", [1, 8])
    gq_d = din("att_q_norm_g", [64, 1])
    gk_d = din("att_k_norm_g", [64, 1])
    gao_d = din("att_out_norm_g", [8, 64])
    gxq_d = din("xatt_q_norm_g", [2, 128])
    gxk_d = din("xatt_k_norm_g", [2, 128])
    y_d = nc.dram_tensor("y", [NT, D], F32, kind="ExternalOutput").ap()
    dbg_d = {n: nc.dram_tensor("dbg_" + n, s, dt, kind="ExternalOutput").ap() for (n, s, dt) in dbg}

    XNT = dscr("XNT", [8, 128, NT + 4], BF16)
    SSDT = dscr("SSDT", [4, 128, NT], BF16)
    ATTT = dscr("ATTT", [4, 128, NT], BF16)
    VS = dscr("VS", [NT + 2 * PADV, 520], BF16)
    WINB = dscr("WINB", [128, 8, INC], BF16)
    WOUTB = dscr("WOUTB", [128, 8, D], BF16)
    WQB = dscr("WQB", [128, 8, D], BF16)
    WKVB = dscr("WKVB", [128, 8, 2 * D], BF16)
    WOB = dscr("WOB", [128, 8, D], BF16)
    W1B = dscr("W1B", [128, 8, 4 * D], BF16)
    W2B = dscr("W2B", [128, 32, D], BF16)

    BSTD = dscr("BSTD", [NCH, 128, 512], BF16)
    ATTF = dscr("ATTF", [8, 64, NT], F32)
    XST_D = dscr("XST_D", [NMT, 128, 4, 512], F32)
    BCT_D = dscr("BCT_D", [NMT, 128, 4, 512], BF16)
    SZ_D = dscr("SZ_D", [NMT, 128, 4, 512], F32)
    with ExitStack() as es:
        tr = TR(nc, es)
        _n = [0]

        def sb(shape, dt, name=None):
            _n[0] += 1
            return es.enter_context(nc.sbuf_tensor("s_" + (name or f"t{_n[0]}"), shape, dt))

        ARW = 43008
        ARENA = sb([128, ARW], F32, "arena")
        apos = [0]

        def phase():
            if os.environ.get('PHASE_DBG'):
                print('phase boundary', dict(tr.cnt))
            tr.barrier()
            apos[0] = 0

        def ph(shape, dt, parts=128):
            n = 1
            for v in shape[1:]:
                n *= v
            w = n if dt == F32 else (n + 1) // 2
            o = apos[0]
            apos[0] += w
            assert apos[0] <= ARW, ("arena overflow", apos[0])
            v = ARENA[0:shape[0], o:o + w]
            if dt != F32:
                v = v.bitcast(dt)
            if len(shape) == 3:
                v = v.rearrange("p (a b) -> p a b", a=shape[1])
            elif len(shape) == 4:
                v = v.rearrange("p (a b c) -> p a b c", a=shape[1], b=shape[2])
            return v

        PB = [es.enter_context(nc.psum_tensor(f"pb{i}", [128, 512], F32)) for i in range(8)]
        pk = lambda i: ('ps', i)
        pbf = lambda i: PB[i][:].bitcast(BF16)

        CF = sb([128, NCM, 128], F32, "CF")
        CB = sb([128, NCM, 128], BF16, "CB")
        vtab = sb([128, 18], F32, "vtab")
        epsT = sb([128, 1], F32, "epsT")
        tr.dma('sp', CF[:].rearrange("p a b -> p (a b)"), cst_d, key=('c0', 1), writes=['CF'])
        tr.dma('sp', vtab[:], vt_d, key=('c0', 2), writes=['vtab'])
        tr.op('dve', 'tensor_copy', out=CB[:], in_=CF[:], reads=['CF'], writes=['CB'])
        tr.op('dve', 'memset', epsT[:], EPS, writes=['eps'])
        cf = lambda i: CF[:, i, :]
        cb = lambda i: CB[:, i, :]
        FUSE = vtab[:, 1:2]

        gB = sb([128, 4, D], F32, "gB")
        for i, gd in enumerate([g_mix_d, g_xatt_d, g_mem_d, g_mlp_d]):
            tr.dma('sp', gB[:, i, :], gd.partition_broadcast(128), key=('c0', 3), writes=['gB'])
        gssd = sb([128, 512], F32, "gssd")
        tr.dma('sp', gssd[:], g_ssd_d.partition_broadcast(128), key=('c0', 4), writes=['gssd'])
        cw = sb([128, 8, 5], F32, "cw")
        cbias = sb([128, 8], F32, "cbias")
        for k in range(5):
            tr.dma('sp', cw[:, :, k:k + 1], convw_d[k:k + 1, :].rearrange("o (c p) -> p c o", p=128), key=('c0', 5),
                   writes=['cw'], allow_slow_non_contiguous=True)
        tr.dma('sp', cbias[:].unsqueeze(2), convb_d.rearrange("o (c p) -> p c o", p=128), key=('c0', 6), writes=['cw'],
               allow_slow_non_contiguous=True)
        Abc = sb([128, 16], F32, "Abc")
        dtb = sb([128, 16], F32, "dtb")
        Dbc = sb([128, 8], F32, "Dbc")
        tr.dma('sp', Abc[:], alog_d.partition_broadcast(128), key=('c0', 7), writes=['Abc'])
        tr.dma('sp', dtb[:], dtb_d.partition_broadcast(128), key=('c0', 8), writes=['dtb'])
        tr.dma('sp', Dbc[:], dsk_d.partition_broadcast(128), key=('c0', 9), writes=['Dbc'])
        tr.op('act', 'activation', out=Abc[:], in_=Abc[:], func=AF.Exp, reads=['Abc'], writes=['Abc'])
        tr.op('dve', 'tensor_scalar', out=Abc[:], in0=Abc[:], scalar1=-1.0, scalar2=None, op0=ALU.mult,
              reads=['Abc'], writes=['Abc'])
        gqk = sb([128, 2], F32, "gqk")
        for hb in (0, 64):
            tr.dma('sp', gqk[hb:hb + 64, 0:1], gq_d, key=('c0', 10), writes=['gqk'])
            tr.dma('sp', gqk[hb:hb + 64, 1:2], gk_d, key=('c0', 11), writes=['gqk'])
        gao = sb([64, 8], F32, "gao")
        tr.dma('sp', gao[:], gao_d.rearrange("h e -> e h"), key=('c0', 12), writes=['gao'], allow_slow_non_contiguous=True)
        gx = sb([128, 4], F32, "gx")
        tr.dma('sp', gx[:, 0:2], gxq_d.rearrange("c p -> p c"), key=('c0', 13), writes=['gx'], allow_slow_non_contiguous=True)
        tr.dma('sp', gx[:, 2:4], gxk_d.rearrange("c p -> p c"), key=('c0', 14), writes=['gx'], allow_slow_non_contiguous=True)
        ssq2 = sb([128, 2], F32, "ssq2")
        rstd2 = sb([128, 2], F32, "rstd2")
        xt = sb([128, 2, D], F32, "xt")
        xn2 = sb([128, 2, D], BF16, "xn2")

        phase()
        wst = ph([128, 2, 2048], F32)
        wsb16 = ph([128, 2, 2048], BF16)
        wi = [0]

        pw_steps = []

        def cast_w(nm, src, dst, nk, ncol):
            for k in range(nk):
                for c0 in range(0, ncol, 2048):
                    cn = min(2048, ncol - c0)

                    def step(k=k, c0=c0, cn=cn):
                        s = wi[0] % 2
                        wi[0] += 1
                        tr.dma('sp', wst[:, s, 0:cn], src[k * 128:(k + 1) * 128, c0:c0 + cn], key=('wst', s),
                               writes=[('wst', s)])
                        if wi[0] % 2 == 0:
                            tr.op('dve', 'tensor_copy', out=wsb16[:, s, 0:cn], in_=wst[:, s, 0:cn], reads=[('wst', s)],
                                  writes=[('wsb', s)])
                        else:
                            tr.op('act', 'copy', out=wsb16[:, s, 0:cn], in_=wst[:, s, 0:cn], reads=[('wst', s)],
                                  writes=[('wsb', s)])
                        tr.dma('pool', dst[:, k, c0:c0 + cn], wsb16[:, s, 0:cn], key=('wsbo', s), reads=[('wsb', s)],
                               writes=[nm])
                    pw_steps.append(step)

        for nm, src, dst, nk, ncol in [("WINB", w_in_d, WINB, 8, INC), ("WOUTB", w_out_d, WOUTB, 8, D),
                                       ("WQB", wq_d, WQB, 8, D), ("WKVB", wkv_d, WKVB, 8, 2 * D),
                                       ("WOB", wo_d, WOB, 8, D), ("W1B", w1_d, W1B, 8, 4 * D),
                                       ("W2B", w2_d, W2B, 32, D)]:
            cast_w(nm, src, dst, nk, ncol)

        nslot = [0]

        def rms_rstd(src_ap, n, key_src, sl_, width):
            tr.op('act', 'activation', out=xn2[:, sl_, 0:width], in_=src_ap, func=AF.Square, accum_out=ssq2[:, sl_:sl_ + 1],
                  reads=[key_src], writes=[('xn', sl_), ('ssq', sl_)])
            tr.op('act', 'activation', out=rstd2[:, sl_:sl_ + 1], in_=ssq2[:, sl_:sl_ + 1], func=AF.Ln, scale=1.0 / n,
                  bias=epsT[:], reads=[('ssq', sl_), 'eps'], writes=[('rstd', sl_)])
            tr.op('act', 'activation', out=rstd2[:, sl_:sl_ + 1], in_=rstd2[:, sl_:sl_ + 1], func=AF.Exp, scale=-0.5,
                  reads=[('rstd', sl_)], writes=[('rstd', sl_)])

        def norm_A(src, skey, gi):
            sl_ = nslot[0] % 2
            nslot[0] += 1
            rms_rstd(src, D, skey, sl_, D)
            tr.op('dve', 'scalar_tensor_tensor', out=xn2[:, sl_, :], in0=src, scalar=rstd2[:, sl_:sl_ + 1], in1=gB[:, gi, :],
                  op0=ALU.mult, op1=ALU.mult, reads=[skey, ('rstd', sl_), 'gB'], writes=[('xn', sl_)])
            return sl_

        def norm_B(sl_, dst, dkey, j, tbank):
            pt = pbf(tbank)
            for k in range(8):
                tr.op('pe', 'transpose', out=pt[:, k * 128:(k + 1) * 128], in_=xn2[:, sl_, k * 128:(k + 1) * 128],
                      identity=cb(C_ID), reads=[('xn', sl_), 'CB'], writes=[pk(tbank)])
            tr.op('act', 'copy', out=dst[:, :, j * 128:(j + 1) * 128],
                  in_=pt.rearrange("p (k t) -> p k t", k=8), writes=[pk(tbank), dkey])

        def norm_T(src, skey, gi, dst, dkey, j, tbank):
            sl_ = nslot[0] % 2
            nslot[0] += 1
            rms_rstd(src, D, skey, sl_, D)
            tr.op('dve', 'scalar_tensor_tensor', out=xn2[:, sl_, :], in0=src, scalar=rstd2[:, sl_:sl_ + 1], in1=gB[:, gi, :],
                  op0=ALU.mult, op1=ALU.mult, reads=[skey, ('rstd', sl_), 'gB'], writes=[('xn', sl_)])
            pt = pbf(tbank)
            for k in range(8):
                tr.op('pe', 'transpose', out=pt[:, k * 128:(k + 1) * 128], in_=xn2[:, sl_, k * 128:(k + 1) * 128],
                      identity=cb(C_ID), reads=[('xn', sl_), 'CB'], writes=[pk(tbank)])
            tr.op('act', 'copy', out=dst[:, :, j * 128:(j + 1) * 128],
                  in_=pt.rearrange("p (k t) -> p k t", k=8), writes=[pk(tbank), dkey])

        xnT_st = ph([128, 8, 512], BF16)
        zt = ph([128, 8, 2], BF16)
        tr.op('dve', 'memset', zt, 0.0, writes=['zt'])
        tr.dma('sp', XNT[:, :, 0:2].rearrange("k p t -> p k t"), zt, key='xnt_o', reads=['zt'], writes=['XNT'])
        tr.dma('sp', XNT[:, :, NT + 2:NT + 4].rearrange("k p t -> p k t"), zt, key='xnt_o', reads=['zt'],
               writes=['XNT'])
        for mt in range(NMT):
            for j in range(4):
                t0 = mt * 512 + j * 128
                s = (mt * 4 + j) % 2
                tr.dma('sp', xt[:, s, :], x_d[t0:t0 + 128, :], key=('xt', s), writes=[('xt', s)])
                norm_T(xt[:, s, :], ('xt', s), 0, xnT_st, 'xnT_st', j, j % 2)
                for _ in range(2):
                    if pw_steps:
                        pw_steps.pop(0)()
            tr.dma('sp', XNT[:, :, 2 + mt * 512:2 + mt * 512 + 512].rearrange("k p t -> p k t"), xnT_st,
                   key='xnt_o', reads=['xnT_st'], writes=['XNT'])

        while pw_steps:
            pw_steps.pop(0)()
        phase()
        w_ssd = ph([128, 8, 1552], BF16)
        tr.dma('sp', w_ssd, WINB[:, :, 0:1552], key='w_ssd', writes=['w_ssd'])
        xnTm = ph([128, 2, 8, 516], BF16)
        uT = ph([128, 2, 516], F32)
        cacc = ph([128, 2, 512], F32)
        xsT = ph([128, 2, 4, 512], F32)
        BCT = ph([128, 2, 4, 512], BF16)
        xs = ph([128, 2, 512], F32)
        Btm = ph([128, 2, 256], BF16)
        dtr = ph([128, 2, 16], F32)
        sp1 = ph([128, 2, 16], F32)
        dt = ph([128, 2, 16], F32)
        dtA = ph([128, 2, 16], F32)
        TT = ph([128, 2, 48], F32)
        EE = ph([128, 2, 48], F32)
        dtd = ph([128, 2, 16], F32)
        xdt = ph([128, 2, 2, 512], BF16)
        xdtd = ph([128, 2, 512], BF16)
        R1 = ph([128, 2, 2, 8, 128], F32) if False else ph([128, 4, 8, 128], F32)
        lm = ph([128, 4, 8, 128], F32)
        cbm = ph([128, 4, 2, 128], F32)
        WT = ph([128, 4, 8, 128], BF16)
        sz = ph([128, 2, 512], F32)
        yb = ph([128, 2, 512], F32)
        t2 = ph([128, 2, 512], F32)
        yn = ph([128, 2, 512], BF16)
        ssdT_st = ph([128, 2, 4, 512], BF16)
        stF = ph([128, 512], F32)
        stB = ph([128, 512], F32)
        prevF = ph([128, 2, 512], BF16)
        bstl = ph([128, 2, 512], BF16)
        szst = ph([128, 2, 512], F32)
        Dfull = ph([128, 8, 64], F32)
        tr.op('dve', 'tensor_copy', out=Dfull, in_=Dbc[:].unsqueeze(2).to_broadcast([128, 8, 64]),
              reads=['Dbc'], writes=['Dfull'])
        tr.op('dve', 'memset', stF, 0.0, writes=['stF'])
        tr.op('dve', 'memset', stB, 0.0, writes=['stB'])
        tr.op('dve', 'memset', prevF, 0.0, writes=[('prevF', 0), ('prevF', 1)])
        b8 = lambda ap16: ap16.unsqueeze(2).to_broadcast([128, 8, 64])
        v3 = lambda t: t.rearrange("p (h e) -> p h e", h=8)

        def run_ssd(mode):
            fwd = (mode == 'fwd')
            cached = (mode != 'prep')
            chunks = list(range(NCH)) if mode != 'bwd' else list(range(NCH - 1, -1, -1))
            macros = list(range(NMT)) if mode != 'bwd' else list(range(NMT - 1, -1, -1))
            n = len(chunks)
            di = 0 if fwd else 1
            hs = slice(di * 8, di * 8 + 8)
            st = stF if fwd else stB
            skey = 'stF' if fwd else 'stB'

            def prep_load(mi):
                mt = macros[mi]
                ms = mi % 2
                tr.dma('sp', xnTm[:, ms], XNT[:, :, mt * 512:mt * 512 + 516].rearrange("k p t -> p k t"),
                       key=('xnTm', ms), writes=[('xnTm', ms)])
                if cached:
                    tr.dma('sp', xsT[:, ms], XST_D[mt], key=('xsT_l', ms), reads=[('XSCa', mt)], writes=[('xsT', ms)])

            def prep_load_b(mi):
                mt = macros[mi]
                ms = mi % 2
                tr.dma('sp', BCT[:, ms], BCT_D[mt], key=('BCT_l', ms), reads=[('XSC', mt)], writes=[('BCT', ms)])

            def prep_A(mi, c):
                if cached:
                    return
                mt = macros[mi]
                ms = mi % 2
                xm = xnTm[:, ms]
                bk = 1 + (c % 2)
                col = 512 + c * 128
                for k in range(8):
                    tr.op('pe', 'matmul', PB[bk][:], lhsT=w_ssd[:, k, col:col + 128], rhs=xm[:, k, 2:514],
                          start=(k == 0), stop=(k == 7), reads=['w_ssd', ('xnTm', ms)], writes=[pk(bk)])
                tr.op('act', 'copy', out=uT[:, c % 2, 2:514], in_=PB[bk][:], writes=[pk(bk), ('uT', c % 2)])
                for hh, (a, b) in enumerate([(0, 2), (514, 516)]):
                    for k in range(8):
                        tr.op('pe', 'matmul', PB[3][:, 480 + 2 * hh:482 + 2 * hh], lhsT=w_ssd[:, k, col:col + 128],
                              rhs=xm[:, k, a:b], start=(k == 0), stop=(k == 7), skip_group_check=True,
                              reads=['w_ssd', ('xnTm', ms)], writes=[pk(3)])
                for hh, (a, b) in enumerate([(0, 2), (514, 516)]):
                    atb = (hh == 0 and mt * 512 == SEG) or (hh == 1 and mt * 512 + 512 == SEG)
                    if atb:
                        tr.op('act', 'activation', out=uT[:, c % 2, a:b], in_=PB[3][:, 480 + 2 * hh:482 + 2 * hh],
                              func=AF.Copy, scale=FUSE, reads=['vtab'], writes=[pk(3), ('uT', c % 2)])
                    else:
                        tr.op('act', 'copy', out=uT[:, c % 2, a:b], in_=PB[3][:, 480 + 2 * hh:482 + 2 * hh],
                              writes=[pk(3), ('uT', c % 2)])
                ca = cacc[:, c % 2]
                tr.op('act', 'activation', out=ca, in_=uT[:, c % 2, 0:512], func=AF.Identity, scale=cw[:, c, 0:1],
                      bias=cbias[:, c:c + 1], reads=[('uT', c % 2), 'cw'], writes=[('cacc', c % 2)])

            def prep_B(mi, c):
                mt = macros[mi]
                ms = mi % 2
                xm = xnTm[:, ms]
                ca = cacc[:, c % 2]
                for k in range(1, 5):
                    eng = 'dve'
                    tr.op(eng, 'scalar_tensor_tensor', out=ca, in0=uT[:, c % 2, k:k + 512], scalar=cw[:, c, k:k + 1],
                          in1=ca, op0=ALU.mult, op1=ALU.add, reads=[('uT', c % 2), 'cw'], writes=[('cacc', c % 2)])
                if c < 4:
                    tr.op('act', 'activation', out=xsT[:, ms, c, :], in_=ca, func=AF.Silu, reads=[('cacc', c % 2)],
                          writes=[('xsT', ms)])
                else:
                    tr.op('act', 'activation', out=BCT[:, ms, c - 4, :], in_=ca, func=AF.Silu, reads=[('cacc', c % 2)],
                          writes=[('BCT', ms)])
                if c == 3:
                    tr.dma('pool', XST_D[mt], xsT[:, ms], key=('xst_o', ms), reads=[('xsT', ms)], writes=[('XSCa', mt)])
                if c == 7:
                    tr.dma('pool', BCT_D[mt], BCT[:, ms], key=('bct_o', ms), reads=[('BCT', ms)], writes=[('XSC', mt)])
                    for j in range(4):
                        zs = j % 2
                        for k in range(8):
                            tr.op('pe', 'matmul', PB[7][:], lhsT=xm[:, k, 2 + j * 128:2 + (j + 1) * 128], rhs=w_ssd[:, k, 0:512],
                                  start=(k == 0), stop=(k == 7), reads=['w_ssd', ('xnTm', ms)], writes=[pk(7)])
                        tr.op('act', 'activation', out=szst[:, zs], in_=PB[7][:], func=AF.Silu, writes=[pk(7), ('szst', zs)])
                        tr.dma('pool', SZ_D[mt, :, j, :], szst[:, zs], key=('sz_o', zs), reads=[('szst', zs)],
                               writes=[('SZ', mt, j)])


            def prep_c(mi, c):
                if cached:
                    return
                prep_A(mi, c)
                prep_B(mi, c)

            def S1(p):
                ch = chunks[p]
                mt, j = ch // 4, ch % 4
                ms = (p // 4) % 2
                s = p % 2
                xm = xnTm[:, ms]
                tk = slice(j * 128, (j + 1) * 128)
                for c in range(4):
                    tr.op('pe', 'transpose', out=PB[4][:, c * 128:(c + 1) * 128], in_=xsT[:, ms, c, tk], identity=cf(C_ID),
                          reads=[('xsT', ms), 'CF'], writes=[pk(4)])
                tr.op('act', 'copy', out=xs[:, s], in_=PB[4][:], writes=[pk(4), ('xs', s)])
                pt = pbf(0)
                for g in range(2):
                    tr.op('pe', 'transpose', out=pt[:, g * 128:(g + 1) * 128], in_=BCT[:, ms, g, tk], identity=cb(C_ID),
                          reads=[('BCT', ms), 'CB'], writes=[pk(0)])
                tr.op('act', 'copy', out=Btm[:, s], in_=pt[:, 0:256], writes=[pk(0), ('Btm', s)])
                for k in range(8):
                    tr.op('pe', 'matmul', PB[3][:, 0:16], lhsT=xm[:, k, 2 + j * 128:2 + (j + 1) * 128],
                          rhs=w_ssd[:, k, 1536:1552], start=(k == 0), stop=(k == 7), skip_group_check=True,
                          reads=['w_ssd', ('xnTm', ms)], writes=[pk(3)])
                tr.op('dve', 'tensor_tensor', out=dtr[:, s], in0=PB[3][:, 0:16], in1=dtb[:], op=ALU.add,
                      reads=['dtb'], writes=[pk(3), ('dtr', s)])
                tr.op('act', 'activation', out=sp1[:, s], in_=dtr[:, s], func=AF.Abs, reads=[('dtr', s)],
                      writes=[('sp1', s)])
                tr.op('act', 'activation', out=sp1[:, s], in_=sp1[:, s], func=AF.Exp, scale=-1.0, reads=[('sp1', s)],
                      writes=[('sp1', s)])
                tr.op('act', 'activation', out=sp1[:, s], in_=sp1[:, s], func=AF.Ln, bias=1.0, reads=[('sp1', s)],
                      writes=[('sp1', s)])
                tr.op('dve', 'scalar_tensor_tensor', out=dt[:, s], in0=dtr[:, s], scalar=0.0, in1=sp1[:, s], op0=ALU.max,
                      op1=ALU.add, reads=[('dtr', s), ('sp1', s)], writes=[('dt', s)])
                tr.op('dve', 'tensor_tensor', out=dtA[:, s], in0=dt[:, s], in1=Abc[:], op=ALU.mult,
                      reads=[('dt', s), 'Abc'], writes=[('dtA', s)])
                if fwd:
                    for dd, tri in enumerate([C_U, C_L]):
                        tr.op('pool', 'tensor_tensor', out=R1[:, 2 * s + dd],
                              in0=cf(tri).unsqueeze(1).to_broadcast([128, 8, 128]),
                              in1=dtA[:, s, dd * 8:dd * 8 + 8].unsqueeze(2).to_broadcast([128, 8, 128]), op=ALU.mult,
                              reads=['CF', ('dtA', s)], writes=[('R1', s, dd)])
                    tr.dma('sp', sz[:, s], SZ_D[mt, :, j, :], key=('sz_l', s), reads=[('SZ', mt, j)], writes=[('sz', s)])

            def S2(p):
                ch = chunks[p]
                mt, j = ch // 4, ch % 4
                ms = (p // 4) % 2
                s = p % 2
                tk = slice(j * 128, (j + 1) * 128)
                tr.op('pe', 'matmul', PB[3][:, 16:24], lhsT=cf(C_U), rhs=dtA[:, s, 0:8], start=True, stop=True,
                      skip_group_check=True, reads=['CF', ('dtA', s)], writes=[pk(3)])
                tr.op('pe', 'matmul', PB[3][:, 24:32], lhsT=cf(C_L), rhs=dtA[:, s, 8:16], start=False, stop=True,
                      skip_group_check=True, reads=['CF', ('dtA', s)], writes=[pk(3)])
                tr.op('pe', 'matmul', PB[3][:, 32:48], lhsT=cf(C_ONES), rhs=dtA[:, s, 0:16], start=False, stop=True,
                      skip_group_check=True, reads=['CF', ('dtA', s)], writes=[pk(3)])
                tr.op('act', 'copy', out=TT[:, s, 0:32], in_=PB[3][:, 16:48], writes=[pk(3), ('TT', s)])
                tr.op('dve', 'tensor_tensor', out=TT[:, s, 32:48], in0=TT[:, s, 16:32], in1=TT[:, s, 0:16], op=ALU.subtract,
                      reads=[('TT', s)], writes=[('TT', s)])
                tr.op('act', 'activation', out=EE[:, s], in_=TT[:, s], func=AF.Exp, reads=[('TT', s)], writes=[('EE', s)])
                tr.op('dve', 'tensor_tensor', out=dtd[:, s], in0=dt[:, s], in1=EE[:, s, 32:48], op=ALU.mult,
                      reads=[('dt', s), ('EE', s)], writes=[('dtd', s)])
                tr.op('dve', 'tensor_tensor', out=v3(xdtd[:, s]), in0=v3(xs[:, s]), in1=b8(dtd[:, s, hs]), op=ALU.mult,
                      reads=[('xs', s), ('dtd', s)], writes=[('xdtd', s)])
                if fwd:
                    tr.dma('sp', bstl[:, s, :], BSTD[ch], key=('bstl', s), reads=[('BSTD', ch)], writes=[('bstl', s)])
                    for dd in range(2):
                        tr.op('dve', 'tensor_tensor', out=v3(xdt[:, s, dd]), in0=v3(xs[:, s]),
                              in1=b8(dt[:, s, dd * 8:dd * 8 + 8]), op=ALU.mult, reads=[('xs', s), ('dt', s)],
                              writes=[('xdt', s)])
                    for dd, lt in enumerate([C_LS, C_US]):
                        for hf in range(2):
                            bk = 5 + hf
                            tr.op('pe', 'matmul', PB[bk][:], lhsT=cf(lt),
                                  rhs=R1[:, 2 * s + dd, hf * 4:hf * 4 + 4, :].rearrange("p h i -> p (h i)"), start=True,
                                  stop=True, reads=['CF', ('R1', s, dd)], writes=[pk(bk)])
                            tr.op('act', 'activation',
                                  out=lm[:, 2 * s + dd, hf * 4:hf * 4 + 4, :].rearrange("p h i -> p (h i)"),
                                  in_=PB[bk][:], func=AF.Exp, writes=[pk(bk), ('lm', s, dd)])
                    for g in range(2):
                        tr.op('pe', 'matmul', PB[7][:, g * 128:(g + 1) * 128], lhsT=BCT[:, ms, g, tk], rhs=BCT[:, ms, 2 + g, tk],
                              start=(g == 0), stop=True, skip_group_check=True, reads=[('BCT', ms)], writes=[pk(7)])
                    for dd, tri in enumerate([C_U, C_L]):
                        tr.op('dve', 'tensor_tensor', out=cbm[:, 2 * s + dd],
                              in0=PB[7][:, 0:256].rearrange("p (g i) -> p g i", g=2),
                              in1=cf(tri).unsqueeze(1).to_broadcast([128, 2, 128]), op=ALU.mult,
                              reads=['CF'], writes=[pk(7), ('cbm', s, dd)])
                        tr.op('dve', 'tensor_tensor', out=WT[:, 2 * s + dd].rearrange("p (g r) i -> p g r i", g=2),
                              in0=lm[:, 2 * s + dd].rearrange("p (g r) i -> p g r i", g=2),
                              in1=cbm[:, 2 * s + dd].unsqueeze(2).to_broadcast([128, 2, 4, 128]), op=ALU.mult,
                              reads=[('lm', s, dd), ('cbm', s, dd)], writes=[('WT', s, dd)])
                if not fwd:
                    tr.op('act', 'copy', out=bstl[:, s, :], in_=stB, reads=['stB'], writes=[('bstl', s)])
                    tr.dma('sp', BSTD[ch], bstl[:, s, :], key=('bsto', s), reads=[('bstl', s)], writes=[('BSTD', ch)])
                for g in range(2):
                    tr.op('pe', 'matmul', PB[1][:, g * 256:(g + 1) * 256], lhsT=Btm[:, s, g * 128:(g + 1) * 128],
                          rhs=xdtd[:, s, g * 256:(g + 1) * 256], start=(g == 0), stop=True, skip_group_check=True,
                          reads=[('Btm', s), ('xdtd', s)], writes=[pk(1)])
                tr.op('dve', 'tensor_tensor', out=v3(st), in0=v3(st), in1=b8(EE[:, s, 16 + di * 8:24 + di * 8]),
                      op=ALU.mult, reads=[('EE', s)], writes=[skey])
                tr.op('dve', 'tensor_tensor', out=st, in0=st, in1=PB[1][:], op=ALU.add, writes=[pk(1), skey])
                if (fwd and (ch + 1) * 128 == SEG) or ((not fwd) and ch * 128 == SEG):
                    tr.op('dve', 'tensor_scalar', out=st, in0=st, scalar1=FUSE, scalar2=None, op0=ALU.mult,
                          reads=['vtab'], writes=[skey])
                if fwd:
                    tr.op('dve', 'tensor_copy', out=prevF[:, (p + 1) % 2], in_=stF, reads=['stF'], writes=[('prevF', (p + 1) % 2)])

            def S3(p):
                ch = chunks[p]
                mt, j = ch // 4, ch % 4
                ms = (p // 4) % 2
                s = p % 2
                tk = slice(j * 128, (j + 1) * 128)
                first = True
                for dd in range(2):
                    for h in range(8):
                        tr.op('pe', 'matmul', PB[4][:, h * 64:(h + 1) * 64], lhsT=WT[:, 2 * s + dd, h, :],
                              rhs=xdt[:, s, dd, h * 64:(h + 1) * 64], start=first, stop=True, skip_group_check=True,
                              reads=[('WT', s, dd), ('xdt', s)], writes=[pk(4)])
                        first = False
                for g in range(2):
                    tr.op('pe', 'matmul', PB[5][:, g * 256:(g + 1) * 256], lhsT=BCT[:, ms, 2 + g, tk],
                          rhs=prevF[:, s, g * 256:(g + 1) * 256], start=(g == 0), stop=True, skip_group_check=True,
                          reads=[('BCT', ms), ('prevF', s)], writes=[pk(5)])
                    tr.op('pe', 'matmul', PB[6][:, g * 256:(g + 1) * 256], lhsT=BCT[:, ms, 2 + g, tk],
                          rhs=bstl[:, s, g * 256:(g + 1) * 256], start=(g == 0), stop=True, skip_group_check=True,
                          reads=[('BCT', ms), ('bstl', s)], writes=[pk(6)])
                y_ = yb[:, s]
                t_ = t2[:, s]
                tr.op('dve', 'tensor_tensor', out=v3(y_), in0=v3(PB[5][:]), in1=b8(EE[:, s, 0:8]), op=ALU.mult,
                      reads=[('EE', s)], writes=[pk(5), ('yb', s)])
                tr.op('dve', 'tensor_tensor', out=v3(t_), in0=v3(PB[6][:]), in1=b8(EE[:, s, 8:16]), op=ALU.mult,
                      reads=[('EE', s)], writes=[pk(6), ('t2', s)])
                tr.op('pool', 'tensor_tensor', out=y_, in0=y_, in1=t_, op=ALU.add, reads=[('t2', s)], writes=[('yb', s)])
                tr.op('dve', 'tensor_tensor', out=y_, in0=y_, in1=PB[4][:], op=ALU.add, writes=[pk(4), ('yb', s)])
                tr.op('pool', 'tensor_tensor', out=t_, in0=xs[:, s], in1=Dfull.rearrange("p h e -> p (h e)"),
                      op=ALU.mult, reads=[('xs', s), 'Dfull'], writes=[('t2', s)])
                tr.op('pool', 'tensor_tensor', out=y_, in0=y_, in1=t_, op=ALU.add, reads=[('t2', s)], writes=[('yb', s)])
                tr.op('pool', 'tensor_tensor', out=y_, in0=y_, in1=sz[:, s], op=ALU.mult, reads=[('sz', s)],
                      writes=[('yb', s)])

            def S3b(p):
                s = p % 2
                y_ = yb[:, s]
                rms_rstd(y_, 512, ('yb', s), s, 512)
                tr.op('dve', 'scalar_tensor_tensor', out=yn[:, s], in0=y_, scalar=rstd2[:, s:s + 1], in1=gssd[:],
                      op0=ALU.mult, op1=ALU.mult, reads=[('yb', s), ('rstd', s), 'gssd'], writes=[('yn', s)])

            def S4(p):
                ch = chunks[p]
                mt, j = ch // 4, ch % 4
                mo = mt % 2
                s = p % 2
                tk = slice(j * 128, (j + 1) * 128)
                pt = pbf(2)
                for c in range(4):
                    tr.op('pe', 'transpose', out=pt[:, c * 128:(c + 1) * 128], in_=yn[:, s, c * 128:(c + 1) * 128],
                          identity=cb(C_ID), reads=[('yn', s), 'CB'], writes=[pk(2)])
                tr.op('act', 'copy', out=ssdT_st[:, mo, :, tk], in_=pt[:, 0:512].rearrange("p (c t) -> p c t", c=4),
                      writes=[pk(2), ('ssdT_st', mo)])
                if j == 3:
                    tr.dma('sp', SSDT[:, :, mt * 512:(mt + 1) * 512].rearrange("c p t -> p c t"), ssdT_st[:, mo],
                           key=('ssdt_o', mo), reads=[('ssdT_st', mo)], writes=['SSDT'])

            if mode == 'prep':
                prep_load(0)
                for mi in range(len(macros)):
                    if mi + 1 < len(macros):
                        prep_load(mi + 1)
                    prep_A(mi, 0)
                    for c in range(8):
                        if c + 1 < 8:
                            prep_A(mi, c + 1)
                        prep_B(mi, c)
                return
            for t in range(-7, n + 3):
                if fwd:
                    if 0 <= t - 2 < n:
                        S4(t - 2)
                    if 0 <= t - 1 < n:
                        S3b(t - 1)
                    if 0 <= t < n:
                        S3(t)
                if 0 <= t + 1 < n:
                    S2(t + 1)
                if 0 <= t + 2 < n:
                    S1(t + 2)
                for mi in range(len(macros)):
                    o = t - (4 * mi - 7)
                    if o == 0:
                        prep_load(mi)
                    elif 1 <= o <= 4:
                        prep_c(mi, 2 * (o - 1))
                        prep_c(mi, 2 * (o - 1) + 1)
                        if cached and o == 3:
                            prep_load_b(mi)

        run_ssd('prep')
        run_ssd('bwd')
        run_ssd('fwd')

        if stop_after not in ('ssd',):
            phase()
            w_v = ph([128, 8, 512], BF16)
            tr.dma('sp', w_v, WINB[:, :, 2576:3088], key='w_v', writes=['w_v'])
            xmv = ph([128, 2, 8, 512], BF16)
            vst = ph([128, 2, 4, 520], BF16)
            zrow = ph([128, 520], BF16)
            tr.op('dve', 'memset', zrow, 0.0, writes=['zrow'])
            tr.op('dve', 'memset', vst, 1.0, writes=[('vst', 0), ('vst', 1)])
            for i in range(PADV // 128):
                tr.dma('pool', VS[i * 128:(i + 1) * 128, :], zrow, key='vs_z', reads=['zrow'], writes=['VS'])
                tr.dma('pool', VS[PADV + NT + i * 128:PADV + NT + (i + 1) * 128, :], zrow, key='vs_z', reads=['zrow'],
                       writes=['VS'])
            for mt in range(NMT):
                s = mt % 2
                tr.dma('sp', xmv[:, s], XNT[:, :, 2 + mt * 512:2 + mt * 512 + 512].rearrange("k p t -> p k t"),
                       key=('xmv', s), writes=[('xmv', s)])
                for j in range(4):
                    bk = j % 2
                    for k in range(8):
                        tr.op('pe', 'matmul', PB[bk][:], lhsT=xmv[:, s, k, j * 128:(j + 1) * 128], rhs=w_v[:, k, :],
                              start=(k == 0), stop=(k == 7), reads=['w_v', ('xmv', s)], writes=[pk(bk)])
                    tr.op('act', 'copy', out=vst[:, s, j, :].rearrange("p (h e) -> p h e", h=8)[:, :, 0:64],
                          in_=PB[bk][:].rearrange("p (h e) -> p h e", h=8), writes=[pk(bk), ('vst', s)])
                tr.dma('pool', VS[PADV + mt * 512:PADV + (mt + 1) * 512, :].rearrange("(j p) c -> p j c", p=128),
                       vst[:, s], key=('vs_o', s), reads=[('vst', s)], writes=['VS'])

        if stop_after not in ('ssd', 'p3v'):
            phase()
            w_qk = ph([128, 8, 1024], BF16)
            tr.dma('sp', w_qk, WINB[:, :, 1552:2576], key='w_qk', writes=['w_qk'])
            kT = ph([128, 4, 4096], BF16)
            qT = ph([128, 4, 2048], BF16)
            xma = ph([128, 2, 8, 512], BF16)
            cst = ph([128, 2, 2, 512], F32)
            NRS = 2
            sqb = ph([128, NRS, 512], BF16)
            rsn = ph([128, 2, 512], F32)
            qn = ph([128, 2, 512], F32)
            qnb = ph([128, NRS, 512], BF16)
            tA = ph([128, 2, 512], F32)
            tB = ph([128, 2, 512], F32)
            Vt2 = ph([128, 69, 130], BF16)
            NS = 4
            LAG = 3
            PT = ph([128, NS, 256], BF16)
            accs = ph([65, 2, 2048], F32)
            rden = ph([64, 2, 512], F32)
            sqa = ph([64, 2, 512], BF16)
            ssacc = ph([64, 2048], F32)
            attl = rden
            attT_st = ph([64, 8, 512], BF16)
            tr.op('dve', 'memset', kT, 0.0, writes=['kT'])
            MLU = ph([128, 256], BF16)
            tr.op('dve', 'tensor_copy', out=MLU[:, 0:128], in_=cf(C_L), reads=['CF'], writes=['MLU'])
            tr.op('dve', 'tensor_copy', out=MLU[:, 128:256], in_=cf(C_U), reads=['CF'], writes=['MLU'])
            VCOL = {(1, 1): 0, ('f', 'f'): 1, (0, 0): 2, (1, 'f'): 3, ('f', 1): 4, (1, 0): 5, (0, 1): 6, ('f', 0): 7,
                    (0, 'f'): 8}
            NRS = 2

            def nr_stages(u, unit):
                col, gcol, dst, dkey, xs_ = unit
                s = u % NRS
                s2_ = u % 2
                pbank = u % 4
                b2 = 4 + (u % 2)
                b3 = 6 + (u % 2)

                def s1():
                    for k in range(8):
                        tr.op('pe', 'matmul', PB[pbank][:], lhsT=w_qk[:, k, col:col + 128], rhs=xma[:, xs_, k, :],
                              start=(k == 0), stop=(k == 7), reads=['w_qk', ('xma', xs_)], writes=[pk(pbank)])

                def s2():
                    tr.op('act', 'activation', out=sqb[:, s], in_=PB[pbank][:], func=AF.Square, reads=[pk(pbank)],
                          writes=[('sqb', s)])
                    tr.op('pe', 'matmul', PB[b2][:], lhsT=cb(C_BLK), rhs=sqb[:, s], start=True, stop=True,
                          reads=['CB', ('sqb', s)], writes=[pk(b2)])

                def s3():
                    tr.op('act', 'activation', out=rsn[:, s2_], in_=PB[b2][:], func=AF.Ln, scale=1.0 / 64, bias=epsT[:],
                          reads=['eps'], writes=[pk(b2), ('rsn', s2_)])
                    tr.op('act', 'activation', out=rsn[:, s2_], in_=rsn[:, s2_], func=AF.Exp, scale=-0.5, reads=[('rsn', s2_)], writes=[('rsn', s2_)])
                    tr.op('dve', 'scalar_tensor_tensor', out=qn[:, s2_], in0=PB[pbank][:], scalar=gqk[:, gcol:gcol + 1],
                          in1=rsn[:, s2_], op0=ALU.mult, op1=ALU.mult, reads=['gqk', ('rsn', s2_)],
                          writes=[pk(pbank), ('qn', s2_)])
                    tr.op('act', 'copy', out=qnb[:, s], in_=qn[:, s2_], reads=[('qn', s2_)], writes=[('qnb', s)])
                    tr.op('pe', 'matmul', PB[b3][:], lhsT=cb(C_P), rhs=qnb[:, s], start=True, stop=True,
                          reads=['CB', ('qnb', s)], writes=[pk(b3)])
                    tr.op('pool', 'tensor_tensor', out=tB[:, s2_], in0=qn[:, s2_], in1=cst[:, xs_, 0, :], op=ALU.mult,
                          reads=[('qn', s2_), ('cst', xs_)], writes=[('tB', s2_)])

                def s4():
                    tr.op('dve', 'tensor_tensor', out=tA[:, s2_], in0=PB[b3][:], in1=cst[:, xs_, 1, :], op=ALU.mult,
                          reads=[('cst', xs_)], writes=[pk(b3), ('tA', s2_)])
                    tr.op('dve', 'tensor_tensor', out=dst, in0=tA[:, s2_], in1=tB[:, s2_], op=ALU.add,
                          reads=[('tA', s2_), ('tB', s2_)], writes=[dkey])
                return [s1, s2, s3, s4]

            def run_units(units, hooks):
                st = [nr_stages(u, un) for u, un in enumerate(units)]
                n = len(st)
                for t in range(n + 3):
                    for fn in hooks.get(t, []):
                        fn()
                    for k in range(4):
                        if 0 <= t - k < n:
                            st[t - k][k]()

            slot_ctr = [0]
            pending = []

            def run_pending(nmax):
                k = 0
                while pending and k < nmax:
                    fn = pending.pop(0)
                    sl = slot_ctr[0] % NS
                    fn(sl)
                    k += 1

            def make_post(h, a0):
                ap_ = h % 2
                steps = []
                for b_ in range(4):
                    qs = slice(b_ * 512, (b_ + 1) * 512)

                    def st1(sl, b_=b_, qs=qs):
                        r_ = b_ % 2
                        tr.op('pe', 'matmul', PB[sl][0:64, :], lhsT=CF[0:65, C_SEL, 0:64], rhs=accs[:, ap_, qs], start=True,
                              stop=True, reads=['CF', ('accs', ap_)], writes=[pk(sl)])
                        tr.op('act', 'activation', out=rden[:, r_], in_=PB[sl][0:64, :], func=AF.Ln, writes=[pk(sl), ('rden', r_)])
                        tr.op('act', 'activation', out=rden[:, r_], in_=rden[:, r_], func=AF.Exp, scale=-1.0, reads=[('rden', r_)], writes=[('rden', r_)])
                        tr.op('dve', 'tensor_tensor', out=accs[0:64, ap_, qs], in0=accs[0:64, ap_, qs], in1=rden[:, r_],
                              op=ALU.mult, reads=[('rden', r_)], writes=[('accs', ap_)])
                        tr.op('act', 'activation', out=sqa[:, r_], in_=accs[0:64, ap_, qs], func=AF.Square,
                              reads=[('accs', ap_)], writes=[('sqa', r_)])

                    def st2(sl, b_=b_, qs=qs):
                        r_ = b_ % 2
                        tr.op('pe', 'matmul', PB[sl][0:64, :], lhsT=CB[0:64, C_ONES, 0:64], rhs=sqa[:, r_], start=True,
                              stop=True, reads=['CB', ('sqa', r_)], writes=[pk(sl)])
                        if h == 0:
                            tr.op('act', 'copy', out=ssacc[:, qs], in_=PB[sl][0:64, :], writes=[pk(sl), 'ssacc'])
                        else:
                            tr.op('dve', 'tensor_tensor', out=ssacc[:, qs], in0=ssacc[:, qs], in1=PB[sl][0:64, :],
                                  op=ALU.add, writes=[pk(sl), 'ssacc'])
                        if b_ == 3:
                            tr.dma('sp', ATTF[h, :, a0:a0 + AM], accs[0:64, ap_, :], key=('attf_o', ap_), reads=[('accs', ap_)],
                                   writes=[('ATTF', h)])
                    steps.append(st1)
                    steps.append(st2)
                return steps

            for am in range(NAM):
                a0 = am * AM
                sega = a0 // SEG
                valid = [sbk for sbk in range(8) if 0 <= a0 - 1024 + sbk * 512 < NT]
                units = []
                hooks = {}

                def mk_load(i):
                    sbk = valid[i]
                    tok0 = a0 - 1024 + sbk * 512
                    xs_ = i % 2

                    def fn():
                        tr.dma('sp', xma[:, xs_], XNT[:, :, 2 + tok0:2 + tok0 + 512].rearrange("k p t -> p k t"),
                               key=('xma', xs_), writes=[('xma', xs_)])
                        tr.dma('sp', cst[:, xs_], cs_d[:, :, tok0:tok0 + 512].rearrange("a p t -> p a t"),
                               key=('cst', xs_), writes=[('cst', xs_)])
                    return fn

                for i, sbk in enumerate(valid):
                    f_i = len(units)
                    if i == 0:
                        hooks.setdefault(0, []).append(mk_load(0))
                    if i + 1 < len(valid):
                        hooks.setdefault(f_i + (4 if i > 0 else 0), []).append(mk_load(i + 1))
                    for c in range(4):
                        units.append((512 + c * 128, 1, kT[:, c, sbk * 512:(sbk + 1) * 512], 'kT', i % 2))
                        if 2 <= sbk < 6:
                            units.append((c * 128, 0, qT[:, c, (sbk - 2) * 512:(sbk - 1) * 512], 'qT', i % 2))
                run_units(units, hooks)
                for hp in range(4):
                    tix = {}
                    ti = 0
                    for d in (1, 4, 16):
                        nu = AM // (128 * d)
                        for r in range(d):
                            for u in range(nu + 1):
                                tb = a0 - 64 * d + 128 * d * u
                                st_ = []
                                for hb_ in (tb, tb + 64 * d):
                                    if hb_ < 0 or hb_ >= NT:
                                        st_.append(0)
                                    elif hb_ // SEG == sega:
                                        st_.append(1)
                                    else:
                                        st_.append('f')
                                st_ = tuple(st_)
                                if st_ == (0, 0):
                                    continue
                                row0 = PADV + tb + r
                                tr.dma('pool', Vt2[:, ti, :], VS[row0:row0 + 127 * d + 1:d, hp * 130:(hp + 1) * 130],
                                       key=('vt', ti // 9), reads=['VS'], writes=[('Vt2', ti)])
                                tix[(d, r, u)] = (ti, VCOL[st_])
                                ti += 1
                    for g_ in range((ti + 8) // 9):
                        tr.batch_end(('vt', g_), [('Vt2', t_) for t_ in range(g_ * 9, min(ti, g_ * 9 + 9))])
                    for hh in range(2):
                        h = hp * 2 + hh
                        if os.environ.get('PHASE_DBG') and am == 1:
                            print('head start', h, tr.cnt['pe'])
                        c = hp
                        rb = hh * 64
                        apar = h % 2
                        first_in_bank = [True] * 4
                        items = []
                        for d in (1, 4, 16):
                            nu = AM // (128 * d)
                            for r in range(d):
                                for ub in range(nu):
                                    blocks = []
                                    for bi, u in enumerate((ub, ub + 1)):
                                        if (d, r, u) in tix:
                                            ti_, vc = tix[(d, r, u)]
                                            blocks.append((bi, ti_, vc, 1024 - 64 * d + 128 * d * u + r))
                                    items.append((d, r, ub, 128 * d * ub + r, blocks))
                        slots = {}

                        def emit_S(i):
                            d, r, ub, qc0, blocks = items[i]
                            sl = slot_ctr[0] % NS
                            slot_ctr[0] += 1
                            slots[i] = sl
                            for (bi, ti_, vc, kc0) in blocks:
                                tr.op('pe', 'matmul', PB[sl][:, bi * 128:(bi + 1) * 128],
                                      lhsT=kT[rb:rb + 64, c, kc0:kc0 + 127 * d + 1:d],
                                      rhs=qT[rb:rb + 64, c, qc0:qc0 + 127 * d + 1:d], start=True, stop=True,
                                      skip_group_check=True, reads=['kT', 'qT'], writes=[pk(sl)])
                            lo_ = min(b_[0] for b_ in blocks)
                            hi_ = max(b_[0] for b_ in blocks) + 1
                            if len(blocks) == 2 and blocks[0][2] == blocks[1][2]:
                                vc = blocks[0][2]
                                tr.op('act', 'activation', out=PT[:, sl, :], in_=PB[sl][:, 0:256], func=AF.Exp,
                                      scale=0.125, bias=vtab[:, 9 + vc:10 + vc], reads=['vtab'],
                                      writes=[pk(sl), ('PT', sl)])
                            else:
                                for (bi, ti_, vc, kc0) in blocks:
                                    tr.op('act', 'activation', out=PT[:, sl, bi * 128:(bi + 1) * 128],
                                          in_=PB[sl][:, bi * 128:(bi + 1) * 128], func=AF.Exp, scale=0.125,
                                          bias=vtab[:, 9 + vc:10 + vc], reads=['vtab'], writes=[pk(sl), ('PT', sl)])
                            tr.op('dve', 'tensor_tensor', out=PT[:, sl, lo_ * 128:hi_ * 128],
                                  in0=PT[:, sl, lo_ * 128:hi_ * 128],
                                  in1=CB[:, C_L + lo_:C_L + hi_, :].rearrange("p a b -> p (a b)") if lo_ == 0 and hi_ == 1 else
                                  (MLU[:, lo_ * 128:hi_ * 128]), op=ALU.mult, reads=['MLU'], writes=[('PT', sl)])

                        def emit_PV(i):
                            d, r, ub, qc0, blocks = items[i]
                            sl = slots[i]
                            for (bi, ti_, vc, kc0) in blocks:
                                vl = Vt2[:, ti_, hh * 65:hh * 65 + 65]
                                if d == 16:
                                    pieces = [(b_, PB[4 + b_][0:65, r:512:16],
                                               PT[:, sl, bi * 128 + 32 * b_:bi * 128 + 32 * b_ + 32]) for b_ in range(4)]
                                elif d == 4:
                                    pieces = [(ub, PB[4 + ub][0:65, r:512:4], PT[:, sl, bi * 128:(bi + 1) * 128])]
                                else:
                                    b_ = qc0 // 512
                                    pieces = [(b_, PB[4 + b_][0:65, qc0 % 512:qc0 % 512 + 128],
                                               PT[:, sl, bi * 128:(bi + 1) * 128])]
                                for (b_, oap, rap) in pieces:
                                    tr.op('pe', 'matmul', oap, lhsT=vl, rhs=rap, start=first_in_bank[b_], stop=True,
                                          skip_group_check=True, reads=[('PT', sl), ('Vt2', ti_)], writes=[pk(4 + b_)])
                                    first_in_bank[b_] = False

                        n_it = len(items)
                        for i in range(n_it + LAG):
                            if i < n_it:
                                emit_S(i)
                            if i % 5 == 4:
                                run_pending(1)
                            if i >= LAG:
                                emit_PV(i - LAG)
                        run_pending(100)
                        for b_ in range(4):
                            tr.op('act', 'copy', out=accs[:, apar, b_ * 512:(b_ + 1) * 512], in_=PB[4 + b_][0:65, :],
                                  writes=[pk(4 + b_), ('accs', apar)])
                        pending.extend(make_post(h, a0))
                run_pending(100)
                tr.op('act', 'activation', out=ssacc, in_=ssacc, func=AF.Ln, scale=1.0 / 512, bias=epsT[0:64, :],
                      reads=['ssacc', 'eps'], writes=['ssacc'])
                tr.op('act', 'activation', out=ssacc, in_=ssacc, func=AF.Exp, scale=-0.5, reads=['ssacc'], writes=['ssacc'])
                for qb in range(4):
                    qs = slice(qb * 512, (qb + 1) * 512)
                    for h in range(8):
                        sl = h % 2
                        tr.dma('pool', attl[:, sl, :], ATTF[h, :, a0 + qb * 512:a0 + (qb + 1) * 512], key=('attl', sl),
                               reads=[('ATTF', h)], writes=[('rden', sl)])
                        tr.op('dve', 'scalar_tensor_tensor', out=attT_st[:, h, :], in0=attl[:, sl, :], scalar=gao[:, h:h + 1],
                              in1=ssacc[:, qs], op0=ALU.mult, op1=ALU.mult, reads=[('rden', sl), 'gao', 'ssacc'],
                              writes=['attT_st'])
                    tr.dma('pool', ATTT.rearrange("c (two e) t -> e (c two) t", two=2)[:, :, a0 + qb * 512:a0 + (qb + 1) * 512],
                           attT_st, key='attt_o', reads=['attT_st'], writes=['ATTT'])
        if stop_after not in ('ssd', 'p3v', 'p3a'):
            phase()
            WA = ph([128, 1, 8, 1024], BF16)
            w1p = ph([128, 3, 8, 512], BF16)
            w2p = ph([128, 3, 4, 1024], BF16)
            hTp = ph([128, 2, 4, 512], BF16)
            hbuf = ph([128, 2, 4, 1024], F32)
            mixT = ph([128, 8, 512], BF16)
            hnT = ph([128, 2, 8, 512], BF16)
            kmT = ph([128, 2, 8, 256], BF16)
            Vm = ph([128, 2, 2, 1024], BF16)
            sqx = ph([128, 2, 512], BF16)
            rsx = ph([128, 512], F32)
            qx = ph([128, 2, 512], BF16)
            PTx = ph([128, 2, 512], BF16)
            rdx = ph([128, 512], F32)
            oT = ph([128, 8, 512], BF16)
            rl = ph([128, 2, 512], F32)
            memT = rl.bitcast(BF16).rearrange("p a (b c) -> p (a b) c", b=4)
            wa_i = [0]

            def load_wa(src_ap):
                s = 0
                wa_i[0] += 1
                tr.dma('sp', WA[:, s], src_ap, key=('WA', s), writes=[('WA', s)])
                return s

            for sg in range(2):
                for j in range(2):
                    s = j % 2
                    tr.dma('sp', xt[:, s, :], mem_d[sg, j * 128:(j + 1) * 128, :], key=('xt', s), writes=[('xt', s)])
                    norm_T(xt[:, s, :], ('xt', s), 2, memT, 'memT', j, 0)
                sk = load_wa(WKVB[:, :, 0:1024])
                for hh in range(4):
                    for cc in range(2):
                        c = 2 * hh + cc
                        for k in range(8):
                            tr.op('pe', 'matmul', PB[cc][:, 0:256], lhsT=WA[:, sk, k, c * 128:(c + 1) * 128], rhs=memT[:, k, :],
                                  start=(k == 0), stop=(k == 7), reads=[('WA', sk), 'memT'], writes=[pk(cc)])
                        tr.op('act', 'activation', out=sqx[:, cc, 0:256], in_=PB[cc][:, 0:256], func=AF.Square,
                              reads=[pk(cc)], writes=[('sqx', cc)])
                    for cc in range(2):
                        tr.op('pe', 'matmul', PB[2][:, 0:256], lhsT=cb(C_ONES), rhs=sqx[:, cc, 0:256], start=(cc == 0),
                              stop=(cc == 1), reads=['CB', ('sqx', cc)], writes=[pk(2)])
                    tr.op('act', 'activation', out=rsx[:, 0:256], in_=PB[2][:, 0:256], func=AF.Ln, scale=1.0 / 256,
                          bias=epsT[:], reads=['eps'], writes=[pk(2), 'rsx'])
                    tr.op('act', 'activation', out=rsx[:, 0:256], in_=rsx[:, 0:256], func=AF.Exp, scale=-0.5, reads=['rsx'], writes=['rsx'])
                    for cc in range(2):
                        tr.op('dve', 'scalar_tensor_tensor', out=kmT[:, sg, 2 * hh + cc, :], in0=PB[cc][:, 0:256],
                              scalar=gx[:, 2 + cc:3 + cc], in1=rsx[:, 0:256], op0=ALU.mult, op1=ALU.mult,
                              reads=['gx', 'rsx'], writes=[pk(cc), 'kmT'])
                sv = load_wa(WKVB[:, :, 1024:2048])
                for mtl in range(2):
                    for half in range(2):
                        bk = half
                        for k in range(8):
                            tr.op('pe', 'matmul', PB[bk][:], lhsT=memT[:, k, mtl * 128:(mtl + 1) * 128],
                                  rhs=WA[:, sv, k, half * 512:(half + 1) * 512], start=(k == 0), stop=(k == 7),
                                  reads=[('WA', sv), 'memT'], writes=[pk(bk)])
                        tr.op('act', 'copy', out=Vm[:, sg, mtl, half * 512:(half + 1) * 512], in_=PB[bk][:],
                              writes=[pk(bk), 'Vm'])
            tr.barrier()

            def F_gen(mt):
                t0 = mt * 512
                sg = t0 // SEG
                par = mt % 2
                hb = hbuf[:, par]
                hn = hnT[:, par]
                hkey = lambda j: ('hbuf', par, j)
                nkey = ('hnT', par)
                tr.dma('sp', mixT[:, 0:4, :], SSDT[:, :, t0:t0 + 512].rearrange("c p t -> p c t"), key='mixT',
                       writes=['mixT'])
                tr.dma('sp', mixT[:, 4:8, :], ATTT[:, :, t0:t0 + 512].rearrange("c p t -> p c t"), key='mixT',
                       writes=['mixT'])
                s_ = load_wa(WOUTB)
                yield
                for j in range(4):
                    xs_ = j % 2
                    tr.dma('sp', xt[:, xs_, :], x_d[t0 + j * 128:t0 + (j + 1) * 128, :], key=('xt', xs_), writes=[('xt', xs_)])
                    for half in range(2):
                        bk = half
                        for kc in range(8):
                            tr.op('pe', 'matmul', PB[bk][:], lhsT=mixT[:, kc, j * 128:(j + 1) * 128],
                                  rhs=WA[:, s_, kc, half * 512:(half + 1) * 512], start=(kc == 0), stop=(kc == 7),
                                  reads=[('WA', s_), 'mixT'], writes=[pk(bk)])
                        tr.op('dve', 'tensor_tensor', out=hb[:, j, half * 512:(half + 1) * 512], in0=PB[bk][:],
                              in1=xt[:, xs_, half * 512:(half + 1) * 512], op=ALU.add, reads=[('xt', xs_)],
                              writes=[pk(bk), hkey(j)])
                sl0 = norm_A(hb[:, 0, :], hkey(0), 1)
                yield
                sl1 = norm_A(hb[:, 1, :], hkey(1), 1)
                yield
                norm_B(sl0, hn, nkey, 0, 0)
                sl2 = norm_A(hb[:, 2, :], hkey(2), 1)
                yield
                norm_B(sl1, hn, nkey, 1, 1)
                sl3 = norm_A(hb[:, 3, :], hkey(3), 1)
                yield
                norm_B(sl2, hn, nkey, 2, 0)
                yield
                norm_B(sl3, hn, nkey, 3, 1)
                yield
                s_ = load_wa(WQB)
                for hh in range(4):
                    for cc in range(2):
                        c = 2 * hh + cc
                        for k in range(8):
                            tr.op('pe', 'matmul', PB[cc][:], lhsT=WA[:, s_, k, c * 128:(c + 1) * 128], rhs=hn[:, k, :],
                                  start=(k == 0), stop=(k == 7), reads=[('WA', s_), nkey], writes=[pk(cc)])
                        tr.op('act', 'activation', out=sqx[:, cc, :], in_=PB[cc][:], func=AF.Square, reads=[pk(cc)],
                              writes=[('sqx', cc)])
                    yield
                    for cc in range(2):
                        tr.op('pe', 'matmul', PB[2][:], lhsT=cb(C_ONES), rhs=sqx[:, cc, :], start=(cc == 0), stop=(cc == 1),
                              reads=['CB', ('sqx', cc)], writes=[pk(2)])
                    yield
                    tr.op('act', 'activation', out=rsx, in_=PB[2][:], func=AF.Ln, scale=1.0 / 256, bias=epsT[:],
                          reads=['eps'], writes=[pk(2), 'rsx'])
                    tr.op('act', 'activation', out=rsx, in_=rsx, func=AF.Exp, scale=-0.5, reads=['rsx'], writes=['rsx'])
                    for cc in range(2):
                        tr.op('dve', 'scalar_tensor_tensor', out=qx[:, cc, :], in0=PB[cc][:], scalar=gx[:, cc:cc + 1],
                              in1=rsx, op0=ALU.mult, op1=ALU.mult, reads=['gx', 'rsx'], writes=[pk(cc), 'qx'])
                    yield
                    for mtl in range(2):
                        for cc in range(2):
                            tr.op('pe', 'matmul', PB[2][:], lhsT=kmT[:, sg, 2 * hh + cc, mtl * 128:(mtl + 1) * 128],
                                  rhs=qx[:, cc, :], start=(cc == 0), stop=(cc == 1), reads=['kmT', 'qx'], writes=[pk(2)])
                        tr.op('act', 'activation', out=PTx[:, mtl, :], in_=PB[2][:], func=AF.Exp, scale=1.0 / 16,
                              writes=[pk(2), 'PTx'])
                    yield
                    for mtl in range(2):
                        tr.op('pe', 'matmul', PB[2][:], lhsT=cb(C_ONES), rhs=PTx[:, mtl, :], start=(mtl == 0), stop=(mtl == 1),
                              reads=['CB', 'PTx'], writes=[pk(2)])
                    for cc in range(2):
                        c = 2 * hh + cc
                        for mtl in range(2):
                            tr.op('pe', 'matmul', PB[cc][:], lhsT=Vm[:, sg, mtl, c * 128:(c + 1) * 128], rhs=PTx[:, mtl, :],
                                  start=(mtl == 0), stop=(mtl == 1), reads=['Vm', 'PTx'], writes=[pk(cc)])
                    tr.op('act', 'activation', out=rdx, in_=PB[2][:], func=AF.Ln, writes=[pk(2), 'rdx'])
                    tr.op('act', 'activation', out=rdx, in_=rdx, func=AF.Exp, scale=-1.0, reads=['rdx'], writes=['rdx'])
                    for cc in range(2):
                        c = 2 * hh + cc
                        tr.op('dve', 'tensor_tensor', out=oT[:, c, :], in0=PB[cc][:], in1=rdx, op=ALU.mult, reads=['rdx'],
                              writes=[pk(cc), 'oT'])
                    yield
                s_ = load_wa(WOB)
                yield
                for j in range(4):
                    for half in range(2):
                        bk = half
                        for kc in range(8):
                            tr.op('pe', 'matmul', PB[bk][:], lhsT=oT[:, kc, j * 128:(j + 1) * 128],
                                  rhs=WA[:, s_, kc, half * 512:(half + 1) * 512], start=(kc == 0), stop=(kc == 7),
                                  reads=[('WA', s_), 'oT'], writes=[pk(bk)])
                        tr.op('dve', 'tensor_tensor', out=hb[:, j, half * 512:(half + 1) * 512], in0=PB[bk][:],
                              in1=hb[:, j, half * 512:(half + 1) * 512], op=ALU.add, writes=[pk(bk), hkey(j)])
                sl0 = norm_A(hb[:, 0, :], hkey(0), 3)
                yield
                sl1 = norm_A(hb[:, 1, :], hkey(1), 3)
                yield
                norm_B(sl0, hn, nkey, 0, 0)
                sl2 = norm_A(hb[:, 2, :], hkey(2), 3)
                yield
                norm_B(sl1, hn, nkey, 1, 1)
                sl3 = norm_A(hb[:, 3, :], hkey(3), 3)
                yield
                norm_B(sl2, hn, nkey, 2, 0)
                yield
                norm_B(sl3, hn, nkey, 3, 1)
                yield

            def MLP_gen(mt):
                t0 = mt * 512
                par = mt % 2
                hb = hbuf[:, par]
                hn = hnT[:, par]
                hkey = lambda j: ('hbuf', par, j)
                nkey = ('hnT', par)

                def mlp_load(pc):
                    ws_ = pc % 3
                    tr.dma('sp', w1p[:, ws_], W1B[:, :, pc * 512:(pc + 1) * 512], key=('w1p', ws_), writes=[('w1p', ws_)])
                    tr.dma('pool', w2p[:, ws_], W2B[:, pc * 4:(pc + 1) * 4, :], key=('w2p', ws_), writes=[('w2p', ws_)])

                def mlp_w1(pc):
                    ps_ = pc % 2
                    ws_ = pc % 3
                    if pc + 1 < 8:
                        mlp_load(pc + 1)
                    for f in range(4):
                        bk = 3 + (f % 2)
                        for k in range(8):
                            tr.op('pe', 'matmul', PB[bk][:], lhsT=w1p[:, ws_, k, f * 128:(f + 1) * 128], rhs=hn[:, k, :],
                                  start=(k == 0), stop=(k == 7), reads=[('w1p', ws_), nkey], writes=[pk(bk)])
                        tr.op('act', 'activation', out=rl[:, f % 2, :], in_=PB[bk][:], func=AF.Relu,
                              writes=[pk(bk), ('rl', f % 2)])
                        tr.op('dve', 'tensor_tensor', out=hTp[:, ps_, f, :], in0=rl[:, f % 2, :], in1=rl[:, f % 2, :],
                              op=ALU.mult, reads=[('rl', f % 2)], writes=[('hTp', ps_, f)])
                        yield

                def mlp_w2(pc):
                    ps_ = pc % 2
                    ws_ = pc % 3
                    for j in range(4):
                        for half in range(2):
                            bk = 5 + ((2 * j + half) % 3)
                            for f in range(4):
                                tr.op('pe', 'matmul', PB[bk][:], lhsT=hTp[:, ps_, f, j * 128:(j + 1) * 128],
                                      rhs=w2p[:, ws_, f, half * 512:(half + 1) * 512], start=(f == 0), stop=(f == 3),
                                      reads=[('hTp', ps_, f), ('w2p', ws_)], writes=[pk(bk)])
                            tr.op('dve', 'tensor_tensor', out=hb[:, j, half * 512:(half + 1) * 512], in0=PB[bk][:],
                                  in1=hb[:, j, half * 512:(half + 1) * 512], op=ALU.add, writes=[pk(bk), hkey(j)])
                            yield

                mlp_load(0)
                yield from mlp_w1(0)
                for pc in range(8):
                    if pc + 1 < 8:
                        yield from mlp_w1(pc + 1)
                    yield from mlp_w2(pc)
                tr.dma('pool', y_d[t0:t0 + 512, :].rearrange("(j p) c -> p j c", p=128), hb, key=('y_o', par),
                       reads=[hkey(j) for j in range(4)], writes=['y'])

            def interleave(gm, gf, ratio=4):
                m_alive, f_alive = True, True
                while m_alive or f_alive:
                    if f_alive:
                        try:
                            next(gf)
                        except StopIteration:
                            f_alive = False
                    for _ in range(ratio if f_alive else 1000000):
                        if not m_alive:
                            break
                        try:
                            next(gm)
                        except StopIteration:
                            m_alive = False

            for _ in F_gen(0):
                pass
            for mt in range(NMT):
                gens = [MLP_gen(mt)]
                if mt + 1 < NMT:
                    gens.append(F_gen(mt + 1))
                interleave(*gens) if len(gens) == 2 else [None for _ in gens[0]]
        for n in dbg_d:
            if n == 'ssdt':
                tr.dma('sp', dbg_d[n], SSDT, key='dbg', reads=['SSDT'])
            if n == 'xnt':
                tr.dma('sp', dbg_d[n], XNT, key='dbg', reads=['XNT'])
            if n == 'attt':
                tr.dma('sp', dbg_d[n], ATTT, key='dbg', reads=['ATTT'])
        print('op counts', tr.cnt, 'dma sems', len(tr.dsem))
        tr.emit('sp')
    return nc


def host_inputs(NT, xs_list, mem_list, fuse_list, params, pos_list):
    consts = make_consts()
    half = 8
    inv_freq = np.power(np.float32(500000.0), -np.arange(half, dtype=np.float32) / half).astype(np.float32)
    maps = []
    for c in range(len(xs_list)):
        f = float(fuse_list[c])
        lo = np.arange(128) < 64
        cols = []
        for (a, b) in [(1, 1), (f, f), (0, 0), (1, f), (f, 1), (1, 0), (0, 1), (f, 0), (0, f)]:
            cols.append(np.where(lo, a, b))
        vtab = np.stack(cols, 1).astype(np.float32)
        vtab = np.concatenate([vtab, (vtab - 1.0) * 30000.0], axis=1).astype(np.float32)
        ang = pos_list[c].astype(np.float32)[None, :] * inv_freq[:, None]
        cos = np.ones((128, NT), np.float32)
        sin = np.zeros((128, NT), np.float32)
        for hb in (0, 64):
            cos[hb:hb + 8] = np.cos(ang)
            cos[hb + 8:hb + 16] = np.cos(ang)
            sin[hb:hb + 8] = np.sin(ang)
            sin[hb + 8:hb + 16] = np.sin(ang)
        m = {"x": np.ascontiguousarray(xs_list[c], np.float32), "mem": np.ascontiguousarray(mem_list[c], np.float32),
             "vtab": vtab, "cs": np.stack([cos, sin], 0), "consts": consts}
        p = params
        m["w_in"] = p["w_in"][0]
        m["w_out"] = p["w_out"][0]
        m["xatt_wq"] = p["xatt_wq"][0]
        m["xatt_wkv"] = p["xatt_wkv"][0]
        m["xatt_wo"] = p["xatt_wo"][0]
        m["mlp_w1"] = p["mlp_w1"][0]
        m["mlp_w2"] = p["mlp_w2"][0]
        for k in ["mix_norm_g", "xatt_norm_g", "mem_norm_g", "mlp_norm_g", "ssd_norm_g", "conv_b"]:
            m[k] = p[k].reshape(1, -1)
        m["conv_w"] = p["conv_w"][0]
        m["ssd_A_log"] = p["ssd_A_log"].reshape(1, 16)
        m["ssd_dt_bias"] = p["ssd_dt_bias"].reshape(1, 16)
        m["ssd_D"] = p["ssd_D"].reshape(1, 8)
        m["att_q_norm_g"] = p["att_q_norm_g"].reshape(64, 1)
        m["att_k_norm_g"] = p["att_k_norm_g"].reshape(64, 1)
        m["att_out_norm_g"] = p["att_out_norm_g"].reshape(8, 64)
        m["xatt_q_norm_g"] = p["xatt_q_norm_g"].reshape(2, 128)
        m["xatt_k_norm_g"] = p["xatt_k_norm_g"].reshape(2, 128)
        maps.append({k: np.ascontiguousarray(np.asarray(v, np.float32)) for k, v in m.items()})
    return maps


_NC_CACHE = {}


def kernel(x_prompt, x_sample, mem_prompt, mem_sample, **params):
    NT = 8192
    x_prompt = np.asarray(x_prompt, np.float32)
    x_sample = np.asarray(x_sample, np.float32)
    mem_prompt = np.asarray(mem_prompt, np.float32)
    mem_sample = np.asarray(mem_sample, np.float32)
    p = {k: np.asarray(v, np.float32) for k, v in params.items()}
    xs_list, mem_list, fuse, pos = [], [], [], []
    for b in range(2):
        xs_list.append(x_prompt[b])
        mem_list.append(np.stack([mem_prompt[b], mem_prompt[b]]))
        fuse.append(1)
        pos.append(np.arange(NT))
    for c in range(4):
        xs_list.append(np.concatenate([x_sample[2 * c], x_sample[2 * c + 1]], 0))
        mem_list.append(np.stack([mem_sample[2 * c], mem_sample[2 * c + 1]]))
        fuse.append(0)
        pos.append(np.concatenate([np.arange(NT // 2)] * 2))
    for c in range(2):
        xs_list.append(np.zeros((NT, D), np.float32))
        mem_list.append(np.zeros((2, 256, D), np.float32))
        fuse.append(0)
        pos.append(np.concatenate([np.arange(NT // 2)] * 2))
    maps = host_inputs(NT, xs_list, mem_list, fuse, p, pos)
    if NT not in _NC_CACHE:
        _NC_CACHE[NT] = build(NT)
    res = run_bass_kernel_spmd(_NC_CACHE[NT], maps, core_ids=list(range(8)))
    ys = [np.asarray(r["y"], np.float32) for r in res.results]
    y_prompt = np.stack([ys[0], ys[1]], 0)
    y_sample = np.stack([ys[2 + c // 2][(c % 2) * (NT // 2):(c % 2 + 1) * (NT // 2)] for c in range(8)], 0)
    return (y_prompt, y_sample)
```

```python
import math
import os
ATT_LEVEL = int(os.environ.get('ATT_LEVEL', '9'))
import numpy as np
import concourse.bass as bass
import concourse.mybir as mybir
from concourse.bass_utils import run_bass_kernel_spmd
from contextlib import ExitStack

F32 = mybir.dt.float32
BF16 = mybir.dt.bfloat16
ALU = mybir.AluOpType
AF = mybir.ActivationFunctionType
EPS = 1e-6
ROT = 20000
D = 1024
INC = 3088
PADV = 1024


class TR:
    def __init__(self, nc, es):
        self.nc = nc
        self.es = es
        self.names = ['pe', 'dve', 'act', 'pool', 'sp']
        self.cnt = {e: 0 for e in self.names}
        self.sems = {e: [] for e in self.names}
        self.dsem = {}
        self.dcnt = {}
        self.lastw = {}
        self.readers = {}
        self.waited = {e: {} for e in self.names}
        self.prog = {e: [] for e in self.names}

    def _sem(self, e, idx):
        j = (idx - 1) // ROT
        while len(self.sems[e]) <= j:
            self.sems[e].append(self.es.enter_context(self.nc.semaphore(f"s_{e}_{len(self.sems[e])}")))
        return self.sems[e][j], (idx - 1) % ROT + 1

    def _deps(self, reads, writes):
        deps = []
        for k in reads:
            if k in self.lastw:
                deps.append(self.lastw[k])
        for k in writes:
            if k in self.lastw:
                deps.append(self.lastw[k])
            deps.extend(self.readers.get(k, []))
        return deps

    def _wait(self, e, deps):
        out = []
        for d in deps:
            if d[0] == 'dma':
                _, key, val = d
                if self.waited[e].get(('dma', key), 0) >= val:
                    continue
                out.append((self.dsem[key], val))
                self.waited[e][('dma', key)] = val
            else:
                f, idx = d
                if f == 'pe' and e == 'pe':
                    continue
                if self.waited[e].get(f, 0) >= idx:
                    continue
                out.append(self._sem(f, idx))
                self.waited[e][f] = idx
        return out

    def _commit(self, ref, reads, writes):
        for k in reads:
            self.readers.setdefault(k, []).append(ref)
        for k in writes:
            self.lastw[k] = ref
            self.readers[k] = []

    def op(self, e, meth, *args, reads=(), writes=(), **kw):
        fn = (lambda h, meth=meth, args=args, kw=kw: getattr(h, meth)(*args, **kw))
        w = self._wait(e, self._deps(reads, writes))
        self.cnt[e] += 1
        s, v = self._sem(e, self.cnt[e])
        self.prog[e].append((w, fn, (s, 1)))
        self._commit((e, self.cnt[e]), reads, writes)

    def dma(self, q, out, in_, key, reads=(), writes=(), **kw):
        w = self._wait(q, self._deps(reads, writes))
        if key not in self.dsem:
            self.dsem[key] = self.es.enter_context(self.nc.semaphore(f"d_{len(self.dsem)}"))
            self.dcnt[key] = 0
        self.dcnt[key] += 16
        self.prog[q].append((w, (lambda h, out=out, in_=in_, kw=kw: h.dma_start(out=out, in_=in_, **kw)),
                             (self.dsem[key], 16)))
        self._commit(('dma', key, self.dcnt[key]), reads, writes)

    def batch_end(self, key, res_keys):
        for k in res_keys:
            self.lastw[k] = ('dma', key, self.dcnt[key])

    def barrier(self):
        fin = [(self.dsem[k], v) for k, v in self.dcnt.items()]
        for e in self.names:
            if self.cnt[e] > 0:
                fin.append(self._sem(e, self.cnt[e]))
        for e in self.names:
            w = []
            for (sm, v) in fin:
                w.append((sm, v))
            self.cnt[e] += 1
            s_, v_ = self._sem(e, self.cnt[e])
            self.prog[e].append((w, (lambda h: h.nop()), (s_, 1)))
        self.lastw = {}
        self.readers = {}

    def emit(self, final_q='sp'):
        fin = [(self.dsem[k], v) for k, v in self.dcnt.items()]
        for e in self.names:
            if self.cnt[e] > 0:
                fin.append(self._sem(e, self.cnt[e]))
        blk = self.es.enter_context(self.nc.Block())
        hmap = {'pe': blk.tensor, 'dve': blk.vector, 'act': blk.scalar, 'pool': blk.gpsimd, 'sp': blk.sync}
        for e in self.names:
            prog = self.prog[e]
            extra = fin if e == final_q else []

            def body(h, prog=prog, extra=extra):
                for (w, fn, inc) in prog:
                    for (s, v) in w:
                        h.wait_ge(s, v)
                    fn(h).then_inc(inc[0], inc[1])
                for (s, v) in extra:
                    h.wait_ge(s, v)
            if prog or extra:
                hmap[e](body)


def make_consts():
    k = np.arange(128)[:, None]
    i = np.arange(128)[None, :]
    ident = (k == i)
    U = (k <= i)
    L = (k >= i)
    Us = (k < i)
    Ls = (k > i)
    ones = np.ones((128, 128), bool)
    blk = ((k // 64) == (i // 64))
    P = np.zeros((128, 128), np.float32)
    for hb in (0, 64):
        for e in range(8):
            P[hb + e + 8, hb + e] = -1.0
            P[hb + e, hb + e + 8] = 1.0
    sel = np.zeros((128, 128), np.float32)
    sel[64, :] = 1.0
    negl = np.where(L, 0.0, -30000.0)
    negu = np.where(U, 0.0, -30000.0)
    mats = [ident, U, L, Us, Ls, ones, blk, P, sel, negl, negu]
    return np.concatenate([np.asarray(m, np.float32) for m in mats], axis=1)


C_ID, C_U, C_L, C_US, C_LS, C_ONES, C_BLK, C_P, C_SEL, C_NEGL, C_NEGU = range(11)
NCM = 11


def build(NT, dbg=(), stop_after=None):
    SEG = NT // 2
    NMT = NT // 512
    NCH = NT // 128
    AM = 2048
    NAM = NT // AM
    nc = bass.Bass("TRN2", target_bir_lowering=False)
    din = lambda n, s, dt=F32: nc.dram_tensor(n, s, dt, kind="ExternalInput").ap()
    dscr = lambda n, s, dt: nc.dram_tensor(n, s, dt, kind="Internal").ap()
    x_d = din("x", [NT, D])
    mem_d = din("mem", [2, 256, D])
    vt_d = din("vtab", [128, 18])
    cs_d = din("cs", [2, 128, NT])
    cst_d = din("consts", [128, NCM * 128])
    w_in_d = din("w_in", [D, INC])
    w_out_d = din("w_out", [D, D])
    wq_d = din("xatt_wq", [D, D])
    wkv_d = din("xatt_wkv", [D, 2 * D])
    wo_d = din("xatt_wo", [D, D])
    w1_d = din("mlp_w1", [D, 4 * D])
    w2_d = din("mlp_w2", [4 * D, D])
    g_mix_d = din("mix_norm_g", [1, D])
    g_xatt_d = din("xatt_norm_g", [1, D])
    g_mem_d = din("mem_norm_g", [1, D])
    g_mlp_d = din("mlp_norm_g", [1, D])
    g_ssd_d = din("ssd_norm_g", [1, 512])
    convw_d = din("conv_w", [5, D])
    convb_d = din("conv_b", [1, D])
    alog_d = din("ssd_A_log", [1, 16])
    dtb_d = din("ssd_dt_bias", [1, 16])
    dsk_d = din("ssd_D", [1, 8])
    gq_d = din("att_q_norm_g", [64, 1])
    gk_d = din("att_k_norm_g", [64, 1])
    gao_d = din("att_out_norm_g", [8, 64])
    gxq_d = din("xatt_q_norm_g", [2, 128])
    gxk_d = din("xatt_k_norm_g", [2, 128])
    y_d = nc.dram_tensor("y", [NT, D], F32, kind="ExternalOutput").ap()
    dbg_d = {n: nc.dram_tensor("dbg_" + n, s, dt, kind="ExternalOutput").ap() for (n, s, dt) in dbg}

    XNT = dscr("XNT", [8, 128, NT + 4], BF16)
    SSDT = dscr("SSDT", [4, 128, NT], BF16)
    ATTT = dscr("ATTT", [4, 128, NT], BF16)
    VS = dscr("VS", [NT + 2 * PADV, 520], BF16)
    WINB = dscr("WINB", [128, 8, INC], BF16)
    WOUTB = dscr("WOUTB", [128, 8, D], BF16)
    WQB = dscr("WQB", [128, 8, D], BF16)
    WKVB = dscr("WKVB", [128, 8, 2 * D], BF16)
    WOB = dscr("WOB", [128, 8, D], BF16)
    W1B = dscr("W1B", [128, 8, 4 * D], BF16)
    W2B = dscr("W2B", [128, 32, D], BF16)

    BSTD = dscr("BSTD", [NCH, 128, 512], BF16)
    ATTF = dscr("ATTF", [8, 64, NT], F32)
    XST_D = dscr("XST_D", [NMT, 128, 4, 512], F32)
    BCT_D = dscr("BCT_D", [NMT, 128, 4, 512], BF16)
    SZ_D = dscr("SZ_D", [NMT, 128, 4, 512], F32)
    with ExitStack() as es:
        tr = TR(nc, es)
        _n = [0]

        def sb(shape, dt, name=None):
            _n[0] += 1
            return es.enter_context(nc.sbuf_tensor("s_" + (name or f"t{_n[0]}"), shape, dt))

        ARW = 43008
        ARENA = sb([128, ARW], F32, "arena")
        apos = [0]

        def phase():
            if os.environ.get('PHASE_DBG'):
                print('phase boundary', dict(tr.cnt))
            tr.barrier()
            apos[0] = 0

        def ph(shape, dt, parts=128):
            n = 1
            for v in shape[1:]:
                n *= v
            w = n if dt == F32 else (n + 1) // 2
            o = apos[0]
            apos[0] += w
            assert apos[0] <= ARW, ("arena overflow", apos[0])
            v = ARENA[0:shape[0], o:o + w]
            if dt != F32:
                v = v.bitcast(dt)
            if len(shape) == 3:
                v = v.rearrange("p (a b) -> p a b", a=shape[1])
            elif len(shape) == 4:
                v = v.rearrange("p (a b c) -> p a b c", a=shape[1], b=shape[2])
            return v

        PB = [es.enter_context(nc.psum_tensor(f"pb{i}", [128, 512], F32)) for i in range(8)]
        pk = lambda i: ('ps', i)
        pbf = lambda i: PB[i][:].bitcast(BF16)

        CF = sb([128, NCM, 128], F32, "CF")
        CB = sb([128, NCM, 128], BF16, "CB")
        vtab = sb([128, 18], F32, "vtab")
        epsT = sb([128, 1], F32, "epsT")
        tr.dma('sp', CF[:].rearrange("p a b -> p (a b)"), cst_d, key=('c0', 1), writes=['CF'])
        tr.dma('sp', vtab[:], vt_d, key=('c0', 2), writes=['vtab'])
        tr.op('dve', 'tensor_copy', out=CB[:], in_=CF[:], reads=['CF'], writes=['CB'])
        tr.op('dve', 'memset', epsT[:], EPS, writes=['eps'])
        cf = lambda i: CF[:, i, :]
        cb = lambda i: CB[:, i, :]
        FUSE = vtab[:, 1:2]

        gB = sb([128, 4, D], F32, "gB")
        for i, gd in enumerate([g_mix_d, g_xatt_d, g_mem_d, g_mlp_d]):
            tr.dma('sp', gB[:, i, :], gd.partition_broadcast(128), key=('c0', 3), writes=['gB'])
        gssd = sb([128, 512], F32, "gssd")
        tr.dma('sp', gssd[:], g_ssd_d.partition_broadcast(128), key=('c0', 4), writes=['gssd'])
        cw = sb([128, 8, 5], F32, "cw")
        cbias = sb([128, 8], F32, "cbias")
        for k in range(5):
            tr.dma('sp', cw[:, :, k:k + 1], convw_d[k:k + 1, :].rearrange("o (c p) -> p c o", p=128), key=('c0', 5),
                   writes=['cw'], allow_slow_non_contiguous=True)
        tr.dma('sp', cbias[:].unsqueeze(2), convb_d.rearrange("o (c p) -> p c o", p=128), key=('c0', 6), writes=['cw'],
               allow_slow_non_contiguous=True)
        Abc = sb([128, 16], F32, "Abc")
        dtb = sb([128, 16], F32, "dtb")
        Dbc = sb([128, 8], F32, "Dbc")
        tr.dma('sp', Abc[:], alog_d.partition_broadcast(128), key=('c0', 7), writes=['Abc'])
        tr.dma('sp', dtb[:], dtb_d.partition_broadcast(128), key=('c0', 8), writes=['dtb'])
        tr.dma('sp', Dbc[:], dsk_d.partition_broadcast(128), key=('c0', 9), writes=['Dbc'])
        tr.op('act', 'activation', out=Abc[:], in_=Abc[:], func=AF.Exp, reads=['Abc'], writes=['Abc'])
        tr.op('dve', 'tensor_scalar', out=Abc[:], in0=Abc[:], scalar1=-1.0, scalar2=None, op0=ALU.mult,
              reads=['Abc'], writes=['Abc'])
        gqk = sb([128, 2], F32, "gqk")
        for hb in (0, 64):
            tr.dma('sp', gqk[hb:hb + 64, 0:1], gq_d, key=('c0', 10), writes=['gqk'])
            tr.dma('sp', gqk[hb:hb + 64, 1:2], gk_d, key=('c0', 11), writes=['gqk'])
        gao = sb([64, 8], F32, "gao")
        tr.dma('sp', gao[:], gao_d.rearrange("h e -> e h"), key=('c0', 12), writes=['gao'], allow_slow_non_contiguous=True)
        gx = sb([128, 4], F32, "gx")
        tr.dma('sp', gx[:, 0:2], gxq_d.rearrange("c p -> p c"), key=('c0', 13), writes=['gx'], allow_slow_non_contiguous=True)
        tr.dma('sp', gx[:, 2:4], gxk_d.rearrange("c p -> p c"), key=('c0', 14), writes=['gx'], allow_slow_non_contiguous=True)
        ssq2 = sb([128, 2], F32, "ssq2")
        rstd2 = sb([128, 2], F32, "rstd2")
        xt = sb([128, 2, D], F32, "xt")
        xn2 = sb([128, 2, D], BF16, "xn2")

        phase()
        wst = ph([128, 2, 2048], F32)
        wsb16 = ph([128, 2, 2048], BF16)
        wi = [0]

        pw_steps = []

        def cast_w(nm, src, dst, nk, ncol):
            for k in range(nk):
                for c0 in range(0, ncol, 2048):
                    cn = min(2048, ncol - c0)

                    def step(k=k, c0=c0, cn=cn):
                        s = wi[0] % 2
                        wi[0] += 1
                        tr.dma('sp', wst[:, s, 0:cn], src[k * 128:(k + 1) * 128, c0:c0 + cn], key=('wst', s),
                               writes=[('wst', s)])
                        if wi[0] % 2 == 0:
                            tr.op('dve', 'tensor_copy', out=wsb16[:, s, 0:cn], in_=wst[:, s, 0:cn], reads=[('wst', s)],
                                  writes=[('wsb', s)])
                        else:
                            tr.op('act', 'copy', out=wsb16[:, s, 0:cn], in_=wst[:, s, 0:cn], reads=[('wst', s)],
                                  writes=[('wsb', s)])
                        tr.dma('pool', dst[:, k, c0:c0 + cn], wsb16[:, s, 0:cn], key=('wsbo', s), reads=[('wsb', s)],
                               writes=[nm])
                    pw_steps.append(step)

        for nm, src, dst, nk, ncol in [("WINB", w_in_d, WINB, 8, INC), ("WOUTB", w_out_d, WOUTB, 8, D),
                                       ("WQB", wq_d, WQB, 8, D), ("WKVB", wkv_d, WKVB, 8, 2 * D),
                                       ("WOB", wo_d, WOB, 8, D), ("W1B", w1_d, W1B, 8, 4 * D),
                                       ("W2B", w2_d, W2B, 32, D)]:
            cast_w(nm, src, dst, nk, ncol)

        nslot = [0]

        def rms_rstd(src_ap, n, key_src, sl_, width):
            tr.op('act', 'activation', out=xn2[:, sl_, 0:width], in_=src_ap, func=AF.Square, accum_out=ssq2[:, sl_:sl_ + 1],
                  reads=[key_src], writes=[('xn', sl_), ('ssq', sl_)])
            tr.op('act', 'activation', out=rstd2[:, sl_:sl_ + 1], in_=ssq2[:, sl_:sl_ + 1], func=AF.Ln, scale=1.0 / n,
                  bias=epsT[:], reads=[('ssq', sl_), 'eps'], writes=[('rstd', sl_)])
            tr.op('act', 'activation', out=rstd2[:, sl_:sl_ + 1], in_=rstd2[:, sl_:sl_ + 1], func=AF.Exp, scale=-0.5,
                  reads=[('rstd', sl_)], writes=[('rstd', sl_)])

        def norm_A(src, skey, gi):
            sl_ = nslot[0] % 2
            nslot[0] += 1
            rms_rstd(src, D, skey, sl_, D)
            tr.op('dve', 'scalar_tensor_tensor', out=xn2[:, sl_, :], in0=src, scalar=rstd2[:, sl_:sl_ + 1], in1=gB[:, gi, :],
                  op0=ALU.mult, op1=ALU.mult, reads=[skey, ('rstd', sl_), 'gB'], writes=[('xn', sl_)])
            return sl_

        def norm_B(sl_, dst, dkey, j, tbank):
            pt = pbf(tbank)
            for k in range(8):
                tr.op('pe', 'transpose', out=pt[:, k * 128:(k + 1) * 128], in_=xn2[:, sl_, k * 128:(k + 1) * 128],
                      identity=cb(C_ID), reads=[('xn', sl_), 'CB'], writes=[pk(tbank)])
            tr.op('act', 'copy', out=dst[:, :, j * 128:(j + 1) * 128],
                  in_=pt.rearrange("p (k t) -> p k t", k=8), writes=[pk(tbank), dkey])

        def norm_T(src, skey, gi, dst, dkey, j, tbank):
            sl_ = nslot[0] % 2
            nslot[0] += 1
            rms_rstd(src, D, skey, sl_, D)
            tr.op('dve', 'scalar_tensor_tensor', out=xn2[:, sl_, :], in0=src, scalar=rstd2[:, sl_:sl_ + 1], in1=gB[:, gi, :],
                  op0=ALU.mult, op1=ALU.mult, reads=[skey, ('rstd', sl_), 'gB'], writes=[('xn', sl_)])
            pt = pbf(tbank)
            for k in range(8):
                tr.op('pe', 'transpose', out=pt[:, k * 128:(k + 1) * 128], in_=xn2[:, sl_, k * 128:(k + 1) * 128],
                      identity=cb(C_ID), reads=[('xn', sl_), 'CB'], writes=[pk(tbank)])
            tr.op('act', 'copy', out=dst[:, :, j * 128:(j + 1) * 128],
                  in_=pt.rearrange("p (k t) -> p k t", k=8), writes=[pk(tbank), dkey])

        xnT_st = ph([128, 8, 512], BF16)
        zt = ph([128, 8, 2], BF16)
        tr.op('dve', 'memset', zt, 0.0, writes=['zt'])
        tr.dma('sp', XNT[:, :, 0:2].rearrange("k p t -> p k t"), zt, key='xnt_o', reads=['zt'], writes=['XNT'])
        tr.dma('sp', XNT[:, :, NT + 2:NT + 4].rearrange("k p t -> p k t"), zt, key='xnt_o', reads=['zt'],
               writes=['XNT'])
        for mt in range(NMT):
            for j in range(4):
                t0 = mt * 512 + j * 128
                s = (mt * 4 + j) % 2
                tr.dma('sp', xt[:, s, :], x_d[t0:t0 + 128, :], key=('xt', s), writes=[('xt', s)])
                norm_T(xt[:, s, :], ('xt', s), 0, xnT_st, 'xnT_st', j, j % 2)
                for _ in range(2):
                    if pw_steps:
                        pw_steps.pop(0)()
            tr.dma('sp', XNT[:, :, 2 + mt * 512:2 + mt * 512 + 512].rearrange("k p t -> p k t"), xnT_st,
                   key='xnt_o', reads=['xnT_st'], writes=['XNT'])

        while pw_steps:
            pw_steps.pop(0)()
        phase()
        w_ssd = ph([128, 8, 1552], BF16)
        tr.dma('sp', w_ssd, WINB[:, :, 0:1552], key='w_ssd', writes=['w_ssd'])
        xnTm = ph([128, 2, 8, 516], BF16)
        uT = ph([128, 2, 516], F32)
        cacc = ph([128, 2, 512], F32)
        xsT = ph([128, 2, 4, 512], F32)
        BCT = ph([128, 2, 4, 512], BF16)
        xs = ph([128, 2, 512], F32)
        Btm = ph([128, 2, 256], BF16)
        dtr = ph([128, 2, 16], F32)
        sp1 = ph([128, 2, 16], F32)
        dt = ph([128, 2, 16], F32)
        dtA = ph([128, 2, 16], F32)
        TT = ph([128, 2, 48], F32)
        EE = ph([128, 2, 48], F32)
        dtd = ph([128, 2, 16], F32)
        xdt = ph([128, 2, 2, 512], BF16)
        xdtd = ph([128, 2, 512], BF16)
        R1 = ph([128, 2, 2, 8, 128], F32) if False else ph([128, 4, 8, 128], F32)
        lm = ph([128, 4, 8, 128], F32)
        cbm = ph([128, 4, 2, 128], F32)
        WT = ph([128, 4, 8, 128], BF16)
        sz = ph([128, 2, 512], F32)
        yb = ph([128, 2, 512], F32)
        t2 = ph([128, 2, 512], F32)
        yn = ph([128, 2, 512], BF16)
        ssdT_st = ph([128, 2, 4, 512], BF16)
        stF = ph([128, 512], F32)
        stB = ph([128, 512], F32)
        prevF = ph([128, 2, 512], BF16)
        bstl = ph([128, 2, 512], BF16)
        szst = ph([128, 2, 512], F32)
        Dfull = ph([128, 8, 64], F32)
        tr.op('dve', 'tensor_copy', out=Dfull, in_=Dbc[:].unsqueeze(2).to_broadcast([128, 8, 64]),
              reads=['Dbc'], writes=['Dfull'])
        tr.op('dve', 'memset', stF, 0.0, writes=['stF'])
        tr.op('dve', 'memset', stB, 0.0, writes=['stB'])
        tr.op('dve', 'memset', prevF, 0.0, writes=[('prevF', 0), ('prevF', 1)])
        b8 = lambda ap16: ap16.unsqueeze(2).to_broadcast([128, 8, 64])
        v3 = lambda t: t.rearrange("p (h e) -> p h e", h=8)

        def run_ssd(mode):
            fwd = (mode == 'fwd')
            cached = (mode != 'prep')
            chunks = list(range(NCH)) if mode != 'bwd' else list(range(NCH - 1, -1, -1))
            macros = list(range(NMT)) if mode != 'bwd' else list(range(NMT - 1, -1, -1))
            n = len(chunks)
            di = 0 if fwd else 1
            hs = slice(di * 8, di * 8 + 8)
            st = stF if fwd else stB
            skey = 'stF' if fwd else 'stB'

            def prep_load(mi):
                mt = macros[mi]
                ms = mi % 2
                tr.dma('sp', xnTm[:, ms], XNT[:, :, mt * 512:mt * 512 + 516].rearrange("k p t -> p k t"),
                       key=('xnTm', ms), writes=[('xnTm', ms)])
                if cached:
                    tr.dma('sp', xsT[:, ms], XST_D[mt], key=('xsT_l', ms), reads=[('XSCa', mt)], writes=[('xsT', ms)])

            def prep_load_b(mi):
                mt = macros[mi]
                ms = mi % 2
                tr.dma('sp', BCT[:, ms], BCT_D[mt], key=('BCT_l', ms), reads=[('XSC', mt)], writes=[('BCT', ms)])

            def prep_A(mi, c):
                if cached:
                    return
                mt = macros[mi]
                ms = mi % 2
                xm = xnTm[:, ms]
                bk = 1 + (c % 2)
                col = 512 + c * 128
                for k in range(8):
                    tr.op('pe', 'matmul', PB[bk][:], lhsT=w_ssd[:, k, col:col + 128], rhs=xm[:, k, 2:514],
                          start=(k == 0), stop=(k == 7), reads=['w_ssd', ('xnTm', ms)], writes=[pk(bk)])
                tr.op('act', 'copy', out=uT[:, c % 2, 2:514], in_=PB[bk][:], writes=[pk(bk), ('uT', c % 2)])
                for hh, (a, b) in enumerate([(0, 2), (514, 516)]):
                    for k in range(8):
                        tr.op('pe', 'matmul', PB[3][:, 480 + 2 * hh:482 + 2 * hh], lhsT=w_ssd[:, k, col:col + 128],
                              rhs=xm[:, k, a:b], start=(k == 0), stop=(k == 7), skip_group_check=True,
                              reads=['w_ssd', ('xnTm', ms)], writes=[pk(3)])
                for hh, (a, b) in enumerate([(0, 2), (514, 516)]):
                    atb = (hh == 0 and mt * 512 == SEG) or (hh == 1 and mt * 512 + 512 == SEG)
                    if atb:
                        tr.op('act', 'activation', out=uT[:, c % 2, a:b], in_=PB[3][:, 480 + 2 * hh:482 + 2 * hh],
                              func=AF.Copy, scale=FUSE, reads=['vtab'], writes=[pk(3), ('uT', c % 2)])
                    else:
                        tr.op('act', 'copy', out=uT[:, c % 2, a:b], in_=PB[3][:, 480 + 2 * hh:482 + 2 * hh],
                              writes=[pk(3), ('uT', c % 2)])
                ca = cacc[:, c % 2]
                tr.op('act', 'activation', out=ca, in_=uT[:, c % 2, 0:512], func=AF.Identity, scale=cw[:, c, 0:1],
                      bias=cbias[:, c:c + 1], reads=[('uT', c % 2), 'cw'], writes=[('cacc', c % 2)])

            def prep_B(mi, c):
                mt = macros[mi]
                ms = mi % 2
                xm = xnTm[:, ms]
                ca = cacc[:, c % 2]
                for k in range(1, 5):
                    eng = 'dve'
                    tr.op(eng, 'scalar_tensor_tensor', out=ca, in0=uT[:, c % 2, k:k + 512], scalar=cw[:, c, k:k + 1],
                          in1=ca, op0=ALU.mult, op1=ALU.add, reads=[('uT', c % 2), 'cw'], writes=[('cacc', c % 2)])
                if c < 4:
                    tr.op('act', 'activation', out=xsT[:, ms, c, :], in_=ca, func=AF.Silu, reads=[('cacc', c % 2)],
                          writes=[('xsT', ms)])
                else:
                    tr.op('act', 'activation', out=BCT[:, ms, c - 4, :], in_=ca, func=AF.Silu, reads=[('cacc', c % 2)],
                          writes=[('BCT', ms)])
                if c == 3:
                    tr.dma('pool', XST_D[mt], xsT[:, ms], key=('xst_o', ms), reads=[('xsT', ms)], writes=[('XSCa', mt)])
                if c == 7:
                    tr.dma('pool', BCT_D[mt], BCT[:, ms], key=('bct_o', ms), reads=[('BCT', ms)], writes=[('XSC', mt)])
                    for j in range(4):
                        zs = j % 2
                        for k in range(8):
                            tr.op('pe', 'matmul', PB[7][:], lhsT=xm[:, k, 2 + j * 128:2 + (j + 1) * 128], rhs=w_ssd[:, k, 0:512],
                                  start=(k == 0), stop=(k == 7), reads=['w_ssd', ('xnTm', ms)], writes=[pk(7)])
                        tr.op('act', 'activation', out=szst[:, zs], in_=PB[7][:], func=AF.Silu, writes=[pk(7), ('szst', zs)])
                        tr.dma('pool', SZ_D[mt, :, j, :], szst[:, zs], key=('sz_o', zs), reads=[('szst', zs)],
                               writes=[('SZ', mt, j)])


            def prep_c(mi, c):
                if cached:
                    return
                prep_A(mi, c)
                prep_B(mi, c)

            def S1(p):
                ch = chunks[p]
                mt, j = ch // 4, ch % 4
                ms = (p // 4) % 2
                s = p % 2
                xm = xnTm[:, ms]
                tk = slice(j * 128, (j + 1) * 128)
                for c in range(4):
                    tr.op('pe', 'transpose', out=PB[4][:, c * 128:(c + 1) * 128], in_=xsT[:, ms, c, tk], identity=cf(C_ID),
                          reads=[('xsT', ms), 'CF'], writes=[pk(4)])
                tr.op('act', 'copy', out=xs[:, s], in_=PB[4][:], writes=[pk(4), ('xs', s)])
                pt = pbf(0)
                for g in range(2):
                    tr.op('pe', 'transpose', out=pt[:, g * 128:(g + 1) * 128], in_=BCT[:, ms, g, tk], identity=cb(C_ID),
                          reads=[('BCT', ms), 'CB'], writes=[pk(0)])
                tr.op('act', 'copy', out=Btm[:, s], in_=pt[:, 0:256], writes=[pk(0), ('Btm', s)])
                for k in range(8):
                    tr.op('pe', 'matmul', PB[3][:, 0:16], lhsT=xm[:, k, 2 + j * 128:2 + (j + 1) * 128],
                          rhs=w_ssd[:, k, 1536:1552], start=(k == 0), stop=(k == 7), skip_group_check=True,
                          reads=['w_ssd', ('xnTm', ms)], writes=[pk(3)])
                tr.op('dve', 'tensor_tensor', out=dtr[:, s], in0=PB[3][:, 0:16], in1=dtb[:], op=ALU.add,
                      reads=['dtb'], writes=[pk(3), ('dtr', s)])
                tr.op('act', 'activation', out=sp1[:, s], in_=dtr[:, s], func=AF.Abs, reads=[('dtr', s)],
                      writes=[('sp1', s)])
                tr.op('act', 'activation', out=sp1[:, s], in_=sp1[:, s], func=AF.Exp, scale=-1.0, reads=[('sp1', s)],
                      writes=[('sp1', s)])
                tr.op('act', 'activation', out=sp1[:, s], in_=sp1[:, s], func=AF.Ln, bias=1.0, reads=[('sp1', s)],
                      writes=[('sp1', s)])
                tr.op('dve', 'scalar_tensor_tensor', out=dt[:, s], in0=dtr[:, s], scalar=0.0, in1=sp1[:, s], op0=ALU.max,
                      op1=ALU.add, reads=[('dtr', s), ('sp1', s)], writes=[('dt', s)])
                tr.op('dve', 'tensor_tensor', out=dtA[:, s], in0=dt[:, s], in1=Abc[:], op=ALU.mult,
                      reads=[('dt', s), 'Abc'], writes=[('dtA', s)])
                if fwd:
                    for dd, tri in enumerate([C_U, C_L]):
                        tr.op('pool', 'tensor_tensor', out=R1[:, 2 * s + dd],
                              in0=cf(tri).unsqueeze(1).to_broadcast([128, 8, 128]),
                              in1=dtA[:, s, dd * 8:dd * 8 + 8].unsqueeze(2).to_broadcast([128, 8, 128]), op=ALU.mult,
                              reads=['CF', ('dtA', s)], writes=[('R1', s, dd)])
                    tr.dma('sp', sz[:, s], SZ_D[mt, :, j, :], key=('sz_l', s), reads=[('SZ', mt, j)], writes=[('sz', s)])

            def S2(p):
                ch = chunks[p]
                mt, j = ch // 4, ch % 4
                ms = (p // 4) % 2
                s = p % 2
                tk = slice(j * 128, (j + 1) * 128)
                tr.op('pe', 'matmul', PB[3][:, 16:24], lhsT=cf(C_U), rhs=dtA[:, s, 0:8], start=True, stop=True,
                      skip_group_check=True, reads=['CF', ('dtA', s)], writes=[pk(3)])
                tr.op('pe', 'matmul', PB[3][:, 24:32], lhsT=cf(C_L), rhs=dtA[:, s, 8:16], start=False, stop=True,
                      skip_group_check=True, reads=['CF', ('dtA', s)], writes=[pk(3)])
                tr.op('pe', 'matmul', PB[3][:, 32:48], lhsT=cf(C_ONES), rhs=dtA[:, s, 0:16], start=False, stop=True,
                      skip_group_check=True, reads=['CF', ('dtA', s)], writes=[pk(3)])
                tr.op('act', 'copy', out=TT[:, s, 0:32], in_=PB[3][:, 16:48], writes=[pk(3), ('TT', s)])
                tr.op('dve', 'tensor_tensor', out=TT[:, s, 32:48], in0=TT[:, s, 16:32], in1=TT[:, s, 0:16], op=ALU.subtract,
                      reads=[('TT', s)], writes=[('TT', s)])
                tr.op('act', 'activation', out=EE[:, s], in_=TT[:, s], func=AF.Exp, reads=[('TT', s)], writes=[('EE', s)])
                tr.op('dve', 'tensor_tensor', out=dtd[:, s], in0=dt[:, s], in1=EE[:, s, 32:48], op=ALU.mult,
                      reads=[('dt', s), ('EE', s)], writes=[('dtd', s)])
                tr.op('dve', 'tensor_tensor', out=v3(xdtd[:, s]), in0=v3(xs[:, s]), in1=b8(dtd[:, s, hs]), op=ALU.mult,
                      reads=[('xs', s), ('dtd', s)], writes=[('xdtd', s)])
                if fwd:
                    tr.dma('sp', bstl[:, s, :], BSTD[ch], key=('bstl', s), reads=[('BSTD', ch)], writes=[('bstl', s)])
                    for dd in range(2):
                        tr.op('dve', 'tensor_tensor', out=v3(xdt[:, s, dd]), in0=v3(xs[:, s]),
                              in1=b8(dt[:, s, dd * 8:dd * 8 + 8]), op=ALU.mult, reads=[('xs', s), ('dt', s)],
                              writes=[('xdt', s)])
                    for dd, lt in enumerate([C_LS, C_US]):
                        for hf in range(2):
                            bk = 5 + hf
                            tr.op('pe', 'matmul', PB[bk][:], lhsT=cf(lt),
                                  rhs=R1[:, 2 * s + dd, hf * 4:hf * 4 + 4, :].rearrange("p h i -> p (h i)"), start=True,
                                  stop=True, reads=['CF', ('R1', s, dd)], writes=[pk(bk)])
                            tr.op('act', 'activation',
                                  out=lm[:, 2 * s + dd, hf * 4:hf * 4 + 4, :].rearrange("p h i -> p (h i)"),
                                  in_=PB[bk][:], func=AF.Exp, writes=[pk(bk), ('lm', s, dd)])
                    for g in range(2):
                        tr.op('pe', 'matmul', PB[7][:, g * 128:(g + 1) * 128], lhsT=BCT[:, ms, g, tk], rhs=BCT[:, ms, 2 + g, tk],
                              start=(g == 0), stop=True, skip_group_check=True, reads=[('BCT', ms)], writes=[pk(7)])
                    for dd, tri in enumerate([C_U, C_L]):
                        tr.op('dve', 'tensor_tensor', out=cbm[:, 2 * s + dd],
                              in0=PB[7][:, 0:256].rearrange("p (g i) -> p g i", g=2),
                              in1=cf(tri).unsqueeze(1).to_broadcast([128, 2, 128]), op=ALU.mult,
                              reads=['CF'], writes=[pk(7), ('cbm', s, dd)])
                        tr.op('dve', 'tensor_tensor', out=WT[:, 2 * s + dd].rearrange("p (g r) i -> p g r i", g=2),
                              in0=lm[:, 2 * s + dd].rearrange("p (g r) i -> p g r i", g=2),
                              in1=cbm[:, 2 * s + dd].unsqueeze(2).to_broadcast([128, 2, 4, 128]), op=ALU.mult,
                              reads=[('lm', s, dd), ('cbm', s, dd)], writes=[('WT', s, dd)])
                if not fwd:
                    tr.op('act', 'copy', out=bstl[:, s, :], in_=stB, reads=['stB'], writes=[('bstl', s)])
                    tr.dma('sp', BSTD[ch], bstl[:, s, :], key=('bsto', s), reads=[('bstl', s)], writes=[('BSTD', ch)])
                for g in range(2):
                    tr.op('pe', 'matmul', PB[1][:, g * 256:(g + 1) * 256], lhsT=Btm[:, s, g * 128:(g + 1) * 128],
                          rhs=xdtd[:, s, g * 256:(g + 1) * 256], start=(g == 0), stop=True, skip_group_check=True,
                          reads=[('Btm', s), ('xdtd', s)], writes=[pk(1)])
                tr.op('dve', 'tensor_tensor', out=v3(st), in0=v3(st), in1=b8(EE[:, s, 16 + di * 8:24 + di * 8]),
                      op=ALU.mult, reads=[('EE', s)], writes=[skey])
                tr.op('dve', 'tensor_tensor', out=st, in0=st, in1=PB[1][:], op=ALU.add, writes=[pk(1), skey])
                if (fwd and (ch + 1) * 128 == SEG) or ((not fwd) and ch * 128 == SEG):
                    tr.op('dve', 'tensor_scalar', out=st, in0=st, scalar1=FUSE, scalar2=None, op0=ALU.mult,
                          reads=['vtab'], writes=[skey])
                if fwd:
                    tr.op('dve', 'tensor_copy', out=prevF[:, (p + 1) % 2], in_=stF, reads=['stF'], writes=[('prevF', (p + 1) % 2)])

            def S3(p):
                ch = chunks[p]
                mt, j = ch // 4, ch % 4
                ms = (p // 4) % 2
                s = p % 2
                tk = slice(j * 128, (j + 1) * 128)
                first = True
                for dd in range(2):
                    for h in range(8):
                        tr.op('pe', 'matmul', PB[4][:, h * 64:(h + 1) * 64], lhsT=WT[:, 2 * s + dd, h, :],
                              rhs=xdt[:, s, dd, h * 64:(h + 1) * 64], start=first, stop=True, skip_group_check=True,
                              reads=[('WT', s, dd), ('xdt', s)], writes=[pk(4)])
                        first = False
                for g in range(2):
                    tr.op('pe', 'matmul', PB[5][:, g * 256:(g + 1) * 256], lhsT=BCT[:, ms, 2 + g, tk],
                          rhs=prevF[:, s, g * 256:(g + 1) * 256], start=(g == 0), stop=True, skip_group_check=True,
                          reads=[('BCT', ms), ('prevF', s)], writes=[pk(5)])
                    tr.op('pe', 'matmul', PB[6][:, g * 256:(g + 1) * 256], lhsT=BCT[:, ms, 2 + g, tk],
                          rhs=bstl[:, s, g * 256:(g + 1) * 256], start=(g == 0), stop=True, skip_group_check=True,
                          reads=[('BCT', ms), ('bstl', s)], writes=[pk(6)])
                y_ = yb[:, s]
                t_ = t2[:, s]
                tr.op('dve', 'tensor_tensor', out=v3(y_), in0=v3(PB[5][:]), in1=b8(EE[:, s, 0:8]), op=ALU.mult,
                      reads=[('EE', s)], writes=[pk(5), ('yb', s)])
                tr.op('dve', 'tensor_tensor', out=v3(t_), in0=v3(PB[6][:]), in1=b8(EE[:, s, 8:16]), op=ALU.mult,
                      reads=[('EE', s)], writes=[pk(6), ('t2', s)])
                tr.op('pool', 'tensor_tensor', out=y_, in0=y_, in1=t_, op=ALU.add, reads=[('t2', s)], writes=[('yb', s)])
                tr.op('dve', 'tensor_tensor', out=y_, in0=y_, in1=PB[4][:], op=ALU.add, writes=[pk(4), ('yb', s)])
                tr.op('pool', 'tensor_tensor', out=t_, in0=xs[:, s], in1=Dfull.rearrange("p h e -> p (h e)"),
                      op=ALU.mult, reads=[('xs', s), 'Dfull'], writes=[('t2', s)])
                tr.op('pool', 'tensor_tensor', out=y_, in0=y_, in1=t_, op=ALU.add, reads=[('t2', s)], writes=[('yb', s)])
                tr.op('pool', 'tensor_tensor', out=y_, in0=y_, in1=sz[:, s], op=ALU.mult, reads=[('sz', s)],
                      writes=[('yb', s)])

            def S3b(p):
                s = p % 2
                y_ = yb[:, s]
                rms_rstd(y_, 512, ('yb', s), s, 512)
                tr.op('dve', 'scalar_tensor_tensor', out=yn[:, s], in0=y_, scalar=rstd2[:, s:s + 1], in1=gssd[:],
                      op0=ALU.mult, op1=ALU.mult, reads=[('yb', s), ('rstd', s), 'gssd'], writes=[('yn', s)])

            def S4(p):
                ch = chunks[p]
                mt, j = ch // 4, ch % 4
                mo = mt % 2
                s = p % 2
                tk = slice(j * 128, (j + 1) * 128)
                pt = pbf(2)
                for c in range(4):
                    tr.op('pe', 'transpose', out=pt[:, c * 128:(c + 1) * 128], in_=yn[:, s, c * 128:(c + 1) * 128],
                          identity=cb(C_ID), reads=[('yn', s), 'CB'], writes=[pk(2)])
                tr.op('act', 'copy', out=ssdT_st[:, mo, :, tk], in_=pt[:, 0:512].rearrange("p (c t) -> p c t", c=4),
                      writes=[pk(2), ('ssdT_st', mo)])
                if j == 3:
                    tr.dma('sp', SSDT[:, :, mt * 512:(mt + 1) * 512].rearrange("c p t -> p c t"), ssdT_st[:, mo],
                           key=('ssdt_o', mo), reads=[('ssdT_st', mo)], writes=['SSDT'])

            if mode == 'prep':
                prep_load(0)
                for mi in range(len(macros)):
                    if mi + 1 < len(macros):
                        prep_load(mi + 1)
                    prep_A(mi, 0)
                    for c in range(8):
                        if c + 1 < 8:
                            prep_A(mi, c + 1)
                        prep_B(mi, c)
                return
            for t in range(-7, n + 3):
                if fwd:
                    if 0 <= t - 2 < n:
                        S4(t - 2)
                    if 0 <= t - 1 < n:
                        S3b(t - 1)
                    if 0 <= t < n:
                        S3(t)
                if 0 <= t + 1 < n:
                    S2(t + 1)
                if 0 <= t + 2 < n:
                    S1(t + 2)
                for mi in range(len(macros)):
                    o = t - (4 * mi - 7)
                    if o == 0:
                        prep_load(mi)
                    elif 1 <= o <= 4:
                        prep_c(mi, 2 * (o - 1))
                        prep_c(mi, 2 * (o - 1) + 1)
                        if cached and o == 3:
                            prep_load_b(mi)

        run_ssd('prep')
        run_ssd('bwd')
        run_ssd('fwd')

        if stop_after not in ('ssd',):
            phase()
            w_v = ph([128, 8, 512], BF16)
            tr.dma('sp', w_v, WINB[:, :, 2576:3088], key='w_v', writes=['w_v'])
            xmv = ph([128, 2, 8, 512], BF16)
            vst = ph([128, 2, 4, 520], BF16)
            zrow = ph([128, 520], BF16)
            tr.op('dve', 'memset', zrow, 0.0, writes=['zrow'])
            tr.op('dve', 'memset', vst, 1.0, writes=[('vst', 0), ('vst', 1)])
            for i in range(PADV // 128):
                tr.dma('pool', VS[i * 128:(i + 1) * 128, :], zrow, key='vs_z', reads=['zrow'], writes=['VS'])
                tr.dma('pool', VS[PADV + NT + i * 128:PADV + NT + (i + 1) * 128, :], zrow, key='vs_z', reads=['zrow'],
                       writes=['VS'])
            for mt in range(NMT):
                s = mt % 2
                tr.dma('sp', xmv[:, s], XNT[:, :, 2 + mt * 512:2 + mt * 512 + 512].rearrange("k p t -> p k t"),
                       key=('xmv', s), writes=[('xmv', s)])
                for j in range(4):
                    bk = j % 2
                    for k in range(8):
                        tr.op('pe', 'matmul', PB[bk][:], lhsT=xmv[:, s, k, j * 128:(j + 1) * 128], rhs=w_v[:, k, :],
                              start=(k == 0), stop=(k == 7), reads=['w_v', ('xmv', s)], writes=[pk(bk)])
                    tr.op('act', 'copy', out=vst[:, s, j, :].rearrange("p (h e) -> p h e", h=8)[:, :, 0:64],
                          in_=PB[bk][:].rearrange("p (h e) -> p h e", h=8), writes=[pk(bk), ('vst', s)])
                tr.dma('pool', VS[PADV + mt * 512:PADV + (mt + 1) * 512, :].rearrange("(j p) c -> p j c", p=128),
                       vst[:, s], key=('vs_o', s), reads=[('vst', s)], writes=['VS'])

        if stop_after not in ('ssd', 'p3v'):
            phase()
            w_qk = ph([128, 8, 1024], BF16)
            tr.dma('sp', w_qk, WINB[:, :, 1552:2576], key='w_qk', writes=['w_qk'])
            kT = ph([128, 4, 4096], BF16)
            qT = ph([128, 4, 2048], BF16)
            xma = ph([128, 2, 8, 512], BF16)
            cst = ph([128, 2, 2, 512], F32)
            NRS = 2
            sqb = ph([128, NRS, 512], BF16)
            rsn = ph([128, 2, 512], F32)
            qn = ph([128, 2, 512], F32)
            qnb = ph([128, NRS, 512], BF16)
            tA = ph([128, 2, 512], F32)
            tB = ph([128, 2, 512], F32)
            Vt2 = ph([128, 69, 130], BF16)
            NS = 4
            LAG = 3
            PT = ph([128, NS, 256], BF16)
            accs = ph([65, 2, 2048], F32)
            rden = ph([64, 2, 512], F32)
            sqa = ph([64, 2, 512], BF16)
            ssacc = ph([64, 2048], F32)
            attl = rden
            attT_st = ph([64, 8, 512], BF16)
            tr.op('dve', 'memset', kT, 0.0, writes=['kT'])
            MLU = ph([128, 256], BF16)
            tr.op('dve', 'tensor_copy', out=MLU[:, 0:128], in_=cf(C_L), reads=['CF'], writes=['MLU'])
            tr.op('dve', 'tensor_copy', out=MLU[:, 128:256], in_=cf(C_U), reads=['CF'], writes=['MLU'])
            VCOL = {(1, 1): 0, ('f', 'f'): 1, (0, 0): 2, (1, 'f'): 3, ('f', 1): 4, (1, 0): 5, (0, 1): 6, ('f', 0): 7,
                    (0, 'f'): 8}
            NRS = 2

            def nr_stages(u, unit):
                col, gcol, dst, dkey, xs_ = unit
                s = u % NRS
                s2_ = u % 2
                pbank = u % 4
                b2 = 4 + (u % 2)
                b3 = 6 + (u % 2)

                def s1():
                    for k in range(8):
                        tr.op('pe', 'matmul', PB[pbank][:], lhsT=w_qk[:, k, col:col + 128], rhs=xma[:, xs_, k, :],
                              start=(k == 0), stop=(k == 7), reads=['w_qk', ('xma', xs_)], writes=[pk(pbank)])

                def s2():
                    tr.op('act', 'activation', out=sqb[:, s], in_=PB[pbank][:], func=AF.Square, reads=[pk(pbank)],
                          writes=[('sqb', s)])
                    tr.op('pe', 'matmul', PB[b2][:], lhsT=cb(C_BLK), rhs=sqb[:, s], start=True, stop=True,
                          reads=['CB', ('sqb', s)], writes=[pk(b2)])

                def s3():
                    tr.op('act', 'activation', out=rsn[:, s2_], in_=PB[b2][:], func=AF.Ln, scale=1.0 / 64, bias=epsT[:],
                          reads=['eps'], writes=[pk(b2), ('rsn', s2_)])
                    tr.op('act', 'activation', out=rsn[:, s2_], in_=rsn[:, s2_], func=AF.Exp, scale=-0.5, reads=[('rsn', s2_)], writes=[('rsn', s2_)])
                    tr.op('dve', 'scalar_tensor_tensor', out=qn[:, s2_], in0=PB[pbank][:], scalar=gqk[:, gcol:gcol + 1],
                          in1=rsn[:, s2_], op0=ALU.mult, op1=ALU.mult, reads=['gqk', ('rsn', s2_)],
                          writes=[pk(pbank), ('qn', s2_)])
                    tr.op('act', 'copy', out=qnb[:, s], in_=qn[:, s2_], reads=[('qn', s2_)], writes=[('qnb', s)])
                    tr.op('pe', 'matmul', PB[b3][:], lhsT=cb(C_P), rhs=qnb[:, s], start=True, stop=True,
                          reads=['CB', ('qnb', s)], writes=[pk(b3)])
                    tr.op('pool', 'tensor_tensor', out=tB[:, s2_], in0=qn[:, s2_], in1=cst[:, xs_, 0, :], op=ALU.mult,
                          reads=[('qn', s2_), ('cst', xs_)], writes=[('tB', s2_)])

                def s4():
                    tr.op('dve', 'tensor_tensor', out=tA[:, s2_], in0=PB[b3][:], in1=cst[:, xs_, 1, :], op=ALU.mult,
                          reads=[('cst', xs_)], writes=[pk(b3), ('tA', s2_)])
                    tr.op('dve', 'tensor_tensor', out=dst, in0=tA[:, s2_], in1=tB[:, s2_], op=ALU.add,
                          reads=[('tA', s2_), ('tB', s2_)], writes=[dkey])
                return [s1, s2, s3, s4]

            def run_units(units, hooks):
                st = [nr_stages(u, un) for u, un in enumerate(units)]
                n = len(st)
                for t in range(n + 3):
                    for fn in hooks.get(t, []):
                        fn()
                    for k in range(4):
                        if 0 <= t - k < n:
                            st[t - k][k]()

            slot_ctr = [0]
            pending = []

            def run_pending(nmax):
                k = 0
                while pending and k < nmax:
                    fn = pending.pop(0)
                    sl = slot_ctr[0] % NS
                    fn(sl)
                    k += 1

            def make_post(h, a0):
                ap_ = h % 2
                steps = []
                for b_ in range(4):
                    qs = slice(b_ * 512, (b_ + 1) * 512)

                    def st1(sl, b_=b_, qs=qs):
                        r_ = b_ % 2
                        tr.op('pe', 'matmul', PB[sl][0:64, :], lhsT=CF[0:65, C_SEL, 0:64], rhs=accs[:, ap_, qs], start=True,
                              stop=True, reads=['CF', ('accs', ap_)], writes=[pk(sl)])
                        tr.op('act', 'activation', out=rden[:, r_], in_=PB[sl][0:64, :], func=AF.Ln, writes=[pk(sl), ('rden', r_)])
                        tr.op('act', 'activation', out=rden[:, r_], in_=rden[:, r_], func=AF.Exp, scale=-1.0, reads=[('rden', r_)], writes=[('rden', r_)])
                        tr.op('dve', 'tensor_tensor', out=accs[0:64, ap_, qs], in0=accs[0:64, ap_, qs], in1=rden[:, r_],
                              op=ALU.mult, reads=[('rden', r_)], writes=[('accs', ap_)])
                        tr.op('act', 'activation', out=sqa[:, r_], in_=accs[0:64, ap_, qs], func=AF.Square,
                              reads=[('accs', ap_)], writes=[('sqa', r_)])

                    def st2(sl, b_=b_, qs=qs):
                        r_ = b_ % 2
                        tr.op('pe', 'matmul', PB[sl][0:64, :], lhsT=CB[0:64, C_ONES, 0:64], rhs=sqa[:, r_], start=True,
                              stop=True, reads=['CB', ('sqa', r_)], writes=[pk(sl)])
                        if h == 0:
                            tr.op('act', 'copy', out=ssacc[:, qs], in_=PB[sl][0:64, :], writes=[pk(sl), 'ssacc'])
                        else:
                            tr.op('dve', 'tensor_tensor', out=ssacc[:, qs], in0=ssacc[:, qs], in1=PB[sl][0:64, :],
                                  op=ALU.add, writes=[pk(sl), 'ssacc'])
                        if b_ == 3:
                            tr.dma('sp', ATTF[h, :, a0:a0 + AM], accs[0:64, ap_, :], key=('attf_o', ap_), reads=[('accs', ap_)],
                                   writes=[('ATTF', h)])
                    steps.append(st1)
                    steps.append(st2)
                return steps

            for am in range(NAM):
                a0 = am * AM
                sega = a0 // SEG
                valid = [sbk for sbk in range(8) if 0 <= a0 - 1024 + sbk * 512 < NT]
                units = []
                hooks = {}

                def mk_load(i):
                    sbk = valid[i]
                    tok0 = a0 - 1024 + sbk * 512
                    xs_ = i % 2

                    def fn():
                        tr.dma('sp', xma[:, xs_], XNT[:, :, 2 + tok0:2 + tok0 + 512].rearrange("k p t -> p k t"),
                               key=('xma', xs_), writes=[('xma', xs_)])
                        tr.dma('sp', cst[:, xs_], cs_d[:, :, tok0:tok0 + 512].rearrange("a p t -> p a t"),
                               key=('cst', xs_), writes=[('cst', xs_)])
                    return fn

                for i, sbk in enumerate(valid):
                    f_i = len(units)
                    if i == 0:
                        hooks.setdefault(0, []).append(mk_load(0))
                    if i + 1 < len(valid):
                        hooks.setdefault(f_i + (4 if i > 0 else 0), []).append(mk_load(i + 1))
                    for c in range(4):
                        units.append((512 + c * 128, 1, kT[:, c, sbk * 512:(sbk + 1) * 512], 'kT', i % 2))
                        if 2 <= sbk < 6:
                            units.append((c * 128, 0, qT[:, c, (sbk - 2) * 512:(sbk - 1) * 512], 'qT', i % 2))
                run_units(units, hooks)
                for hp in range(4):
                    tix = {}
                    ti = 0
                    for d in (1, 4, 16):
                        nu = AM // (128 * d)
                        for r in range(d):
                            for u in range(nu + 1):
                                tb = a0 - 64 * d + 128 * d * u
                                st_ = []
                                for hb_ in (tb, tb + 64 * d):
                                    if hb_ < 0 or hb_ >= NT:
                                        st_.append(0)
                                    elif hb_ // SEG == sega:
                                        st_.append(1)
                                    else:
                                        st_.append('f')
                                st_ = tuple(st_)
                                if st_ == (0, 0):
                                    continue
                                row0 = PADV + tb + r
                                tr.dma('pool', Vt2[:, ti, :], VS[row0:row0 + 127 * d + 1:d, hp * 130:(hp + 1) * 130],
                                       key=('vt', ti // 9), reads=['VS'], writes=[('Vt2', ti)])
                                tix[(d, r, u)] = (ti, VCOL[st_])
                                ti += 1
                    for g_ in range((ti + 8) // 9):
                        tr.batch_end(('vt', g_), [('Vt2', t_) for t_ in range(g_ * 9, min(ti, g_ * 9 + 9))])
                    for hh in range(2):
                        h = hp * 2 + hh
                        if os.environ.get('PHASE_DBG') and am == 1:
                            print('head start', h, tr.cnt['pe'])
                        c = hp
                        rb = hh * 64
                        apar = h % 2
                        first_in_bank = [True] * 4
                        items = []
                        for d in (1, 4, 16):
                            nu = AM // (128 * d)
                            for r in range(d):
                                for ub in range(nu):
                                    blocks = []
                                    for bi, u in enumerate((ub, ub + 1)):
                                        if (d, r, u) in tix:
                                            ti_, vc = tix[(d, r, u)]
                                            blocks.append((bi, ti_, vc, 1024 - 64 * d + 128 * d * u + r))
                                    items.append((d, r, ub, 128 * d * ub + r, blocks))
                        slots = {}

                        def emit_S(i):
                            d, r, ub, qc0, blocks = items[i]
                            sl = slot_ctr[0] % NS
                            slot_ctr[0] += 1
                            slots[i] = sl
                            for (bi, ti_, vc, kc0) in blocks:
                                tr.op('pe', 'matmul', PB[sl][:, bi * 128:(bi + 1) * 128],
                                      lhsT=kT[rb:rb + 64, c, kc0:kc0 + 127 * d + 1:d],
                                      rhs=qT[rb:rb + 64, c, qc0:qc0 + 127 * d + 1:d], start=True, stop=True,
                                      skip_group_check=True, reads=['kT', 'qT'], writes=[pk(sl)])
                            lo_ = min(b_[0] for b_ in blocks)
                            hi_ = max(b_[0] for b_ in blocks) + 1
                            if len(blocks) == 2 and blocks[0][2] == blocks[1][2]:
                                vc = blocks[0][2]
                                tr.op('act', 'activation', out=PT[:, sl, :], in_=PB[sl][:, 0:256], func=AF.Exp,
                                      scale=0.125, bias=vtab[:, 9 + vc:10 + vc], reads=['vtab'],
                                      writes=[pk(sl), ('PT', sl)])
                            else:
                                for (bi, ti_, vc, kc0) in blocks:
                                    tr.op('act', 'activation', out=PT[:, sl, bi * 128:(bi + 1) * 128],
                                          in_=PB[sl][:, bi * 128:(bi + 1) * 128], func=AF.Exp, scale=0.125,
                                          bias=vtab[:, 9 + vc:10 + vc], reads=['vtab'], writes=[pk(sl), ('PT', sl)])
                            tr.op('dve', 'tensor_tensor', out=PT[:, sl, lo_ * 128:hi_ * 128],
                                  in0=PT[:, sl, lo_ * 128:hi_ * 128],
                                  in1=CB[:, C_L + lo_:C_L + hi_, :].rearrange("p a b -> p (a b)") if lo_ == 0 and hi_ == 1 else
                                  (MLU[:, lo_ * 128:hi_ * 128]), op=ALU.mult, reads=['MLU'], writes=[('PT', sl)])

                        def emit_PV(i):
                            d, r, ub, qc0, blocks = items[i]
                            sl = slots[i]
                            for (bi, ti_, vc, kc0) in blocks:
                                vl = Vt2[:, ti_, hh * 65:hh * 65 + 65]
                                if d == 16:
                                    pieces = [(b_, PB[4 + b_][0:65, r:512:16],
                                               PT[:, sl, bi * 128 + 32 * b_:bi * 128 + 32 * b_ + 32]) for b_ in range(4)]
                                elif d == 4:
                                    pieces = [(ub, PB[4 + ub][0:65, r:512:4], PT[:, sl, bi * 128:(bi + 1) * 128])]
                                else:
                                    b_ = qc0 // 512
                                    pieces = [(b_, PB[4 + b_][0:65, qc0 % 512:qc0 % 512 + 128],
                                               PT[:, sl, bi * 128:(bi + 1) * 128])]
                                for (b_, oap, rap) in pieces:
                                    tr.op('pe', 'matmul', oap, lhsT=vl, rhs=rap, start=first_in_bank[b_], stop=True,
                                          skip_group_check=True, reads=[('PT', sl), ('Vt2', ti_)], writes=[pk(4 + b_)])
                                    first_in_bank[b_] = False

                        n_it = len(items)
                        for i in range(n_it + LAG):
                            if i < n_it:
                                emit_S(i)
                            if i % 5 == 4:
                                run_pending(1)
                            if i >= LAG:
                                emit_PV(i - LAG)
                        run_pending(100)
                        for b_ in range(4):
                            tr.op('act', 'copy', out=accs[:, apar, b_ * 512:(b_ + 1) * 512], in_=PB[4 + b_][0:65, :],
                                  writes=[pk(4 + b_), ('accs', apar)])
                        pending.extend(make_post(h, a0))
                run_pending(100)
                tr.op('act', 'activation', out=ssacc, in_=ssacc, func=AF.Ln, scale=1.0 / 512, bias=epsT[0:64, :],
                      reads=['ssacc', 'eps'], writes=['ssacc'])
                tr.op('act', 'activation', out=ssacc, in_=ssacc, func=AF.Exp, scale=-0.5, reads=['ssacc'], writes=['ssacc'])
                for qb in range(4):
                    qs = slice(qb * 512, (qb + 1) * 512)
                    for h in range(8):
                        sl = h % 2
                        tr.dma('pool', attl[:, sl, :], ATTF[h, :, a0 + qb * 512:a0 + (qb + 1) * 512], key=('attl', sl),
                               reads=[('ATTF', h)], writes=[('rden', sl)])
                        tr.op('dve', 'scalar_tensor_tensor', out=attT_st[:, h, :], in0=attl[:, sl, :], scalar=gao[:, h:h + 1],
                              in1=ssacc[:, qs], op0=ALU.mult, op1=ALU.mult, reads=[('rden', sl), 'gao', 'ssacc'],
                              writes=['attT_st'])
                    tr.dma('pool', ATTT.rearrange("c (two e) t -> e (c two) t", two=2)[:, :, a0 + qb * 512:a0 + (qb + 1) * 512],
                           attT_st, key='attt_o', reads=['attT_st'], writes=['ATTT'])
        if stop_after not in ('ssd', 'p3v', 'p3a'):
            phase()
            WA = ph([128, 1, 8, 1024], BF16)
            w1p = ph([128, 3, 8, 512], BF16)
            w2p = ph([128, 3, 4, 1024], BF16)
            hTp = ph([128, 2, 4, 512], BF16)
            hbuf = ph([128, 2, 4, 1024], F32)
            mixT = ph([128, 8, 512], BF16)
            hnT = ph([128, 2, 8, 512], BF16)
            kmT = ph([128, 2, 8, 256], BF16)
            Vm = ph([128, 2, 2, 1024], BF16)
            sqx = ph([128, 2, 512], BF16)
            rsx = ph([128, 512], F32)
            qx = ph([128, 2, 512], BF16)
            PTx = ph([128, 2, 512], BF16)
            rdx = ph([128, 512], F32)
            oT = ph([128, 8, 512], BF16)
            rl = ph([128, 2, 512], F32)
            memT = rl.bitcast(BF16).rearrange("p a (b c) -> p (a b) c", b=4)
            wa_i = [0]

            def load_wa(src_ap):
                s = 0
                wa_i[0] += 1
                tr.dma('sp', WA[:, s], src_ap, key=('WA', s), writes=[('WA', s)])
                return s

            for sg in range(2):
                for j in range(2):
                    s = j % 2
                    tr.dma('sp', xt[:, s, :], mem_d[sg, j * 128:(j + 1) * 128, :], key=('xt', s), writes=[('xt', s)])
                    norm_T(xt[:, s, :], ('xt', s), 2, memT, 'memT', j, 0)
                sk = load_wa(WKVB[:, :, 0:1024])
                for hh in range(4):
                    for cc in range(2):
                        c = 2 * hh + cc
                        for k in range(8):
                            tr.op('pe', 'matmul', PB[cc][:, 0:256], lhsT=WA[:, sk, k, c * 128:(c + 1) * 128], rhs=memT[:, k, :],
                                  start=(k == 0), stop=(k == 7), reads=[('WA', sk), 'memT'], writes=[pk(cc)])
                        tr.op('act', 'activation', out=sqx[:, cc, 0:256], in_=PB[cc][:, 0:256], func=AF.Square,
                              reads=[pk(cc)], writes=[('sqx', cc)])
                    for cc in range(2):
                        tr.op('pe', 'matmul', PB[2][:, 0:256], lhsT=cb(C_ONES), rhs=sqx[:, cc, 0:256], start=(cc == 0),
                              stop=(cc == 1), reads=['CB', ('sqx', cc)], writes=[pk(2)])
                    tr.op('act', 'activation', out=rsx[:, 0:256], in_=PB[2][:, 0:256], func=AF.Ln, scale=1.0 / 256,
                          bias=epsT[:], reads=['eps'], writes=[pk(2), 'rsx'])
                    tr.op('act', 'activation', out=rsx[:, 0:256], in_=rsx[:, 0:256], func=AF.Exp, scale=-0.5, reads=['rsx'], writes=['rsx'])
                    for cc in range(2):
                        tr.op('dve', 'scalar_tensor_tensor', out=kmT[:, sg, 2 * hh + cc, :], in0=PB[cc][:, 0:256],
                              scalar=gx[:, 2 + cc:3 + cc], in1=rsx[:, 0:256], op0=ALU.mult, op1=ALU.mult,
                              reads=['gx', 'rsx'], writes=[pk(cc), 'kmT'])
                sv = load_wa(WKVB[:, :, 1024:2048])
                for mtl in range(2):
                    for half in range(2):
                        bk = half
                        for k in range(8):
                            tr.op('pe', 'matmul', PB[bk][:], lhsT=memT[:, k, mtl * 128:(mtl + 1) * 128],
                                  rhs=WA[:, sv, k, half * 512:(half + 1) * 512], start=(k == 0), stop=(k == 7),
                                  reads=[('WA', sv), 'memT'], writes=[pk(bk)])
                        tr.op('act', 'copy', out=Vm[:, sg, mtl, half * 512:(half + 1) * 512], in_=PB[bk][:],
                              writes=[pk(bk), 'Vm'])
            tr.barrier()

            def F_gen(mt):
                t0 = mt * 512
                sg = t0 // SEG
                par = mt % 2
                hb = hbuf[:, par]
                hn = hnT[:, par]
                hkey = lambda j: ('hbuf', par, j)
                nkey = ('hnT', par)
                tr.dma('sp', mixT[:, 0:4, :], SSDT[:, :, t0:t0 + 512].rearrange("c p t -> p c t"), key='mixT',
                       writes=['mixT'])
                tr.dma('sp', mixT[:, 4:8, :], ATTT[:, :, t0:t0 + 512].rearrange("c p t -> p c t"), key='mixT',
                       writes=['mixT'])
                s_ = load_wa(WOUTB)
                yield
                for j in range(4):
                    xs_ = j % 2
                    tr.dma('sp', xt[:, xs_, :], x_d[t0 + j * 128:t0 + (j + 1) * 128, :], key=('xt', xs_), writes=[('xt', xs_)])
                    for half in range(2):
                        bk = half
                        for kc in range(8):
                            tr.op('pe', 'matmul', PB[bk][:], lhsT=mixT[:, kc, j * 128:(j + 1) * 128],
                                  rhs=WA[:, s_, kc, half * 512:(half + 1) * 512], start=(kc == 0), stop=(kc == 7),
                                  reads=[('WA', s_), 'mixT'], writes=[pk(bk)])
                        tr.op('dve', 'tensor_tensor', out=hb[:, j, half * 512:(half + 1) * 512], in0=PB[bk][:],
                              in1=xt[:, xs_, half * 512:(half + 1) * 512], op=ALU.add, reads=[('xt', xs_)],
                              writes=[pk(bk), hkey(j)])
                sl0 = norm_A(hb[:, 0, :], hkey(0), 1)
                yield
                sl1 = norm_A(hb[:, 1, :], hkey(1), 1)
                yield
                norm_B(sl0, hn, nkey, 0, 0)
                sl2 = norm_A(hb[:, 2, :], hkey(2), 1)
                yield
                norm_B(sl1, hn, nkey, 1, 1)
                sl3 = norm_A(hb[:, 3, :], hkey(3), 1)
                yield
                norm_B(sl2, hn, nkey, 2, 0)
                yield
                norm_B(sl3, hn, nkey, 3, 1)
                yield
                s_ = load_wa(WQB)
                for hh in range(4):
                    for cc in range(2):
                        c = 2 * hh + cc
                        for k in range(8):
                            tr.op('pe', 'matmul', PB[cc][:], lhsT=WA[:, s_, k, c * 128:(c + 1) * 128], rhs=hn[:, k, :],
                                  start=(k == 0), stop=(k == 7), reads=[('WA', s_), nkey], writes=[pk(cc)])
                        tr.op('act', 'activation', out=sqx[:, cc, :], in_=PB[cc][:], func=AF.Square, reads=[pk(cc)],
                              writes=[('sqx', cc)])
                    yield
                    for cc in range(2):
                        tr.op('pe', 'matmul', PB[2][:], lhsT=cb(C_ONES), rhs=sqx[:, cc, :], start=(cc == 0), stop=(cc == 1),
                              reads=['CB', ('sqx', cc)], writes=[pk(2)])
                    yield
                    tr.op('act', 'activation', out=rsx, in_=PB[2][:], func=AF.Ln, scale=1.0 / 256, bias=epsT[:],
                          reads=['eps'], writes=[pk(2), 'rsx'])
                    tr.op('act', 'activation', out=rsx, in_=rsx, func=AF.Exp, scale=-0.5, reads=['rsx'], writes=['rsx'])
                    for cc in range(2):
                        tr.op('dve', 'scalar_tensor_tensor', out=qx[:, cc, :], in0=PB[cc][:], scalar=gx[:, cc:cc + 1],
                              in1=rsx, op0=ALU.mult, op1=ALU.mult, reads=['gx', 'rsx'], writes=[pk(cc), 'qx'])
                    yield
                    for mtl in range(2):
                        for cc in range(2):
                            tr.op('pe', 'matmul', PB[2][:], lhsT=kmT[:, sg, 2 * hh + cc, mtl * 128:(mtl + 1) * 128],
                                  rhs=qx[:, cc, :], start=(cc == 0), stop=(cc == 1), reads=['kmT', 'qx'], writes=[pk(2)])
                        tr.op('act', 'activation', out=PTx[:, mtl, :], in_=PB[2][:], func=AF.Exp, scale=1.0 / 16,
                              writes=[pk(2), 'PTx'])
                    yield
                    for mtl in range(2):
                        tr.op('pe', 'matmul', PB[2][:], lhsT=cb(C_ONES), rhs=PTx[:, mtl, :], start=(mtl == 0), stop=(mtl == 1),
                              reads=['CB', 'PTx'], writes=[pk(2)])
                    for cc in range(2):
                        c = 2 * hh + cc
                        for mtl in range(2):
                            tr.op('pe', 'matmul', PB[cc][:], lhsT=Vm[:, sg, mtl, c * 128:(c + 1) * 128], rhs=PTx[:, mtl, :],
                                  start=(mtl == 0), stop=(mtl == 1), reads=['Vm', 'PTx'], writes=[pk(cc)])
                    tr.op('act', 'activation', out=rdx, in_=PB[2][:], func=AF.Ln, writes=[pk(2), 'rdx'])
                    tr.op('act', 'activation', out=rdx, in_=rdx, func=AF.Exp, scale=-1.0, reads=['rdx'], writes=['rdx'])
                    for cc in range(2):
                        c = 2 * hh + cc
                        tr.op('dve', 'tensor_tensor', out=oT[:, c, :], in0=PB[cc][:], in1=rdx, op=ALU.mult, reads=['rdx'],
                              writes=[pk(cc), 'oT'])
                    yield
                s_ = load_wa(WOB)
                yield
                for j in range(4):
                    for half in range(2):
                        bk = half
                        for kc in range(8):
                            tr.op('pe', 'matmul', PB[bk][:], lhsT=oT[:, kc, j * 128:(j + 1) * 128],
                                  rhs=WA[:, s_, kc, half * 512:(half + 1) * 512], start=(kc == 0), stop=(kc == 7),
                                  reads=[('WA', s_), 'oT'], writes=[pk(bk)])
                        tr.op('dve', 'tensor_tensor', out=hb[:, j, half * 512:(half + 1) * 512], in0=PB[bk][:],
                              in1=hb[:, j, half * 512:(half + 1) * 512], op=ALU.add, writes=[pk(bk), hkey(j)])
                sl0 = norm_A(hb[:, 0, :], hkey(0), 3)
                yield
                sl1 = norm_A(hb[:, 1, :], hkey(1), 3)
                yield
                norm_B(sl0, hn, nkey, 0, 0)
                sl2 = norm_A(hb[:, 2, :], hkey(2), 3)
                yield
                norm_B(sl1, hn, nkey, 1, 1)
                sl3 = norm_A(hb[:, 3, :], hkey(3), 3)
                yield
                norm_B(sl2, hn, nkey, 2, 0)
                yield
                norm_B(sl3, hn, nkey, 3, 1)
                yield

            def MLP_gen(mt):
                t0 = mt * 512
                par = mt % 2
                hb = hbuf[:, par]
                hn = hnT[:, par]
                hkey = lambda j: ('hbuf', par, j)
                nkey = ('hnT', par)

                def mlp_load(pc):
                    ws_ = pc % 3
                    tr.dma('sp', w1p[:, ws_], W1B[:, :, pc * 512:(pc + 1) * 512], key=('w1p', ws_), writes=[('w1p', ws_)])
                    tr.dma('pool', w2p[:, ws_], W2B[:, pc * 4:(pc + 1) * 4, :], key=('w2p', ws_), writes=[('w2p', ws_)])

                def mlp_w1(pc):
                    ps_ = pc % 2
                    ws_ = pc % 3
                    if pc + 1 < 8:
                        mlp_load(pc + 1)
                    for f in range(4):
                        bk = 3 + (f % 2)
                        for k in range(8):
                            tr.op('pe', 'matmul', PB[bk][:], lhsT=w1p[:, ws_, k, f * 128:(f + 1) * 128], rhs=hn[:, k, :],
                                  start=(k == 0), stop=(k == 7), reads=[('w1p', ws_), nkey], writes=[pk(bk)])
                        tr.op('act', 'activation', out=rl[:, f % 2, :], in_=PB[bk][:], func=AF.Relu,
                              writes=[pk(bk), ('rl', f % 2)])
                        tr.op('dve', 'tensor_tensor', out=hTp[:, ps_, f, :], in0=rl[:, f % 2, :], in1=rl[:, f % 2, :],
                              op=ALU.mult, reads=[('rl', f % 2)], writes=[('hTp', ps_, f)])
                        yield

                def mlp_w2(pc):
                    ps_ = pc % 2
                    ws_ = pc % 3
                    for j in range(4):
                        for half in range(2):
                            bk = 5 + ((2 * j + half) % 3)
                            for f in range(4):
                                tr.op('pe', 'matmul', PB[bk][:], lhsT=hTp[:, ps_, f, j * 128:(j + 1) * 128],
                                      rhs=w2p[:, ws_, f, half * 512:(half + 1) * 512], start=(f == 0), stop=(f == 3),
                                      reads=[('hTp', ps_, f), ('w2p', ws_)], writes=[pk(bk)])
                            tr.op('dve', 'tensor_tensor', out=hb[:, j, half * 512:(half + 1) * 512], in0=PB[bk][:],
                                  in1=hb[:, j, half * 512:(half + 1) * 512], op=ALU.add, writes=[pk(bk), hkey(j)])
                            yield

                mlp_load(0)
                yield from mlp_w1(0)
                for pc in range(8):
                    if pc + 1 < 8:
                        yield from mlp_w1(pc + 1)
                    yield from mlp_w2(pc)
                tr.dma('pool', y_d[t0:t0 + 512, :].rearrange("(j p) c -> p j c", p=128), hb, key=('y_o', par),
                       reads=[hkey(j) for j in range(4)], writes=['y'])

            def interleave(gm, gf, ratio=3):
                m_alive, f_alive = True, True
                while m_alive or f_alive:
                    if f_alive:
                        try:
                            next(gf)
                        except StopIteration:
                            f_alive = False
                    for _ in range(ratio if f_alive else 1000000):
                        if not m_alive:
                            break
                        try:
                            next(gm)
                        except StopIteration:
                            m_alive = False

            for _ in F_gen(0):
                pass
            for mt in range(NMT):
                gens = [MLP_gen(mt)]
                if mt + 1 < NMT:
                    gens.append(F_gen(mt + 1))
                interleave(*gens) if len(gens) == 2 else [None for _ in gens[0]]
        for n in dbg_d:
            if n == 'ssdt':
                tr.dma('sp', dbg_d[n], SSDT, key='dbg', reads=['SSDT'])
            if n == 'xnt':
                tr.dma('sp', dbg_d[n], XNT, key='dbg', reads=['XNT'])
            if n == 'attt':
                tr.dma('sp', dbg_d[n], ATTT, key='dbg', reads=['ATTT'])
        print('op counts', tr.cnt, 'dma sems', len(tr.dsem))
        tr.emit('sp')
    return nc


def host_inputs(NT, xs_list, mem_list, fuse_list, params, pos_list):
    consts = make_consts()
    half = 8
    inv_freq = np.power(np.float32(500000.0), -np.arange(half, dtype=np.float32) / half).astype(np.float32)
    maps = []
    for c in range(len(xs_list)):
        f = float(fuse_list[c])
        lo = np.arange(128) < 64
        cols = []
        for (a, b) in [(1, 1), (f, f), (0, 0), (1, f), (f, 1), (1, 0), (0, 1), (f, 0), (0, f)]:
            cols.append(np.where(lo, a, b))
        vtab = np.stack(cols, 1).astype(np.float32)
        vtab = np.concatenate([vtab, (vtab - 1.0) * 30000.0], axis=1).astype(np.float32)
        ang = pos_list[c].astype(np.float32)[None, :] * inv_freq[:, None]
        cos = np.ones((128, NT), np.float32)
        sin = np.zeros((128, NT), np.float32)
        for hb in (0, 64):
            cos[hb:hb + 8] = np.cos(ang)
            cos[hb + 8:hb + 16] = np.cos(ang)
            sin[hb:hb + 8] = np.sin(ang)
            sin[hb + 8:hb + 16] = np.sin(ang)
        m = {"x": np.ascontiguousarray(xs_list[c], np.float32), "mem": np.ascontiguousarray(mem_list[c], np.float32),
             "vtab": vtab, "cs": np.stack([cos, sin], 0), "consts": consts}
        p = params
        m["w_in"] = p["w_in"][0]
        m["w_out"] = p["w_out"][0]
        m["xatt_wq"] = p["xatt_wq"][0]
        m["xatt_wkv"] = p["xatt_wkv"][0]
        m["xatt_wo"] = p["xatt_wo"][0]
        m["mlp_w1"] = p["mlp_w1"][0]
        m["mlp_w2"] = p["mlp_w2"][0]
        for k in ["mix_norm_g", "xatt_norm_g", "mem_norm_g", "mlp_norm_g", "ssd_norm_g", "conv_b"]:
            m[k] = p[k].reshape(1, -1)
        m["conv_w"] = p["conv_w"][0]
        m["ssd_A_log"] = p["ssd_A_log"].reshape(1, 16)
        m["ssd_dt_bias"] = p["ssd_dt_bias"].reshape(1, 16)
        m["ssd_D"] = p["ssd_D"].reshape(1, 8)
        m["att_q_norm_g"] = p["att_q_norm_g"].reshape(64, 1)
        m["att_k_norm_g"] = p["att_k_norm_g"].reshape(64, 1)
        m["att_out_norm_g"] = p["att_out_norm_g"].reshape(8, 64)
        m["xatt_q_norm_g"] = p["xatt_q_norm_g"].reshape(2, 128)
        m["xatt_k_norm_g"] = p["xatt_k_norm_g"].reshape(2, 128)
        maps.append({k: np.ascontiguousarray(np.asarray(v, np.float32)) for k, v in m.items()})
    return maps


_NC_CACHE = {}


def kernel(x_prompt, x_sample, mem_prompt, mem_sample, **params):
    NT = 8192
    x_prompt = np.asarray(x_prompt, np.float32)
    x_sample = np.asarray(x_sample, np.float32)
    mem_prompt = np.asarray(mem_prompt, np.float32)
    mem_sample = np.asarray(mem_sample, np.float32)
    p = {k: np.asarray(v, np.float32) for k, v in params.items()}
    xs_list, mem_list, fuse, pos = [], [], [], []
    for b in range(2):
        xs_list.append(x_prompt[b])
        mem_list.append(np.stack([mem_prompt[b], mem_prompt[b]]))
        fuse.append(1)
        pos.append(np.arange(NT))
    for c in range(4):
        xs_list.append(np.concatenate([x_sample[2 * c], x_sample[2 * c + 1]], 0))
        mem_list.append(np.stack([mem_sample[2 * c], mem_sample[2 * c + 1]]))
        fuse.append(0)
        pos.append(np.concatenate([np.arange(NT // 2)] * 2))
    for c in range(2):
        xs_list.append(np.zeros((NT, D), np.float32))
        mem_list.append(np.zeros((2, 256, D), np.float32))
        fuse.append(0)
        pos.append(np.concatenate([np.arange(NT // 2)] * 2))
    maps = host_inputs(NT, xs_list, mem_list, fuse, p, pos)
    if NT not in _NC_CACHE:
        _NC_CACHE[NT] = build(NT)
    res = run_bass_kernel_spmd(_NC_CACHE[NT], maps, core_ids=list(range(8)))
    ys = [np.asarray(r["y"], np.float32) for r in res.results]
    y_prompt = np.stack([ys[0], ys[1]], 0)
    y_sample = np.stack([ys[2 + c // 2][(c % 2) * (NT // 2):(c % 2 + 1) * (NT // 2)] for c in range(8)], 0)
    return (y_prompt, y_sample)
```
